# Optimizing a Trainium2 kernel written in Bass

```python
import jax, jax.numpy as jnp
from jax import lax
import numpy as np

D_MODEL = 1024
BATCH = 8
SEQ = 2048
DEPTH = 1
DEC_BATCH = 128
DEC_SEQ = 4
PAST_LEN = 16384
PAGE_SIZE = 128

N_META = 16
D_MIX = 2 * D_MODEL
D_CONV_GRP = D_MODEL
D_SSM = D_MIX - D_CONV_GRP
SSM_HEAD_DIM = 64
SSM_HEADS = D_SSM // SSM_HEAD_DIM
SSM_GROUPS = 2
SSM_STATE = 128
SSM_CONV = 4
CHUNK = 128
CONF_KERNEL = 31
FFN_KERNEL = 3
D_FF = 2816
GN = SSM_GROUPS * SSM_STATE
XBC_DIM = D_SSM + 2 * GN
OFF_Z = 2 * D_CONV_GRP
OFF_XBC = OFF_Z + D_SSM
OFF_DT = OFF_XBC + XBC_DIM
IN_PROJ_DIM = OFF_DT + SSM_HEADS
EPS = 1e-5

kernel_name = "hymba_conformer_ssd_convffn_step"


def _rmsnorm(x, g):
    xf = x.astype(jnp.float32)
    y = xf * lax.rsqrt(jnp.mean(xf * xf, axis=-1, keepdims=True) + EPS)
    return (y * g.astype(jnp.float32)).astype(x.dtype)


def _layernorm(x, g, b):
    xf = x.astype(jnp.float32)
    mu = jnp.mean(xf, axis=-1, keepdims=True)
    var = jnp.mean(jnp.square(xf - mu), axis=-1, keepdims=True)
    y = (xf - mu) * lax.rsqrt(var + EPS)
    return (y * g.astype(jnp.float32) + b.astype(jnp.float32)).astype(x.dtype)


def _group_rmsnorm(x, g):
    shp = x.shape
    xf = x.astype(jnp.float32).reshape(shp[:-1] + (SSM_GROUPS, shp[-1] // SSM_GROUPS))
    y = xf * lax.rsqrt(jnp.mean(xf * xf, axis=-1, keepdims=True) + EPS)
    return (y.reshape(shp) * g.astype(jnp.float32)).astype(x.dtype)


def _causal_dwconv(prev, x, w, b):
    width, ch = w.shape
    full = jnp.concatenate([prev.astype(x.dtype), x], axis=1)
    y = lax.conv_general_dilated(full, w[:, None, :].astype(x.dtype), window_strides=(1,),
                                 padding='VALID', dimension_numbers=('NWC', 'WIO', 'NWC'),
                                 feature_group_count=ch)
    return y + b.astype(y.dtype), full[:, full.shape[1] - (width - 1):]


def _ssd(x, dt, A, Bm, Cm, D, h0):
    f32 = jnp.float32
    b, L, H, P = x.shape
    G, N = Bm.shape[2], Bm.shape[3]
    E = H // G
    pad = (-L) % CHUNK
    fpad = lambda t: jnp.pad(t.astype(f32), [(0, 0), (pad, 0)] + [(0, 0)] * (t.ndim - 2))
    nc = (L + pad) // CHUNK
    xs = fpad(x).reshape(b, nc, CHUNK, G, E, P)
    dts = fpad(dt).reshape(b, nc, CHUNK, G, E)
    Bs = fpad(Bm).reshape(b, nc, CHUNK, G, N)
    Cs = fpad(Cm).reshape(b, nc, CHUNK, G, N)
    a_cs = jnp.cumsum(dts * A.astype(f32).reshape(G, E), axis=2)
    xdt = xs * dts[..., None]
    diff = a_cs[:, :, :, None] - a_cs[:, :, None, :]
    mask = jnp.tril(jnp.ones((CHUNK, CHUNK), bool))[:, :, None, None]
    decay = jnp.where(mask, jnp.exp(jnp.where(mask, diff, 0.0)), 0.0)
    cb = jnp.einsum('bclgn,bcsgn->bclsg', Cs, Bs)
    y_diag = jnp.einsum('bclsg,bclsge,bcsgep->bclgep', cb, decay, xdt)
    decay_to_end = jnp.exp(a_cs[:, :, -1:] - a_cs)
    states = jnp.einsum('bclgn,bclge,bclgep->bcgepn', Bs, decay_to_end, xdt)
    chunk_decay = jnp.exp(a_cs[:, :, -1])

    def step(h, inp):
        s, d = inp
        return h * d[..., None, None] + s, h

    h_init = h0.astype(f32).reshape(b, G, E, P, N)
    h_final, h_in = lax.scan(step, h_init, (jnp.swapaxes(states, 0, 1), jnp.swapaxes(chunk_decay, 0, 1)))
    h_in = jnp.swapaxes(h_in, 0, 1)
    y_off = jnp.einsum('bclgn,bcgepn,bclge->bclgep', Cs, h_in, jnp.exp(a_cs))
    y = y_diag + y_off + xs * D.astype(f32).reshape(G, E)[..., None]
    y = y.reshape(b, nc * CHUNK, H, P)[:, pad:]
    return y.astype(x.dtype), h_final.reshape(b, H, P, N)


def _layer(h, conf_prev, xbc_prev, ssm_prev, ffn_prev, p):
    (g_mix, w_in, conf_w, conf_b, ln_g, ln_b, sconv_w, sconv_b, dt_bias, a_log, d_skip,
     snorm_g, w_out, g_ffn, w_up, fconv_w, fconv_b, w_down) = p
    b, L, _ = h.shape
    u = _rmsnorm(h, g_mix)
    proj = u @ w_in
    conf_in = proj[..., :OFF_Z]
    z = proj[..., OFF_Z:OFF_XBC]
    xbc = proj[..., OFF_XBC:OFF_DT]
    dt_raw = proj[..., OFF_DT:]
    glu = conf_in[..., :D_CONV_GRP] * jax.nn.sigmoid(conf_in[..., D_CONV_GRP:])
    c, conf_buf = _causal_dwconv(conf_prev, glu, conf_w, conf_b)
    c = jax.nn.silu(_layernorm(c, ln_g, ln_b))
    xbc_c, xbc_buf = _causal_dwconv(xbc_prev, xbc, sconv_w, sconv_b)
    xbc_c = jax.nn.silu(xbc_c)
    xs = xbc_c[..., :D_SSM].reshape(b, L, SSM_HEADS, SSM_HEAD_DIM)
    Bm = xbc_c[..., D_SSM:D_SSM + GN].reshape(b, L, SSM_GROUPS, SSM_STATE)
    Cm = xbc_c[..., D_SSM + GN:].reshape(b, L, SSM_GROUPS, SSM_STATE)
    dt = jax.nn.softplus(dt_raw.astype(jnp.float32) + dt_bias.astype(jnp.float32))
    A = -jnp.exp(a_log.astype(jnp.float32))
    y, h_ssm = _ssd(xs, dt, A, Bm, Cm, d_skip, ssm_prev)
    y = _group_rmsnorm(y.reshape(b, L, D_SSM) * jax.nn.silu(z), snorm_g)
    h = h + jnp.concatenate([c, y], axis=-1) @ w_out
    up, ffn_buf = _causal_dwconv(ffn_prev, _rmsnorm(h, g_ffn) @ w_up, fconv_w, fconv_b)
    h = h + (jax.nn.silu(up[..., :D_FF]) * up[..., D_FF:]) @ w_down
    return h, conf_buf, xbc_buf, h_ssm.astype(ssm_prev.dtype), ffn_buf


def setup_inputs(seed: int = 0) -> dict:
    key = jax.random.key(seed)
    ks = jax.random.split(key, 32)
    f32 = jnp.float32
    nrm = lambda k, shape, s: jax.random.normal(k, shape, f32) * s
    dt0 = jnp.exp(jax.random.uniform(ks[14], (DEPTH, SSM_HEADS), f32, np.log(1e-3), np.log(1e-1)))
    return {
        "x_prompt": nrm(ks[0], (BATCH, SEQ, D_MODEL), 1.0),
        "x_sample": nrm(ks[1], (DEC_BATCH, DEC_SEQ, D_MODEL), 1.0),
        "state_conf_conv": nrm(ks[2], (DEPTH, DEC_BATCH, CONF_KERNEL - 1, D_CONV_GRP), 0.5),
        "state_xbc_conv": nrm(ks[3], (DEPTH, DEC_BATCH, SSM_CONV - 1, XBC_DIM), 1.0),
        "state_ssm": nrm(ks[4], (DEPTH, DEC_BATCH, SSM_HEADS, SSM_HEAD_DIM, SSM_STATE), 0.5),
        "state_ffn_conv": nrm(ks[5], (DEPTH, DEC_BATCH, FFN_KERNEL - 1, 2 * D_FF), 1.0),
        "meta_tokens": nrm(ks[6], (N_META, D_MODEL), 1.0),
        "norm_mix_g": 1.0 + nrm(ks[7], (DEPTH, D_MODEL), 0.02),
        "w_in": nrm(ks[8], (DEPTH, D_MODEL, IN_PROJ_DIM), D_MODEL ** -0.5),
        "conf_conv_w": nrm(ks[9], (DEPTH, CONF_KERNEL, D_CONV_GRP), CONF_KERNEL ** -0.5),
        "conf_conv_b": nrm(ks[10], (DEPTH, D_CONV_GRP), 0.02),
        "conf_ln_g": 1.0 + nrm(ks[11], (DEPTH, D_CONV_GRP), 0.02),
        "conf_ln_b": nrm(ks[12], (DEPTH, D_CONV_GRP), 0.02),
        "ssm_conv_w": nrm(ks[13], (DEPTH, SSM_CONV, XBC_DIM), SSM_CONV ** -0.5),
        "ssm_conv_b": nrm(ks[15], (DEPTH, XBC_DIM), 0.02),
        "dt_bias": dt0 + jnp.log(-jnp.expm1(-dt0)),
        "a_log": jnp.log(jax.random.uniform(ks[16], (DEPTH, SSM_HEADS), f32, 1.0, 16.0)),
        "d_skip": 1.0 + nrm(ks[17], (DEPTH, SSM_HEADS), 0.1),
        "ssm_norm_g": 1.0 + nrm(ks[18], (DEPTH, D_SSM), 0.02),
        "w_out": nrm(ks[19], (DEPTH, D_MIX, D_MODEL), D_MIX ** -0.5),
        "norm_ffn_g": 1.0 + nrm(ks[20], (DEPTH, D_MODEL), 0.02),
        "w_up": nrm(ks[21], (DEPTH, D_MODEL, 2 * D_FF), D_MODEL ** -0.5),
        "ffn_conv_w": nrm(ks[22], (DEPTH, FFN_KERNEL, 2 * D_FF), FFN_KERNEL ** -0.5),
        "ffn_conv_b": nrm(ks[23], (DEPTH, 2 * D_FF), 0.02),
        "w_down": nrm(ks[24], (DEPTH, D_FF, D_MODEL), D_FF ** -0.5),
        "norm_final_g": 1.0 + nrm(ks[25], (D_MODEL,), 0.02),
    }


def reference(x_prompt, x_sample, state_conf_conv, state_xbc_conv, state_ssm, state_ffn_conv,
              meta_tokens, norm_mix_g, w_in, conf_conv_w, conf_conv_b, conf_ln_g, conf_ln_b,
              ssm_conv_w, ssm_conv_b, dt_bias, a_log, d_skip, ssm_norm_g, w_out, norm_ffn_g,
              w_up, ffn_conv_w, ffn_conv_b, w_down, norm_final_g):
    bp = x_prompt.shape[0]
    dt_ = x_prompt.dtype
    hp = jnp.concatenate([jnp.broadcast_to(meta_tokens.astype(dt_)[None], (bp, N_META, D_MODEL)), x_prompt], axis=1)
    hs = x_sample
    pc, px, ps, pf = [], [], [], []
    sc, sx, ss, sf = [], [], [], []
    for l in range(DEPTH):
        p = (norm_mix_g[l], w_in[l], conf_conv_w[l], conf_conv_b[l], conf_ln_g[l], conf_ln_b[l],
             ssm_conv_w[l], ssm_conv_b[l], dt_bias[l], a_log[l], d_skip[l], ssm_norm_g[l], w_out[l],
             norm_ffn_g[l], w_up[l], ffn_conv_w[l], ffn_conv_b[l], w_down[l])
        hp, c_b, x_b, s_b, f_b = _layer(
            hp,
            jnp.zeros((bp, CONF_KERNEL - 1, D_CONV_GRP), dt_),
            jnp.zeros((bp, SSM_CONV - 1, XBC_DIM), dt_),
            jnp.zeros((bp, SSM_HEADS, SSM_HEAD_DIM, SSM_STATE), jnp.float32),
            jnp.zeros((bp, FFN_KERNEL - 1, 2 * D_FF), dt_), p)
        pc.append(c_b); px.append(x_b); ps.append(s_b); pf.append(f_b)
        hs, c_b, x_b, s_b, f_b = _layer(hs, state_conf_conv[l], state_xbc_conv[l], state_ssm[l],
                                        state_ffn_conv[l], p)
        sc.append(c_b); sx.append(x_b); ss.append(s_b); sf.append(f_b)
    y_prompt = _rmsnorm(hp, norm_final_g)[:, N_META:]
    y_sample = _rmsnorm(hs, norm_final_g)
    return (y_prompt, y_sample,
            jnp.stack(pc), jnp.stack(px), jnp.stack(ps), jnp.stack(pf),
            jnp.stack(sc), jnp.stack(sx), jnp.stack(ss), jnp.stack(sf))
```

```python
import contextlib
import os
import numpy as np
import concourse.bass as bass
import concourse.mybir as mybir
from concourse.bass_utils import run_bass_kernel_spmd

F32 = mybir.dt.float32
BF16 = mybir.dt.bfloat16
AF = mybir.ActivationFunctionType
ALU = mybir.AluOpType

D = 1024
NMETA = 16
SEQ = 2048
TP = NMETA + SEQ
NSEQ_S = 16
LS = 4
NS = NSEQ_S * LS
NTOK = TP + NS
DFF = 2816
INP = 4624
OFF_Z, OFF_XBC, OFF_DT = 2048, 3072, 4608
EPS = 1e-5
NEGBIG = -30000.0


class Res:
    __slots__ = ("name", "last_w", "rd_eng", "rd_dma", "excl")

    def __init__(self, name, excl=False):
        self.name = name
        self.excl = excl
        self.last_w = None
        self.rd_eng = {}
        self.rd_dma = []


class Op:
    __slots__ = ("eng", "fn", "deps", "is_dma", "key", "kidx", "signal", "sigcount", "idx", "tag")


class Sched:
    ENGS = ("sp", "act", "pool", "dve", "pe")

    def __init__(self):
        self.streams = {e: [] for e in self.ENGS}
        self.dma_keys = {}
        self.nops = 0

    def _add(self, eng, fn, reads=(), writes=(), dma_key=None, chain=True, tag=""):
        op = Op()
        op.eng = eng
        op.fn = fn
        op.is_dma = dma_key is not None
        op.key = dma_key
        op.signal = False
        op.sigcount = 0
        op.idx = self.nops
        op.tag = tag
        op.kidx = 0
        self.nops += 1
        deps = {}

        def dep(d, raw):
            if d is None:
                return
            p = deps.get(d.idx)
            deps[d.idx] = (d, raw or (p[1] if p else False))

        for r in reads:
            dep(r.last_w, True)
            if r.excl:
                for e2, rd in r.rd_eng.items():
                    if e2 != eng:
                        dep(rd, False)
        for w in writes:
            lw = w.last_w
            if not (dma_key is not None and not chain and lw is not None and lw.is_dma and lw.key == dma_key):
                dep(lw, False)
            for rd in w.rd_eng.values():
                dep(rd, False)
            for rd in w.rd_dma:
                dep(rd, False)
        if op.is_dma:
            lst = self.dma_keys.setdefault(dma_key, [])
            if chain and lst:
                dep(lst[-1], True)
            op.kidx = len(lst)
            lst.append(op)
        final = []
        for d, raw in deps.values():
            if d is op:
                continue
            if d.is_dma:
                final.append(d)
            elif d.eng == eng:
                if eng == "pe":
                    continue
                final.append(d)
            else:
                final.append(d)
        for d in final:
            d.signal = True
        op.deps = final
        for r in reads:
            if op.is_dma:
                r.rd_dma.append(op)
            else:
                r.rd_eng[eng] = op
        for w in writes:
            w.last_w = op
            w.rd_eng = {}
            w.rd_dma = []
        self.streams[eng].append(op)
        return op

    def add(self, *a, **k):
        return self._add(*a, **k)

    def barrier(self):
        lasts = {}
        for e in self.ENGS:
            l = None
            for op in reversed(self.streams[e]):
                if op.fn is not None and not op.is_dma:
                    l = op
                    break
            lasts[e] = l
        dmal = [l[-1] for l in self.dma_keys.values() if l]
        for e in self.ENGS:
            op = Op()
            op.eng = e
            op.fn = None
            op.is_dma = False
            op.key = None
            op.kidx = 0
            op.signal = False
            op.sigcount = 0
            op.idx = self.nops
            op.tag = "barrier"
            self.nops += 1
            deps = []
            for e2 in self.ENGS:
                l = lasts[e2]
                if e2 != e and l is not None:
                    deps.append(l)
            deps.extend(dmal)
            for d in deps:
                d.signal = True
            op.deps = deps
            self.streams[e].append(op)

    def emit(self, nc):
        for e in self.ENGS:
            c = 0
            for op in self.streams[e]:
                if op.signal and not op.is_dma and op.fn is not None:
                    c += 1
                    op.sigcount = c
        with contextlib.ExitStack() as st:
            esem = {e: st.enter_context(nc.semaphore("es_" + e)) for e in self.ENGS}
            dsem = {k: st.enter_context(nc.semaphore("ds_%d" % i)) for i, k in enumerate(self.dma_keys)}
            block = st.enter_context(nc.Block())

            def run(ename):
                def body(e):
                    waited = {}
                    for op in self.streams[ename]:
                        need = {}
                        for d in op.deps:
                            if d.is_dma:
                                s, v = dsem[d.key], 16 * (d.kidx + 1)
                            else:
                                s, v = esem[d.eng], d.sigcount
                            k = id(s)
                            if k not in need or need[k][1] < v:
                                need[k] = (s, v)
                        for k, (s, v) in need.items():
                            if waited.get(k, 0) >= v:
                                continue
                            e.wait_ge(s, v)
                            waited[k] = v
                        if op.fn is None:
                            continue
                        inst = op.fn(e)
                        if op.is_dma:
                            inst.then_inc(dsem[op.key], 16)
                        elif op.signal:
                            inst.then_inc(esem[ename], 1)
                    if ename == "sp":
                        for k, lst in self.dma_keys.items():
                            if lst:
                                v = 16 * len(lst)
                                if waited.get(id(dsem[k]), 0) < v:
                                    e.wait_ge(dsem[k], v)

                return body

            block.sync(run("sp"))
            block.scalar(run("act"))
            block.gpsimd(run("pool"))
            block.vector(run("dve"))
            block.tensor(run("pe"))


def interleave(S, fns, turns=None):
    import threading
    n = len(fns)
    turns = turns or [1] * n
    go = [threading.Semaphore(0) for _ in range(n)]
    back = threading.Semaphore(0)
    done = [False] * n
    err = []
    cur = [0]
    orig_add = S.add

    def hooked(*a, **k):
        r = orig_add(*a, **k)
        i = cur[0]
        back.release()
        go[i].acquire()
        return r

    def worker(i):
        go[i].acquire()
        try:
            fns[i]()
        except BaseException as ex:
            err.append(ex)
        done[i] = True
        back.release()

    ths = [threading.Thread(target=worker, args=(i,)) for i in range(n)]
    for t in ths:
        t.start()
    S.add = hooked
    try:
        while not all(done):
            for i in range(n):
                for _ in range(turns[i]):
                    if done[i]:
                        break
                    cur[0] = i
                    go[i].release()
                    back.acquire()
    finally:
        S.add = orig_add
    for t in ths:
        t.join()
    if err:
        raise err[0]


def V(a, p0, npart, off, dims):
    ps = a.ap[0][0]
    return bass.AP(a.tensor, a.offset + p0 * ps + off, [[ps, npart]] + [[s, n] for s, n in dims])


class Buf:
    __slots__ = ("t", "a", "r", "shape")

    def __init__(self, t, name, shape):
        self.t = t
        self.a = t.ap()
        self.r = Res(name)
        self.shape = shape


class Arena:
    def __init__(self, nc):
        self.nc = nc
        self.off = (nc.sbuf_base + 63) // 64 * 64
        self.top = nc.sbuf_top
        self.n = 0

    def alloc(self, name, shape, dt):
        sz = int(np.prod(shape[1:])) * (4 if dt == F32 else 2)
        sz = (sz + 63) // 64 * 64
        assert self.off + sz <= self.top, "SBUF overflow at %s: need %d have %d" % (name, sz, self.top - self.off)
        self.n += 1
        t = self.nc.alloc_sbuf_tensor_at("%s_%d" % (name, self.n), list(shape), dt, offset=self.off)
        self.off += sz
        return Buf(t, name, shape)

    def mark(self):
        return self.off

    def reset(self, m):
        self.off = m


def tile_rows(i):
    if i == 0:
        return 0, NMETA
    if i <= 16:
        return NMETA + 128 * (i - 1), 128
    return TP, NS


def build_program(debug=False):
    nc = bass.Bass("TRN2", target_bir_lowering=False)

    def din(name, shape):
        return nc.dram_tensor(name, list(shape), F32, kind="ExternalInput").ap()

    def dout(name, shape):
        return nc.dram_tensor(name, list(shape), F32, kind="ExternalOutput").ap()

    xp = din("xp", [SEQ, D])
    xs = din("xs", [NSEQ_S, LS, D])
    st_conf = din("st_conf", [NSEQ_S, 30, D])
    st_xbc = din("st_xbc", [NSEQ_S, 3, 1536])
    st_ssm = din("st_ssm", [NSEQ_S, 1024, 128])
    st_ffn = din("st_ffn", [NSEQ_S, 2, 2 * DFF])
    meta = din("meta", [NMETA, D])
    w_in = din("w_in", [D, INP])
    w_out = din("w_out", [2 * D, D])
    w_up = din("w_up", [D, 2 * DFF])
    w_down = din("w_down", [DFF, D])
    vtabA = din("vtabA", [36, D])
    vtabB = din("vtabB", [5, 1536])
    vtabC = din("vtabC", [4, 2 * DFF])
    v16 = din("v16", [3, 16])
    snorm_g = din("snorm_g", [1, D])
    gfin = din("gfin", [1, D])
    c_ident = din("c_ident", [128, 128])
    c_m1 = din("c_m1", [128, 128])
    c_su = din("c_su", [128, 128])
    c_neg = din("c_neg", [128, 4, 128])
    c_m1s = din("c_m1s", [NS, NS])
    c_sus = din("c_sus", [NS, NS])
    c_negs = din("c_negs", [NS, 4, NS])
    c_rowsel = din("c_rowsel", [NS, NSEQ_S])
    c_colsel = din("c_colsel", [128, NSEQ_S, NS])

    yp = dout("yp", [SEQ, D])
    ys = dout("ys", [NSEQ_S, LS, D])
    o_pconf = dout("o_pconf", [30, D])
    o_pxbc = dout("o_pxbc", [3, 1536])
    o_pssm = dout("o_pssm", [1024, 128])
    o_pffn = dout("o_pffn", [2, 2 * DFF])
    o_sconf = dout("o_sconf", [NSEQ_S, 30, D])
    o_sxbc = dout("o_sxbc", [NSEQ_S, 3, 1536])
    o_sssm = dout("o_sssm", [NSEQ_S, 1024, 128])
    o_sffn = dout("o_sffn", [NSEQ_S, 2, 2 * DFF])
    skind = "ExternalOutput" if debug else "Internal"
    h_a = nc.dram_tensor("h_a", [NTOK, D], F32, kind=skind).ap()
    h_b = nc.dram_tensor("h_b", [NTOK, D], F32, kind=skind).ap()
    h_c = nc.dram_tensor("h_c", [NTOK, D], F32, kind=skind).ap()
    r_ha, r_hb, r_hc = Res("h_a"), Res("h_b"), Res("h_c")

    S = Sched()
    AR = Arena(nc)
    PSt = nc.alloc_psum_tensor("ps", [128, 4096], F32)
    PS = PSt.ap()
    PB = [Res("bank%d" % i, excl=True) for i in range(8)]

    def bank(b, n=512):
        return PS[:, b * 512:b * 512 + n]

    def bankb(b):
        return PS[:, b * 512:(b + 1) * 512].bitcast(BF16)

    kctr = [0]

    def rot(prefix, n):
        kctr[0] += 1
        return "%s%d" % (prefix, kctr[0] % n)

    def dma(eng, out, in_, reads=(), writes=(), key=None, chain=True):
        return S.add(eng, lambda e: e.dma_start(out=out, in_=in_), reads=reads, writes=writes,
                     dma_key=key, chain=chain)

    ident_f = AR.alloc("ident_f", [128, 128], F32)
    ident_b = AR.alloc("ident_b", [128, 128], BF16)
    m1_b = AR.alloc("m1_b", [128, 128], BF16)
    su_b = AR.alloc("su_b", [128, 128], BF16)
    neg_b = AR.alloc("neg_b", [128, 4, 128], BF16)
    m1_f = AR.alloc("m1_f", [128, 128], F32)
    ones_f = AR.alloc("ones_f", [128, 128], F32)
    ones_b = AR.alloc("ones_b", [128, 128], BF16)
    vecA = AR.alloc("vecA", [128, 8, 36], F32)
    vecB = AR.alloc("vecB", [128, 12, 5], F32)
    vecC = AR.alloc("vecC", [128, 44, 4], F32)
    gfin_bc = AR.alloc("gfin_bc", [128, D], F32)
    snorm_bc = AR.alloc("snorm_bc", [128, D], F32)
    dtb_bc = AR.alloc("dtb_bc", [128, 16], F32)
    A_bc = AR.alloc("A_bc", [128, 16], F32)
    D_bc = AR.alloc("D_bc", [128, 16], F32)
    eps_c = AR.alloc("eps_c", [128, 1], F32)

    dma("sp", ident_f.a, c_ident, writes=[ident_f.r], key="c0")
    dma("sp", m1_f.a, c_m1, writes=[m1_f.r], key="c1")
    dma("pool", ident_b.a, c_ident, writes=[ident_b.r], key="c2")
    dma("pool", m1_b.a, c_m1, writes=[m1_b.r], key="c3")
    dma("pool", su_b.a, c_su, writes=[su_b.r], key="c4")
    dma("pool", neg_b.a, c_neg, writes=[neg_b.r], key="c5")
    S.add("dve", lambda e: e.memset(ones_f.a, 1.0), writes=[ones_f.r])
    S.add("dve", lambda e: e.memset(ones_b.a, 1.0), writes=[ones_b.r])
    S.add("dve", lambda e: e.memset(eps_c.a, EPS), writes=[eps_c.r])

    def bc_src(ap2d, row, n):
        return bass.AP(ap2d.tensor, ap2d.offset + row * n, [[0, 128], [1, n]])

    SK = os.environ.get("MK_SK", "")
    if "b" not in SK:
        dma("sp", gfin_bc.a, bc_src(gfin, 0, D), writes=[gfin_bc.r], key="c0")
        dma("sp", snorm_bc.a, bc_src(snorm_g, 0, D), writes=[snorm_bc.r], key="c1")
    if "d" not in SK:
        dma("sp", dtb_bc.a, bc_src(v16, 0, 16), writes=[dtb_bc.r], key="c0")
        dma("sp", A_bc.a, bc_src(v16, 1, 16), writes=[A_bc.r], key="c1")
        dma("sp", D_bc.a, bc_src(v16, 2, 16), writes=[D_bc.r], key="c0")
    S.add("act", lambda e: e.activation(out=A_bc.a, in_=A_bc.a, func=AF.Exp), reads=[A_bc.r], writes=[A_bc.r])
    S.add("dve", lambda e: e.tensor_scalar(out=A_bc.a, in0=A_bc.a, scalar1=-1.0, scalar2=None, op0=ALU.mult),
          reads=[A_bc.r], writes=[A_bc.r])

    gmark = AR.mark()
    stA = AR.alloc("stA", [36, D], F32)
    stB = AR.alloc("stB", [5, 1536], F32)
    stC = AR.alloc("stC", [4, 2 * DFF], F32)
    dma("sp", stA.a, vtabA, writes=[stA.r], key="c1")
    dma("sp", stB.a, vtabB, writes=[stB.r], key="c0")
    dma("sp", stC.a, vtabC, writes=[stC.r], key="c1")

    def vec_tr(st, nrow, nj, vec, bk):
        def f(e):
            for j in range(nj):
                i = e.transpose(out=bank(bk)[:, j * nrow:(j + 1) * nrow],
                                in_=st.a[0:nrow, j * 128:(j + 1) * 128], identity=ident_f.a[0:nrow, 0:nrow])
            return i
        S.add("pe", f, reads=[st.r, ident_f.r], writes=[PB[bk]])
        S.add("dve", lambda e: e.tensor_copy(out=vec.a, in_=bank(bk)[:, 0:nj * nrow].rearrange("p (j r) -> p j r", r=nrow)),
              reads=[PB[bk]], writes=[vec.r])

    if "v" not in SK:
        vec_tr(stA, 36, 8, vecA, 0)
        vec_tr(stB, 5, 12, vecB, 1)
        vec_tr(stC, 4, 44, vecC, 2)
    S.barrier()
    AR.reset(gmark)
    pass_mark = AR.mark()
    STOP = os.environ.get("MK_STOP", "")
    if STOP == "setup":
        S.emit(nc)
        return nc

    def make_front(nslots_x):
        fe = {}
        fe["xt"] = [AR.alloc("xt%d" % i, [128, D], F32) for i in range(nslots_x)]
        fe["junk"] = AR.alloc("junk", [128, D], BF16)
        fe["ss"] = AR.alloc("ss", [128, 1], F32)
        fe["rstd"] = AR.alloc("rstd", [128, 1], F32)
        fe["u"] = [AR.alloc("u%d" % i, [128, D], BF16) for i in range(2)]
        fe["n"] = 0
        return fe

    def tile_src(i):
        if i == 0:
            return [(meta, 0, NMETA)]
        if i <= 16:
            return [(xp[128 * (i - 1):128 * i, :], 0, 128)]
        return [(xs[:, l, :], 16 * l, 16) for l in range(LS)]

    def rms_to_uT(fe, xbuf, r, gcol, uT, col0, tbank=0):
        fe["n"] += 1
        u = fe["u"][fe["n"] % 2]
        junk, ss, rstd = fe["junk"], fe["ss"], fe["rstd"]
        xa = xbuf.a[0:r, :]
        S.add("dve", lambda e: e.scalar_tensor_tensor(out=junk.a[0:r, :], in0=xa, scalar=1.0, in1=xa, op0=ALU.mult,
                                                      op1=ALU.mult, accum_out=ss.a[0:r, :]),
              reads=[xbuf.r], writes=[junk.r, ss.r])
        S.add("act", lambda e: e.activation(out=rstd.a[0:r, :], in_=ss.a[0:r, :], func=AF.Sqrt, bias=eps_c.a[0:r, :],
                                            scale=1.0 / D),
              reads=[ss.r, eps_c.r], writes=[rstd.r])
        S.add("dve", lambda e: e.reciprocal(out=rstd.a[0:r, :], in_=rstd.a[0:r, :]), reads=[rstd.r], writes=[rstd.r])
        S.add("dve", lambda e: e.tensor_scalar(out=u.a[0:r, :], in0=xa, scalar1=rstd.a[0:r, :], scalar2=None, op0=ALU.mult),
              reads=[xbuf.r, rstd.r], writes=[u.r])
        pb = bankb(tbank)

        def tr(e):
            for j in range(8):
                i = e.transpose(out=pb[:, j * 128:j * 128 + r], in_=u.a[0:r, j * 128:(j + 1) * 128],
                                identity=ident_b.a[0:r, 0:r])
            return i
        S.add("pe", tr, reads=[u.r, ident_b.r], writes=[PB[tbank]])
        gb = V(vecA.a, 0, 128, gcol, [(36, 8), (0, r)])
        S.add("dve", lambda e: e.tensor_tensor(out=uT.a[:, :, col0:col0 + r],
                                               in0=pb.rearrange("p (j t) -> p j t", j=8)[:, :, 0:r], in1=gb, op=ALU.mult),
              reads=[PB[tbank], vecA.r], writes=[uT.r])

    def load_w(dst, src_view, key):
        dma("pool", dst.a if isinstance(dst, Buf) else dst, src_view, writes=[dst.r] if isinstance(dst, Buf) else (), key=key,
            chain=False)

    def blocks(tpb):
        bl = [[0]]
        t = 1
        while t <= 16:
            bl.append(list(range(t, min(t + tpb, 17))))
            t += tpb
        bl.append([17])
        return bl

    NA = 256
    wA_in = AR.alloc("wA_in", [128, 8, 2048], BF16)
    wA_out = AR.alloc("wA_out", [128, 8, D], BF16)
    diag31 = AR.alloc("diag31", [128, 8, 31, 128], BF16)
    w_in_v = w_in.rearrange("(k p) n -> p k n", p=128)
    w_out_v = w_out.rearrange("(k p) n -> p k n", p=128)
    for k in range(8):
        dma("pool", wA_in.a[:, k, :], w_in_v[:, k, 0:2048], writes=[wA_in.r], key="wAi", chain=False)
    dma("pool", wA_out.a, w_out_v[:, 0:8, :], writes=[wA_out.r], key="wAo", chain=False)
    for j in range(8):
        S.add("dve", lambda e, j=j: e.tensor_tensor(out=diag31.a[:, j, :, :], in0=V(ident_b.a, 0, 128, 0, [(0, 31), (1, 128)]),
                                                    in1=V(vecA.a, 0, 128, j * 36, [(1, 31), (0, 128)]), op=ALU.mult),
              reads=[ident_b.r, vecA.r], writes=[diag31.r])
    feA = make_front(4)
    uTA = AR.alloc("uTA", [128, 8, NA], BF16)
    sig = [AR.alloc("sig%d" % i, [128, NA], F32) for i in range(2)]
    LSA = 34
    GW = max(30 + NA, NSEQ_S * LSA)
    glu = AR.alloc("glu", [128, 8, GW], BF16)
    gluh = AR.alloc("gluh", [128, 8, 30], BF16)
    glu32p = AR.alloc("glu32p", [128, 8, 30], F32)
    glu32s = AR.alloc("glu32s", [128, 8, 64], F32)
    cbfs = [AR.alloc("cbf%d" % i, [128, 8, NA], BF16) for i in range(2)]
    mean_s = [AR.alloc("mean_s%d" % i, [128, NA], F32) for i in range(2)]
    ex2_s = [AR.alloc("ex2_s%d" % i, [128, NA], F32) for i in range(2)]
    csq = [AR.alloc("csq%d" % i, [128, NA], BF16) for i in range(2)]
    rsd = AR.alloc("rsd", [128, NA], F32)
    nmr = AR.alloc("nmr", [128, NA], F32)
    tn = [AR.alloc("tn%d" % i, [128, NA], F32) for i in range(2)]
    cT = AR.alloc("cT", [128, 8, NA], BF16)
    houtA = [AR.alloc("houtA%d" % i, [128, D], F32) for i in range(2)]
    stin = AR.alloc("stin", [120, D], F32)
    stout = AR.alloc("stout", [64, D], F32)
    glu_r = [Res("glu%d" % j) for j in range(8)]
    cbf_rs = [[Res("cbf%d_%d" % (i, j)) for j in range(8)] for i in range(2)]
    cT_r = [Res("cT%d" % j) for j in range(8)]
    S.add("pool", lambda e: e.memset(glu.a, 0.0), writes=glu_r)

    xslot = [0]
    houtn = [0]

    def passA_front(tiles, kind, slot):
        N = sum(tile_rows(i)[1] for i in tiles)
        xbufs = []
        col = 0
        for i in tiles:
            rb, r = tile_rows(i)
            xb = feA["xt"][xslot[0] % len(feA["xt"])]
            xslot[0] += 1
            for (src, p0, rr) in tile_src(i):
                dma("sp", xb.a[p0:p0 + rr, :], src, writes=[xb.r], key=rot("xa", 4))
            rms_to_uT(feA, xb, r, 34, uTA, col, tbank=0)
            xbufs.append((xb, rb, r, col))
            col += r
        last_prompt = (kind == "P" and tiles[-1] == 16)
        if kind == "S":
            for grp in range(4):
                dma("sp", stin.a, st_conf[4 * grp:4 * grp + 4, :, :].rearrange("b r d -> (b r) d"), writes=[stin.r], key="sti")

                def tr(e):
                    for j in range(8):
                        i_ = e.transpose(out=PS[0:128, 6 * 512 + j * 128:6 * 512 + j * 128 + 120], in_=stin.a[0:120, j * 128:(j + 1) * 128],
                                         identity=ident_f.a[0:120, 0:120])
                    return i_
                S.add("pe", tr, reads=[stin.r, ident_f.r], writes=[PB[6], PB[7]])
                outv = V(glu.a, 0, 128, grp * 4 * LSA, [(GW, 8), (LSA, 4), (1, 30)])
                inv = V(PS, 0, 128, 6 * 512, [(128, 8), (30, 4), (1, 30)])
                S.add("act", lambda e, outv=outv, inv=inv: e.copy(out=outv, in_=inv), reads=[PB[6], PB[7]], writes=glu_r)
        return dict(N=N, xbufs=xbufs, last_prompt=last_prompt, kind=kind, slot=slot)

    def passA_main(cx):
        N, kind, last_prompt, slot = cx["N"], cx["kind"], cx["last_prompt"], cx["slot"]
        cbf, cbf_r = cbfs[slot], cbf_rs[slot]
        glu32 = glu32p if kind == "P" else glu32s
        def stage1(j):
            bb, ba = 1, 2
            sg = sig[j % 2]

            def mmb(e, j=j):
                for k in range(8):
                    i_ = e.matmul(bank(bb, N), lhsT=wA_in.a[:, k, 1024 + j * 128:1024 + (j + 1) * 128], rhs=uTA.a[:, k, 0:N],
                                  start=(k == 0), stop=(k == 7))
                return i_
            S.add("pe", mmb, reads=[wA_in.r, uTA.r], writes=[PB[bb]])
            S.add("act", lambda e, sg=sg: e.activation(out=sg.a[:, 0:N], in_=bank(bb, N), func=AF.Sigmoid),
                  reads=[PB[bb]], writes=[sg.r])

            def mma(e, j=j):
                for k in range(8):
                    i_ = e.matmul(bank(ba, N), lhsT=wA_in.a[:, k, j * 128:(j + 1) * 128], rhs=uTA.a[:, k, 0:N],
                                  start=(k == 0), stop=(k == 7))
                return i_
            S.add("pe", mma, reads=[wA_in.r, uTA.r], writes=[PB[ba]])
            if kind == "P":
                S.add("dve", lambda e, j=j, sg=sg: e.tensor_tensor(out=glu.a[:, j, 30:30 + N], in0=bank(ba, N), in1=sg.a[:, 0:N],
                                                                   op=ALU.mult),
                      reads=[PB[ba], sg.r], writes=[glu_r[j]])
                if last_prompt:
                    S.add("dve", lambda e, j=j, sg=sg: e.tensor_tensor(out=glu32.a[:, j, 0:30], in0=bank(ba, N)[:, N - 30:N],
                                                                       in1=sg.a[:, N - 30:N], op=ALU.mult),
                          reads=[PB[ba], sg.r], writes=[glu32.r])
            else:
                outv = V(glu.a, 0, 128, j * GW + 30, [(LSA, 16), (1, 4)])
                in0v = V(PS, 0, 128, ba * 512, [(1, 16), (16, 4)])
                in1v = V(sg.a, 0, 128, 0, [(1, 16), (16, 4)])
                S.add("dve", lambda e, outv=outv, in0v=in0v, in1v=in1v: e.tensor_tensor(out=outv, in0=in0v, in1=in1v, op=ALU.mult),
                      reads=[PB[ba], sg.r], writes=[glu_r[j]])
                S.add("dve", lambda e, j=j, sg=sg: e.tensor_tensor(out=glu32.a[:, j, 0:N], in0=bank(ba, N), in1=sg.a[:, 0:N],
                                                                   op=ALU.mult),
                      reads=[PB[ba], sg.r], writes=[glu32.r])

        def stage2(j):
            bc = 4
            if kind == "P":
                def cv(e, j=j, bc=bc):
                    for k in range(31):
                        i_ = e.matmul(bank(bc, N), lhsT=diag31.a[:, j, k, :], rhs=glu.a[:, j, k:k + N], start=(k == 0), stop=(k == 30))
                    return i_
                S.add("pe", cv, reads=[diag31.r, glu_r[j]], writes=[PB[bc]])
                cin = bank(bc, N)
                cout = cbf.a[:, j, 0:N]
            else:
                def cv(e, j=j, bc=bc):
                    for hh in range(2):
                        for k in range(31):
                            i_ = e.matmul(PS[:, bc * 512 + hh * 256:bc * 512 + hh * 256 + 242], lhsT=diag31.a[:, j, k, :],
                                          rhs=glu.a[:, j, hh * 272 + k:hh * 272 + k + 242], start=(k == 0), stop=(k == 30))
                    return i_
                S.add("pe", cv, reads=[diag31.r, glu_r[j]], writes=[PB[bc]])
                cin = V(PS, 0, 128, bc * 512, [(256, 2), (LSA, 8), (1, 4)])
                cout = V(cbf.a, 0, 128, j * NA, [(8, 2), (1, 8), (16, 4)])
            S.add("act", lambda e, j=j, cin=cin, cout=cout: e.activation(out=cout, in_=cin, func=AF.Identity,
                                                                         bias=vecA.a[:, j, 31:32]),
                  reads=[PB[bc], vecA.r], writes=[cbf_r[j]])
            cs = csq[j % 2]
            S.add("act", lambda e, j=j, cs=cs: e.activation(out=cs.a[:, 0:N], in_=cbf.a[:, j, 0:N], func=AF.Square),
                  reads=[cbf_r[j]], writes=[cs.r])

            def st(e, j=j, cs=cs):
                e.matmul(bank(5, N), lhsT=ones_b.a, rhs=cbf.a[:, j, 0:N], start=(j == 0), stop=(j == 7))
                return e.matmul(bank(6, N), lhsT=ones_b.a, rhs=cs.a[:, 0:N], start=(j == 0), stop=(j == 7))
            S.add("pe", st, reads=[ones_b.r, cbf_r[j], cs.r], writes=[PB[5], PB[6]])
            if kind == "P":
                S.add("pool", lambda e, j=j: e.tensor_copy(out=gluh.a[:, j, :], in_=glu.a[:, j, N:N + 30]), reads=[glu_r[j]], writes=[gluh.r])
                S.add("pool", lambda e, j=j: e.tensor_copy(out=glu.a[:, j, 0:30], in_=gluh.a[:, j, :]), reads=[gluh.r], writes=[glu_r[j]])

        stage1(0)
        for j in range(1, 8):
            stage1(j)
            stage2(j - 1)
        stage2(7)
        S.add("dve", lambda e: e.tensor_scalar(out=mean_s[slot].a[:, 0:N], in0=bank(5, N), scalar1=1.0 / D, scalar2=None, op0=ALU.mult),
              reads=[PB[5]], writes=[mean_s[slot].r])
        S.add("dve", lambda e: e.tensor_scalar(out=ex2_s[slot].a[:, 0:N], in0=bank(6, N), scalar1=1.0 / D, scalar2=None, op0=ALU.mult),
              reads=[PB[6]], writes=[ex2_s[slot].r])

    def passA_tail(cx):
        N, kind, last_prompt, slot, xbufs = cx["N"], cx["kind"], cx["last_prompt"], cx["slot"], cx["xbufs"]
        cbf, cbf_r = cbfs[slot], cbf_rs[slot]
        glu32 = glu32p if kind == "P" else glu32s
        mean, ex2 = mean_s[slot], ex2_s[slot]
        S.add("dve", lambda e: e.tensor_tensor(out=nmr.a[:, 0:N], in0=mean.a[:, 0:N], in1=mean.a[:, 0:N], op=ALU.mult),
              reads=[mean.r], writes=[nmr.r])
        S.add("dve", lambda e: e.tensor_tensor(out=rsd.a[:, 0:N], in0=ex2.a[:, 0:N], in1=nmr.a[:, 0:N], op=ALU.subtract),
              reads=[ex2.r, nmr.r], writes=[rsd.r])
        S.add("act", lambda e: e.activation(out=rsd.a[:, 0:N], in_=rsd.a[:, 0:N], func=AF.Sqrt, bias=eps_c.a), reads=[rsd.r, eps_c.r],
              writes=[rsd.r])
        S.add("dve", lambda e: e.reciprocal(out=rsd.a[:, 0:N], in_=rsd.a[:, 0:N]), reads=[rsd.r], writes=[rsd.r])
        S.add("dve", lambda e: e.scalar_tensor_tensor(out=nmr.a[:, 0:N], in0=mean.a[:, 0:N], scalar=-1.0, in1=rsd.a[:, 0:N],
                                                      op0=ALU.mult, op1=ALU.mult),
              reads=[mean.r, rsd.r], writes=[nmr.r])
        for j in range(8):
            t = tn[j % 2]
            S.add("dve", lambda e, j=j, t=t: e.tensor_tensor(out=t.a[:, 0:N], in0=cbf.a[:, j, 0:N], in1=rsd.a[:, 0:N], op=ALU.mult),
                  reads=[cbf_r[j], rsd.r], writes=[t.r])
            S.add("dve", lambda e, t=t: e.tensor_tensor(out=t.a[:, 0:N], in0=t.a[:, 0:N], in1=nmr.a[:, 0:N], op=ALU.add),
                  reads=[t.r, nmr.r], writes=[t.r])
            S.add("act", lambda e, j=j, t=t: e.activation(out=cT.a[:, j, 0:N], in_=t.a[:, 0:N], func=AF.Silu,
                                                          scale=vecA.a[:, j, 32:33], bias=vecA.a[:, j, 33:34]),
                  reads=[t.r, vecA.r], writes=[cT_r[j]])
        for (xb, rb, r, c0) in xbufs:
            def op_(e, r=r, c0=c0):
                for hf in range(2):
                    bo = 3 if hf == 0 else 7
                    for k in range(8):
                        i_ = e.matmul(PS[0:r, bo * 512:(bo + 1) * 512], lhsT=cT.a[:, k, c0:c0 + r],
                                      rhs=wA_out.a[:, k, hf * 512:(hf + 1) * 512], start=(k == 0), stop=(k == 7))
                return i_
            S.add("pe", op_, reads=cT_r + [wA_out.r], writes=[PB[3], PB[7]])
            ho = houtA[houtn[0] % 2]
            houtn[0] += 1
            for hf in range(2):
                bo = 3 if hf == 0 else 7
                S.add("dve", lambda e, r=r, ho=ho, xb=xb, hf=hf, bo=bo: e.tensor_tensor(out=ho.a[0:r, hf * 512:(hf + 1) * 512],
                                                                                        in0=PS[0:r, bo * 512:(bo + 1) * 512],
                                                                                        in1=xb.a[0:r, hf * 512:(hf + 1) * 512], op=ALU.add),
                      reads=[PB[bo], xb.r], writes=[ho.r])
            dma("sp", h_a[rb:rb + r, :], ho.a[0:r, :], reads=[ho.r], writes=[r_ha], key=rot("ha", 2))
        if last_prompt or kind == "S":
            nr = 30 if kind == "P" else NS

            def tro(e):
                for j in range(8):
                    bo = 3 if j < 4 else 7
                    i_ = e.transpose(out=PS[0:nr, bo * 512 + (j % 4) * 128:bo * 512 + (j % 4 + 1) * 128], in_=glu32.a[:, j, 0:nr],
                                     identity=ident_f.a)
                return i_
            S.add("pe", tro, reads=[glu32.r, ident_f.r], writes=[PB[3], PB[7]])
            S.add("act", lambda e: e.copy(out=stout.a[0:nr, 0:512], in_=PS[0:nr, 3 * 512:4 * 512]), reads=[PB[3]], writes=[stout.r])
            S.add("act", lambda e: e.copy(out=stout.a[0:nr, 512:1024], in_=PS[0:nr, 7 * 512:8 * 512]), reads=[PB[7]], writes=[stout.r])
            if kind == "P":
                dma("sp", o_pconf, stout.a[0:30, :], reads=[stout.r], key="oc")
            else:
                for l in range(LS):
                    dma("sp", o_sconf[:, 26 + l, :], stout.a[16 * l:16 * l + 16, :], reads=[stout.r], key="oc")
                dma("sp", o_sconf[:, 0:26, :].rearrange("b r d -> b (r d)"), st_conf[:, 4:30, :].rearrange("b r d -> b (r d)"),
                    key="oc")

    blA = blocks(NA // 128)
    if STOP.startswith("A") and len(STOP) > 1:
        blA = blA[:int(STOP[1:])]
    if os.environ.get("MK_SKIPA"):
        blA = []
    kindof = lambda tl: "S" if tl[0] == 17 else "P"
    NOIL = bool(os.environ.get("MK_NOIL"))
    cxs = {}
    if blA:
        cxs[0] = passA_front(blA[0], kindof(blA[0]), 0)
        passA_main(cxs[0])
    for bi_, tiles in enumerate(blA):
        tl_ = lambda bi_=bi_: passA_tail(cxs[bi_])
        if bi_ + 1 < len(blA):
            nt = blA[bi_ + 1]

            def nf(bi_=bi_, nt=nt):
                cxs[bi_ + 1] = passA_front(nt, kindof(nt), (bi_ + 1) % 2)
                passA_main(cxs[bi_ + 1])
            if NOIL:
                tl_()
                nf()
            else:
                interleave(S, [tl_, nf], turns=[int(x) for x in os.environ.get('MK_TA', '1,2').split(',')])
        else:
            tl_()
    if STOP.startswith("A"):
        S.emit(nc)
        return nc

    S.barrier()
    AR.reset(pass_mark)

    NB = 256
    NCB = 2576
    wB_in = AR.alloc("wB_in", [128, 8, NCB], BF16)
    wB_out = AR.alloc("wB_out", [128, 8, D], BF16)
    diag4 = AR.alloc("diag4", [128, 12, 4, 128], BF16)
    for k in range(8):
        dma("pool", wB_in.a[:, k, :], w_in_v[:, k, OFF_Z:INP], writes=[wB_in.r], key="wBi", chain=False)
    dma("pool", wB_out.a, w_out_v[:, 8:16, :], writes=[wB_out.r], key="wBo", chain=False)
    for j in range(12):
        S.add("dve", lambda e, j=j: e.tensor_tensor(out=diag4.a[:, j, :, :], in0=V(ident_b.a, 0, 128, 0, [(0, 4), (1, 128)]),
                                                    in1=V(vecB.a, 0, 128, j * 5, [(1, 4), (0, 128)]), op=ALU.mult),
              reads=[ident_b.r, vecB.r], writes=[diag4.r])
    feB = make_front(1)
    uTBs = [AR.alloc("uTB%d" % i, [128, 8, NB], BF16) for i in range(2)]
    LSB = 7
    XW = max(3 + NB, NSEQ_S * LSB)
    xbuf_ = AR.alloc("xbcbuf", [128, 12, XW], BF16)
    xbh = AR.alloc("xbh", [128, 12, 3], BF16)
    xbc32p = AR.alloc("xbc32p", [128, 12, 4], F32)
    xcs = [AR.alloc("xc%d" % i, [128, 12, NB], BF16) for i in range(2)]
    zsils = [AR.alloc("zsil0", [128, D], F32)]
    dtt = AR.alloc("dtt", [128, 16], F32)
    at = AR.alloc("at", [128, 16], F32)
    ahi = AR.alloc("ahi", [128, 16], BF16)
    alo = AR.alloc("alo", [128, 16], BF16)
    ares = AR.alloc("ares", [128, 16], F32)
    Rhi = AR.alloc("Rhi", [128, 16, 128], BF16)
    Rlo = AR.alloc("Rlo", [128, 16, 128], BF16)
    decay = AR.alloc("decay", [128, 16, 128], BF16)
    Mm = AR.alloc("Mm", [128, 16, 128], BF16)
    cbt = AR.alloc("cbt", [128, 2, 128], BF16)
    xdt = AR.alloc("xdt", [128, D], BF16)
    xws = [AR.alloc("xw0", [128, D], BF16)]
    yxs = [AR.alloc("yx0", [128, D], F32)]
    btoks = [AR.alloc("btok0", [128, 256], BF16)]
    etaus = [AR.alloc("etau0", [128, 16], F32)]
    dchs = [AR.alloc("dch0", [128, 16], F32)]
    t1 = AR.alloc("t1", [128, D], F32)
    t2 = feB["junk"]
    gss = AR.alloc("gss", [128, 2], F32)
    yn = AR.alloc("yn", [128, D], BF16)
    yT = AR.alloc("yT", [128, 8, NB], BF16)
    S32s = [AR.alloc("S32_0", [128, D], F32)]
    Sbfs = [AR.alloc("Sbf_0", [128, D], BF16)]
    slds = [AR.alloc("sload0", [128, 8, 128], F32)]
    S32, Sbf, sload = S32s[0], Sbfs[0], slds[0]
    hpB = [AR.alloc("hpB%d" % i, [128, D], F32) for i in range(1)]
    houtB = [AR.alloc("houtB%d" % i, [128, D], F32) for i in range(1)]
    stoutB = AR.alloc("stoutB", [48, 1536], F32)
    stinB = AR.alloc("stinB", [48, 1536], F32)
    xbc32s = AR.alloc("xbc32s", [128, 12, 48], F32)
    ovl = AR.mark()
    zsils.append(AR.alloc("zsil1", [128, D], F32))
    xws.append(AR.alloc("xw1", [128, D], BF16))
    yxs.append(AR.alloc("yx1", [128, D], F32))
    btoks.append(AR.alloc("btok1", [128, 256], BF16))
    etaus.append(AR.alloc("etau1", [128, 16], F32))
    dchs.append(AR.alloc("dch1", [128, 16], F32))
    hpB.append(AR.alloc("hpB1", [128, D], F32))
    ovl_hi = AR.mark()
    AR.reset(ovl)
    S32s.append(AR.alloc("S32_1", [128, D], F32))
    Sbfs.append(AR.alloc("Sbf_1", [128, D], BF16))
    slds.append(AR.alloc("sload1", [128, 8, 128], F32))
    m1s_b = AR.alloc("m1s_b", [NS, NS], BF16)
    sus_b = AR.alloc("sus_b", [NS, NS], BF16)
    negs_b = AR.alloc("negs_b", [NS, 4, NS], BF16)
    m1s_f = AR.alloc("m1s_f", [NS, NS], F32)
    rowsel_f = AR.alloc("rowsel_f", [NS, NSEQ_S], F32)
    colsel_b = AR.alloc("colsel_b", [128, NSEQ_S, NS], BF16)
    Cmb = [AR.alloc("Cmb%d" % i, [128, 2, NS], BF16) for i in range(2)]
    btm = [AR.alloc("btm%d" % i, [NS, 256], BF16) for i in range(2)]
    am_ = AR.alloc("am", [NS, NSEQ_S * 16], F32)
    dch_all = AR.alloc("dch_all", [128, NSEQ_S * 16], F32)
    dte_ = AR.alloc("dte", [128, 16], F32)
    AR.off = max(AR.off, ovl_hi)
    S.add("pool", lambda e: e.memset(xbuf_.a, 0.0), writes=[xbuf_.r])
    S.add("pool", lambda e: e.memset(S32.a, 0.0), writes=[S32.r])
    S.add("pool", lambda e: e.memset(Sbf.a, 0.0), writes=[Sbf.r])
    r_dt = r_acs = r_al = r_cbt = r_btk = PB[3]
    B3 = 3 * 512

    def ssd_head(r, csl, batched=False, slot=0, hs=0):
        uTB, xc = uTBs[slot], xcs[slot]
        M1b, SUb, NEGb, M1f = (m1s_b, sus_b, negs_b, m1s_f) if batched else (m1_b, su_b, neg_b, m1_f)
        zsil, xw, btok, etau, dch, xD = zsils[hs], xws[hs], btoks[hs], etaus[hs], dchs[hs], yxs[hs]
        def zf(e):
            for hf in range(2):
                for k in range(8):
                    i_ = e.matmul(PS[0:r, (4 + hf) * 512:(5 + hf) * 512], lhsT=uTB.a[:, k, csl], rhs=wB_in.a[:, k, hf * 512:(hf + 1) * 512],
                                  start=(k == 0), stop=(k == 7))
            return i_
        S.add("pe", zf, reads=[uTB.r, wB_in.r], writes=[PB[4], PB[5]])
        S.add("act", lambda e: e.activation(out=zsil.a[0:r, :], in_=PS[0:r, 4 * 512:6 * 512], func=AF.Silu), reads=[PB[4], PB[5]],
              writes=[zsil.r])

        def dtf(e):
            for k in range(8):
                i_ = e.matmul(PS[0:r, B3:B3 + 16], lhsT=uTB.a[:, k, csl], rhs=wB_in.a[:, k, 2560:2576], start=(k == 0), stop=(k == 7))
            return i_
        S.add("pe", dtf, reads=[uTB.r, wB_in.r], writes=[r_dt])
        S.add("dve", lambda e: e.tensor_tensor(out=dtt.a[0:r, :], in0=PS[0:r, B3:B3 + 16], in1=dtb_bc.a[0:r, :], op=ALU.add),
              reads=[r_dt, dtb_bc.r], writes=[dtt.r])
        S.add("act", lambda e: e.activation(out=dtt.a[0:r, :], in_=dtt.a[0:r, :], func=AF.Exp), reads=[dtt.r], writes=[dtt.r])
        S.add("act", lambda e: e.activation(out=dtt.a[0:r, :], in_=dtt.a[0:r, :], func=AF.Ln, bias=1.0), reads=[dtt.r], writes=[dtt.r])
        S.add("dve", lambda e: e.tensor_tensor(out=at.a[0:r, :], in0=dtt.a[0:r, :], in1=A_bc.a[0:r, :], op=ALU.mult),
              reads=[dtt.r, A_bc.r], writes=[at.r])
        S.add("dve", lambda e: e.tensor_copy(out=ahi.a[0:r, :], in_=at.a[0:r, :]), reads=[at.r], writes=[ahi.r])
        S.add("dve", lambda e: e.tensor_tensor(out=alo.a[0:r, :], in0=at.a[0:r, :], in1=ahi.a[0:r, :], op=ALU.subtract),
              reads=[at.r, ahi.r], writes=[alo.r])
        for (src, dst) in ((ahi, Rhi), (alo, Rlo)):
            S.add("dve", lambda e, src=src, dst=dst: e.tensor_tensor(out=dst.a[0:r, :, 0:r], in0=V(src.a, 0, r, 0, [(1, 16), (0, r)]),
                                                                     in1=V(M1b.a, 0, r, 0, [(0, 16), (1, r)]), op=ALU.mult),
                  reads=[src.r, M1b.r], writes=[dst.r])
        pb0 = bankb(2)

        def trx(e):
            for j in range(8):
                i_ = e.transpose(out=pb0[0:r, j * 128:(j + 1) * 128], in_=xc.a[:, j, csl], identity=ident_b.a)
            return i_
        S.add("pe", trx, reads=[xc.r, ident_b.r], writes=[PB[2]])
        pbk = PS[:, B3 + 384:B3 + 512].bitcast(BF16)

        def trb(e):
            for g in range(2):
                i_ = e.transpose(out=pbk[0:r, g * 128:(g + 1) * 128], in_=xc.a[:, 8 + g, csl], identity=ident_b.a)
            return i_
        S.add("pe", trb, reads=[xc.r, ident_b.r], writes=[r_btk])
        S.add("act", lambda e: e.copy(out=btok.a[0:r, :], in_=pbk[0:r, :]), reads=[r_btk], writes=[btok.r])
        S.add("dve", lambda e: e.tensor_tensor(out=xdt.a[0:r, :].rearrange("p (h q) -> p h q", h=16),
                                               in0=pb0[0:r, :].rearrange("p (h q) -> p h q", h=16),
                                               in1=V(dtt.a, 0, r, 0, [(1, 16), (0, 64)]), op=ALU.mult),
              reads=[PB[2], dtt.r], writes=[xdt.r])
        S.add("dve", lambda e: e.tensor_tensor(out=xD.a[0:r, :].rearrange("p (h q) -> p h q", h=16),
                                               in0=pb0[0:r, :].rearrange("p (h q) -> p h q", h=16),
                                               in1=V(D_bc.a, 0, r, 0, [(1, 16), (0, 64)]), op=ALU.mult),
              reads=[PB[2], D_bc.r], writes=[xD.r])

        for rnd in range(2):
            def dff(e, rnd=rnd):
                for qq in range(2):
                    q = rnd * 2 + qq
                    o = PS[0:r, (4 + qq) * 512:(4 + qq) * 512 + 4 * r].rearrange("p (h l) -> p h l", h=4)
                    e.matmul(o, lhsT=SUb.a[0:r, 0:r], rhs=Rhi.a[0:r, 4 * q:4 * q + 4, 0:r], start=True, stop=False)
                    e.matmul(o, lhsT=SUb.a[0:r, 0:r], rhs=Rlo.a[0:r, 4 * q:4 * q + 4, 0:r], start=False, stop=False)
                    i_ = e.matmul(o, lhsT=ident_b.a[0:r, 0:r], rhs=NEGb.a[0:r, :, 0:r], start=False, stop=True)
                return i_
            S.add("pe", dff, reads=[SUb.r, Rhi.r, Rlo.r, ident_b.r, NEGb.r], writes=[PB[4], PB[5]])
            for qq in range(2):
                q = rnd * 2 + qq
                S.add("act", lambda e, q=q, qq=qq: e.activation(out=decay.a[0:r, 4 * q:4 * q + 4, 0:r],
                                                                in_=PS[0:r, (4 + qq) * 512:(4 + qq) * 512 + 4 * r].rearrange("p (h l) -> p h l", h=4),
                                                                func=AF.Exp),
                      reads=[PB[4 + qq]], writes=[decay.r])

        def cbf_(e):
            for g in range(2):
                i_ = e.matmul(PS[0:r, B3 + 64 + g * 128:B3 + 64 + g * 128 + r], lhsT=xc.a[:, 8 + g, csl], rhs=xc.a[:, 10 + g, csl],
                              start=True, stop=True)
            return i_
        S.add("pe", cbf_, reads=[xc.r], writes=[r_cbt])
        S.add("act", lambda e: e.copy(out=cbt.a[0:r, :, 0:r], in_=PS[0:r, B3 + 64:B3 + 320].rearrange("p (g l) -> p g l", g=2)[:, :, 0:r]),
              reads=[r_cbt], writes=[cbt.r])
        S.add("dve", lambda e: e.tensor_tensor(out=V(Mm.a, 0, r, 0, [(8 * 128, 2), (128, 8), (1, r)]),
                                               in0=V(decay.a, 0, r, 0, [(8 * 128, 2), (128, 8), (1, r)]),
                                               in1=V(cbt.a, 0, r, 0, [(128, 2), (0, 8), (1, r)]), op=ALU.mult),
              reads=[decay.r, cbt.r], writes=[Mm.r])
        if batched:
            S.add("dve", lambda e: e.tensor_reduce(out=dte_.a[0:r, :], in_=decay.a[0:r, :, r - 16:r], axis=mybir.AxisListType.X, op=ALU.add),
                  reads=[decay.r], writes=[dte_.r])
            S.add("dve", lambda e: e.tensor_tensor(out=xw.a[0:r, :].rearrange("p (h q) -> p h q", h=16),
                                                   in0=xdt.a[0:r, :].rearrange("p (h q) -> p h q", h=16),
                                                   in1=V(dte_.a, 0, r, 0, [(1, 16), (0, 64)]), op=ALU.mult),
                  reads=[xdt.r, dte_.r], writes=[xw.r])
        else:
            S.add("dve", lambda e: e.tensor_tensor(out=xw.a[0:r, :].rearrange("p (h q) -> p h q", h=16),
                                                   in0=xdt.a[0:r, :].rearrange("p (h q) -> p h q", h=16),
                                                   in1=V(decay.a, 0, r, r - 1, [(128, 16), (0, 64)]), op=ALU.mult),
                  reads=[xdt.r, decay.r], writes=[xw.r])

        if batched:
            S.add("dve", lambda e: e.tensor_tensor(out=am_.a[0:r, :].rearrange("p (b h) -> p b h", b=NSEQ_S),
                                                   in0=V(at.a, 0, r, 0, [(0, NSEQ_S), (1, 16)]),
                                                   in1=V(rowsel_f.a, 0, r, 0, [(1, NSEQ_S), (0, 16)]), op=ALU.mult),
                  reads=[at.r, rowsel_f.r], writes=[am_.r])

            def acf(e):
                e.matmul(PS[0:r, B3 + 16:B3 + 32], lhsT=M1f.a[0:r, 0:r], rhs=at.a[0:r, :], start=True, stop=True)
                return e.matmul(PS[:, B3 + 64:B3 + 64 + NSEQ_S * 16], lhsT=ones_f.a[0:r, :], rhs=am_.a[0:r, :], start=True, stop=True)
            S.add("pe", acf, reads=[M1f.r, ones_f.r, at.r, am_.r], writes=[r_acs, r_al])
            S.add("act", lambda e: e.activation(out=etau.a[0:r, :], in_=PS[0:r, B3 + 16:B3 + 32], func=AF.Exp), reads=[r_acs], writes=[etau.r])
            S.add("act", lambda e: e.activation(out=dch_all.a, in_=PS[:, B3 + 64:B3 + 64 + NSEQ_S * 16], func=AF.Exp), reads=[r_al],
                  writes=[dch_all.r])
        else:
            def acf(e):
                e.matmul(PS[0:r, B3 + 16:B3 + 32], lhsT=M1f.a[0:r, 0:r], rhs=at.a[0:r, :], start=True, stop=True)
                return e.matmul(PS[:, B3 + 32:B3 + 48], lhsT=ones_f.a[0:r, :], rhs=at.a[0:r, :], start=True, stop=True)
            S.add("pe", acf, reads=[M1f.r, ones_f.r, at.r], writes=[r_acs, r_al])
            S.add("act", lambda e: e.activation(out=etau.a[0:r, :], in_=PS[0:r, B3 + 16:B3 + 32], func=AF.Exp), reads=[r_acs], writes=[etau.r])
            S.add("act", lambda e: e.activation(out=dch.a, in_=PS[:, B3 + 32:B3 + 48], func=AF.Exp), reads=[r_al], writes=[dch.r])

        def ydf(e):
            for h in range(16):
                i_ = e.matmul(PS[0:r, 4 * 512 + h * 64:4 * 512 + (h + 1) * 64], lhsT=Mm.a[0:r, h, 0:r], rhs=xdt.a[0:r, h * 64:(h + 1) * 64],
                              start=True, stop=True)
            return i_
        S.add("pe", ydf, reads=[Mm.r, xdt.r], writes=[PB[4], PB[5]])

        S.add("dve", lambda e: e.tensor_tensor(out=xD.a[0:r, :], in0=PS[0:r, 4 * 512:6 * 512], in1=xD.a[0:r, :], op=ALU.add),
              reads=[PB[4], PB[5], xD.r], writes=[xD.r])

    def ssd_tail(r, csl, batched=False, slot=0, hs=0):
        uTB, xc = uTBs[slot], xcs[slot]
        M1b, SUb, NEGb, M1f = (m1s_b, sus_b, negs_b, m1s_f) if batched else (m1_b, su_b, neg_b, m1_f)
        zsil, xw, btok, etau, dch, xD = zsils[hs], xws[hs], btoks[hs], etaus[hs], dchs[hs], yxs[hs]
        pb6 = bankb(6)
        if not batched:
            def zzf(e):
                for g in range(2):
                    i_ = e.matmul(PS[0:r, (6 + g) * 512:(7 + g) * 512], lhsT=xc.a[:, 10 + g, csl], rhs=Sbf.a[:, g * 512:(g + 1) * 512],
                                  start=True, stop=True)
                return i_
            S.add("pe", zzf, reads=[xc.r, Sbf.r], writes=[PB[6], PB[7]])
            zb = 6
        else:
            zzc = [0]

            def seq_chain(par):
                S32x, Sbfx, sldx = S32s[par], Sbfs[par], slds[par]
                bk = 4 + 2 * par
                for b in range(par, NSEQ_S, 2):
                    cm = Cmb[par]
                    bt = btm[par]
                    dma("sp", sldx.a, st_ssm[b].rearrange("(c q) n -> q c n", q=128), writes=[sldx.r], key="sld%d" % par)

                    def trl(e):
                        for c in range(8):
                            i_ = e.transpose(out=PS[:, bk * 512 + c * 128:bk * 512 + (c + 1) * 128], in_=sldx.a[:, c, :], identity=ident_f.a)
                        return i_
                    S.add("pe", trl, reads=[sldx.r, ident_f.r], writes=[PB[bk], PB[bk + 1]])
                    S.add("dve", lambda e: e.tensor_copy(out=S32x.a, in_=PS[:, bk * 512:(bk + 2) * 512]), reads=[PB[bk], PB[bk + 1]],
                          writes=[S32x.r])
                    S.add("act", lambda e: e.copy(out=Sbfx.a, in_=S32x.a), reads=[S32x.r], writes=[Sbfx.r])
                    S.add("dve", lambda e, b=b, cm=cm: e.tensor_tensor(out=cm.a, in0=xc.a[:, 10:12, 0:r],
                                                                       in1=V(colsel_b.a, 0, 128, b * NS, [(0, 2), (1, r)]), op=ALU.mult),
                          reads=[xc.r, colsel_b.r], writes=[cm.r])
                    S.add("dve", lambda e, b=b, bt=bt: e.tensor_scalar(out=bt.a[0:r, :], in0=btok.a[0:r, :], scalar1=rowsel_f.a[0:r, b:b + 1],
                                                                       scalar2=None, op0=ALU.mult),
                          reads=[btok.r, rowsel_f.r], writes=[bt.r])
                    zi = zzc[0]
                    zzc[0] += 1

                    def zzb(e, cm=cm, zi=zi):
                        for g in range(2):
                            i_ = e.matmul(PS[0:r, (1 + g) * 512:(2 + g) * 512], lhsT=cm.a[:, g, :], rhs=Sbfx.a[:, g * 512:(g + 1) * 512],
                                          start=(zi == 0), stop=(zi == NSEQ_S - 1))
                        return i_
                    S.add("pe", zzb, reads=[cm.r, Sbfx.r], writes=[PB[1], PB[2]])

                    def sub(e, bt=bt):
                        for g in range(2):
                            i_ = e.matmul(PS[:, (bk + g) * 512:(bk + 1 + g) * 512], lhsT=bt.a[0:r, g * 128:(g + 1) * 128],
                                          rhs=xw.a[0:r, g * 512:(g + 1) * 512], start=True, stop=True)
                        return i_
                    S.add("pe", sub, reads=[bt.r, xw.r], writes=[PB[bk], PB[bk + 1]])
                    S.add("dve", lambda e, b=b: e.tensor_tensor(out=S32x.a.rearrange("p (h q) -> p h q", h=16),
                                                                in0=S32x.a.rearrange("p (h q) -> p h q", h=16),
                                                                in1=V(dch_all.a, 0, 128, b * 16, [(1, 16), (0, 64)]), op=ALU.mult),
                          reads=[S32x.r, dch_all.r], writes=[S32x.r])
                    S.add("dve", lambda e: e.tensor_tensor(out=S32x.a, in0=PS[:, bk * 512:(bk + 2) * 512], in1=S32x.a, op=ALU.add),
                          reads=[PB[bk], PB[bk + 1], S32x.r], writes=[S32x.r])
                    state_out(o_sssm[b], S32x, sldx, bk, "oss%d" % par)
            if os.environ.get("MK_NOIL"):
                seq_chain(0)
                seq_chain(1)
            else:
                interleave(S, [lambda: seq_chain(0), lambda: seq_chain(1)])
            zb = 1
        S.add("dve", lambda e, zb=zb: e.tensor_tensor(out=t1.a[0:r, :].rearrange("p (h q) -> p h q", h=16),
                                                      in0=PS[0:r, zb * 512:(zb + 2) * 512].rearrange("p (h q) -> p h q", h=16),
                                                      in1=V(etau.a, 0, r, 0, [(1, 16), (0, 64)]), op=ALU.mult),
              reads=[PB[zb], PB[zb + 1], etau.r], writes=[t1.r])
        S.add("dve", lambda e: e.tensor_tensor(out=t1.a[0:r, :], in0=t1.a[0:r, :], in1=xD.a[0:r, :], op=ALU.add),
              reads=[t1.r, xD.r], writes=[t1.r])
        S.add("dve", lambda e: e.tensor_tensor(out=t1.a[0:r, :], in0=t1.a[0:r, :], in1=zsil.a[0:r, :], op=ALU.mult),
              reads=[t1.r, zsil.r], writes=[t1.r])
        for g in range(2):
            S.add("dve", lambda e, g=g: e.scalar_tensor_tensor(out=t2.a[0:r, g * 512:(g + 1) * 512], in0=t1.a[0:r, g * 512:(g + 1) * 512],
                                                               scalar=1.0, in1=t1.a[0:r, g * 512:(g + 1) * 512], op0=ALU.mult,
                                                               op1=ALU.mult, accum_out=gss.a[0:r, g:g + 1]),
                  reads=[t1.r], writes=[t2.r, gss.r])
        S.add("act", lambda e: e.activation(out=gss.a[0:r, :], in_=gss.a[0:r, :], func=AF.Sqrt, bias=eps_c.a[0:r, :], scale=1.0 / 512),
              reads=[gss.r, eps_c.r], writes=[gss.r])
        S.add("dve", lambda e: e.reciprocal(out=gss.a[0:r, :], in_=gss.a[0:r, :]), reads=[gss.r], writes=[gss.r])
        for g in range(2):
            S.add("dve", lambda e, g=g: e.scalar_tensor_tensor(out=yn.a[0:r, g * 512:(g + 1) * 512], in0=t1.a[0:r, g * 512:(g + 1) * 512],
                                                               scalar=gss.a[0:r, g:g + 1], in1=snorm_bc.a[0:r, g * 512:(g + 1) * 512],
                                                               op0=ALU.mult, op1=ALU.mult),
                  reads=[t1.r, gss.r, snorm_bc.r], writes=[yn.r])

        def try_(e):
            for j in range(8):
                i_ = e.transpose(out=pb6[:, j * 128:j * 128 + r], in_=yn.a[0:r, j * 128:(j + 1) * 128], identity=ident_b.a[0:r, 0:r])
            return i_
        S.add("pe", try_, reads=[yn.r, ident_b.r], writes=[PB[6]])
        S.add("act", lambda e: e.copy(out=yT.a[:, :, csl], in_=pb6.rearrange("p (j t) -> p j t", j=8)[:, :, 0:r]), reads=[PB[6]],
              writes=[yT.r])
        if not batched:

            def suf(e):
                for g in range(2):
                    i_ = e.matmul(PS[:, (6 + g) * 512:(7 + g) * 512], lhsT=btok.a[0:r, g * 128:(g + 1) * 128], rhs=xw.a[0:r, g * 512:(g + 1) * 512],
                                  start=True, stop=True)
                return i_
            S.add("pe", suf, reads=[btok.r, xw.r], writes=[PB[6], PB[7]])
            S.add("dve", lambda e: e.tensor_tensor(out=S32.a.rearrange("p (h q) -> p h q", h=16), in0=S32.a.rearrange("p (h q) -> p h q", h=16),
                                                   in1=V(dch.a, 0, 128, 0, [(1, 16), (0, 64)]), op=ALU.mult),
                  reads=[S32.r, dch.r], writes=[S32.r])
            S.add("dve", lambda e: e.tensor_tensor(out=S32.a, in0=PS[:, 6 * 512:8 * 512], in1=S32.a, op=ALU.add),
                  reads=[PB[6], PB[7], S32.r], writes=[S32.r])
            S.add("act", lambda e: e.copy(out=Sbf.a, in_=S32.a), reads=[S32.r], writes=[Sbf.r])

    def ssd_chunk(r, csl, batched=False, slot=0, hs=0):
        ssd_head(r, csl, batched, slot, hs)
        ssd_tail(r, csl, batched, slot, hs)

    def state_out(dst2d, S32x=None, sldx=None, bk=6, key="oss"):
        S32x = S32x or S32
        sldx = sldx or sload

        def trs(e):
            for c in range(8):
                i_ = e.transpose(out=PS[:, bk * 512 + c * 128:bk * 512 + (c + 1) * 128], in_=S32x.a[:, c * 128:(c + 1) * 128], identity=ident_f.a)
            return i_
        S.add("pe", trs, reads=[S32x.r, ident_f.r], writes=[PB[bk], PB[bk + 1]])
        S.add("act", lambda e: e.copy(out=sldx.a, in_=PS[:, bk * 512:(bk + 2) * 512].rearrange("p (c n) -> p c n", c=8)),
              reads=[PB[bk], PB[bk + 1]], writes=[sldx.r])
        dma("sp", dst2d.rearrange("(c q) n -> q c n", q=128), sldx.a, reads=[sldx.r], key=key)

    hslot = [0]

    def passB_front(tiles, kind, slot, part=None):
        uTB, xc = uTBs[slot], xcs[slot]
        N = sum(tile_rows(i)[1] for i in tiles)
        tinfo = []
        col = 0
        for i in tiles:
            rb, r = tile_rows(i)
            if part in (None, 0):
                xb = feB["xt"][xslot[0] % len(feB["xt"])]
                xslot[0] += 1
                for (src, p0, rr) in tile_src(i):
                    dma("sp", xb.a[p0:p0 + rr, :], src, writes=[xb.r], key=rot("xa", 4))
                rms_to_uT(feB, xb, r, 34, uTB, col, tbank=0)
            tinfo.append((rb, r, col))
            col += r
        last_prompt = (kind == "P" and tiles[-1] == 16)
        if kind == "S" and part in (None, 0):
            dma("sp", stinB.a, st_xbc.rearrange("b r d -> (b r) d"), writes=[stinB.r], key="sti")
            for j in range(12):
                def tr(e, j=j):
                    return e.transpose(out=PS[0:128, 0:48], in_=stinB.a[0:48, j * 128:(j + 1) * 128],
                                       identity=ident_f.a[0:48, 0:48])
                S.add("pe", tr, reads=[stinB.r, ident_f.r], writes=[PB[0]])
                outv = V(xbuf_.a, 0, 128, j * XW, [(LSB, 16), (1, 3)])
                inv = V(PS, 0, 128, 0, [(3, 16), (1, 3)])
                S.add("act", lambda e, outv=outv, inv=inv: e.copy(out=outv, in_=inv), reads=[PB[0]], writes=[xbuf_.r])
        jr = range(12) if part is None else (range(0, 6) if part == 0 else range(6, 12))
        for j in jr:
            def mx(e, j=j):
                for k in range(8):
                    i_ = e.matmul(bank(1, N), lhsT=wB_in.a[:, k, 1024 + j * 128:1024 + (j + 1) * 128], rhs=uTB.a[:, k, 0:N],
                                  start=(k == 0), stop=(k == 7))
                return i_
            S.add("pe", mx, reads=[wB_in.r, uTB.r], writes=[PB[1]])
            if kind == "P":
                S.add("act", lambda e, j=j: e.copy(out=xbuf_.a[:, j, 3:3 + N], in_=bank(1, N)), reads=[PB[1]], writes=[xbuf_.r])
                if last_prompt and "c" not in SK:
                    S.add("dve", lambda e, j=j: e.tensor_copy(out=xbc32p.a[:, j, 0:4], in_=bank(1, N)[:, N - 4:N]), reads=[PB[1]],
                          writes=[xbc32p.r])

                def cv(e, j=j):
                    for k in range(4):
                        i_ = e.matmul(bank(0, N), lhsT=diag4.a[:, j, k, :], rhs=xbuf_.a[:, j, k:k + N], start=(k == 0), stop=(k == 3))
                    return i_
                S.add("pe", cv, reads=[diag4.r, xbuf_.r], writes=[PB[0]])
                cin = bank(0, N)
                cout = xc.a[:, j, 0:N]
            else:
                outv = V(xbuf_.a, 0, 128, j * XW + 3, [(LSB, 16), (1, 4)])
                inv = V(PS, 0, 128, 512, [(1, 16), (16, 4)])
                S.add("act", lambda e, outv=outv, inv=inv: e.copy(out=outv, in_=inv), reads=[PB[1]], writes=[xbuf_.r])
                S.add("dve", lambda e, j=j: e.tensor_copy(out=xbc32s.a[:, j, 0:48], in_=bank(1, N)[:, 16:64]), reads=[PB[1]],
                      writes=[xbc32s.r])

                def cv(e, j=j):
                    for k in range(4):
                        i_ = e.matmul(bank(0, 109), lhsT=diag4.a[:, j, k, :], rhs=xbuf_.a[:, j, k:k + 109], start=(k == 0), stop=(k == 3))
                    return i_
                S.add("pe", cv, reads=[diag4.r, xbuf_.r], writes=[PB[0]])
                cin = V(PS, 0, 128, 0, [(LSB, 16), (1, 4)])
                cout = V(xc.a, 0, 128, j * NB, [(1, 16), (16, 4)])
            S.add("act", lambda e, j=j, cin=cin, cout=cout: e.activation(out=cout, in_=cin, func=AF.Silu, bias=vecB.a[:, j, 4:5]),
                  reads=[PB[0], vecB.r], writes=[xc.r])
        if kind == "P" and part in (None, 1):
            S.add("pool", lambda e: e.tensor_copy(out=xbh.a, in_=xbuf_.a[:, :, N:N + 3]), reads=[xbuf_.r], writes=[xbh.r])
            S.add("pool", lambda e: e.tensor_copy(out=xbuf_.a[:, :, 0:3], in_=xbh.a), reads=[xbh.r], writes=[xbuf_.r])
        return tinfo, N, last_prompt

    def passB_end(tiles, kind, slot, tinfo, N, last_prompt):
        uTB, xc = uTBs[slot], xcs[slot]
        hps = []
        for ti_, (rb, r, c0) in enumerate(tinfo):
            hp = hpB[ti_ % 2] if kind == "P" else hpB[0]
            hps.append(hp)
            dma("sp", hp.a[0:r, :], h_a[rb:rb + r, :], reads=[r_ha], writes=[hp.r], key=rot("hb", 2))
        for ti_, (rb, r, c0) in enumerate(tinfo):
            hp = hps[ti_]
            ho = houtB[0]
            hslot[0] += 1

            def op_(e, r=r, c0=c0):
                for hf in range(2):
                    for k in range(8):
                        i_ = e.matmul(PS[0:r, (6 + hf) * 512:(7 + hf) * 512], lhsT=yT.a[:, k, c0:c0 + r],
                                      rhs=wB_out.a[:, k, hf * 512:(hf + 1) * 512], start=(k == 0), stop=(k == 7))
                return i_
            S.add("pe", op_, reads=[yT.r, wB_out.r], writes=[PB[6], PB[7]])
            S.add("dve", lambda e, r=r, ho=ho, hp=hp: e.tensor_tensor(out=ho.a[0:r, :], in0=PS[0:r, 6 * 512:8 * 512], in1=hp.a[0:r, :],
                                                                      op=ALU.add),
                  reads=[PB[6], PB[7], hp.r], writes=[ho.r])
            dma("sp", h_b[rb:rb + r, :], ho.a[0:r, :], reads=[ho.r], writes=[r_hb], key=rot("hbo", 2))
        if (last_prompt or kind == "S") and "x" not in SK:
            nr = 4 if kind == "P" else 48
            xbc32 = xbc32p if kind == "P" else xbc32s
            for grp in range(3):
                def tro(e, grp=grp):
                    for jj in range(4):
                        j = grp * 4 + jj
                        i_ = e.transpose(out=PS[0:nr, 6 * 512 + jj * 128:6 * 512 + (jj + 1) * 128], in_=xbc32.a[:, j, 0:nr], identity=ident_f.a)
                    return i_
                S.add("pe", tro, reads=[xbc32.r, ident_f.r], writes=[PB[6]])
                S.add("act", lambda e, grp=grp: e.copy(out=stoutB.a[0:nr, grp * 512:(grp + 1) * 512], in_=PS[0:nr, 6 * 512:7 * 512]),
                      reads=[PB[6]], writes=[stoutB.r])
            if kind == "P":
                dma("sp", o_pxbc, stoutB.a[1:4, :], reads=[stoutB.r], key="oc")
            else:
                for l in range(3):
                    dma("sp", o_sxbc[:, l, :], stoutB.a[16 * l:16 * l + 16, :], reads=[stoutB.r], key="oc")

    blB = blocks(NB // 128)
    if STOP.startswith("B") and len(STOP) > 1:
        blB = blB[:int(STOP[1:])]
    if os.environ.get("MK_SKIPB"):
        blB = []
    prB = [t for t in blB if t[0] != 17]
    NOIL = bool(os.environ.get("MK_NOIL"))
    frs = {}
    TB_ = [int(x) for x in os.environ.get('MK_TB', '1,1,1').split(',')]
    chunksB = []
    for bi_, tiles in enumerate(prB):
        c0 = 0
        for ti_, i in enumerate(tiles):
            r = tile_rows(i)[1]
            chunksB.append((bi_, r, c0, ti_ == 0, ti_ == len(tiles) - 1))
            c0 += r
    hasS = any(t[0] == 17 for t in blB)
    if prB:
        frs[0] = passB_front(prB[0], "P", 0)
    for k in range(len(chunksB) + 1):
        chains = []
        turns = []
        if k >= 1:
            (pb_, pr_, pc0, _, plast) = chunksB[k - 1]

            def tailc(pb_=pb_, pr_=pr_, pc0=pc0, plast=plast, k=k):
                ssd_tail(pr_, slice(pc0, pc0 + pr_), False, pb_ % 2, (k - 1) % 2)
                if plast:
                    if prB[pb_][-1] == 16 and "s" not in SK:
                        state_out(o_pssm)
                    passB_end(prB[pb_], "P", pb_ % 2, *frs[pb_])
            chains.append(tailc)
            turns.append(TB_[0])
        if k < len(chunksB):
            (cb_, cr_, cc0, cfirst, clast) = chunksB[k]
            chains.append(lambda cb_=cb_, cr_=cr_, cc0=cc0, k=k: ssd_head(cr_, slice(cc0, cc0 + cr_), False, cb_ % 2, k % 2))
            turns.append(TB_[1])
            if cb_ + 1 < len(prB) or hasS:
                nb = cb_ + 1
                parts = ([0] if cfirst else []) + ([1] if clast else [])

                def frc(nb=nb, parts=parts):
                    for p_ in parts:
                        if nb < len(prB):
                            frs[nb] = passB_front(prB[nb], "P", nb % 2, part=p_)
                        else:
                            frs[nb] = passB_front([17], "S", nb % 2, part=p_)
                chains.append(frc)
                turns.append(TB_[2])
        if NOIL:
            for c_ in chains:
                c_()
        else:
            interleave(S, chains, turns=turns)
    if any(t[0] == 17 for t in blB):
        S.barrier()
        dma("pool", m1s_b.a, c_m1s, writes=[m1s_b.r], key="c2")
        dma("pool", sus_b.a, c_sus, writes=[sus_b.r], key="c3")
        dma("pool", negs_b.a, c_negs, writes=[negs_b.r], key="c4")
        dma("pool", colsel_b.a, c_colsel, writes=[colsel_b.r], key="c5")
        dma("sp", m1s_f.a, c_m1s, writes=[m1s_f.r], key="c0")
        dma("sp", rowsel_f.a, c_rowsel, writes=[rowsel_f.r], key="c1")
        sl_ = len(prB) % 2
        fS = frs[len(prB)] if len(prB) in frs else passB_front([17], "S", sl_)
        ssd_chunk(NS, slice(0, NS), batched=True, slot=sl_, hs=0)
        passB_end([17], "S", sl_, *fS)
    if STOP.startswith("B"):
        S.emit(nc)
        return nc

    S.barrier()
    AR.reset(pass_mark)

    NC = 512
    NQ = 11
    w_up_v = w_up.rearrange("(k p) n -> p k n", p=128)
    w_down_v = w_down.rearrange("(k p) n -> p k n", p=128)
    wC_up = AR.alloc("wC_up", [128, 8, 2 * NQ * 128], BF16)
    wC_dn = AR.alloc("wC_dn", [128, NQ, D], BF16)
    diag3 = AR.alloc("diag3", [128, 2 * NQ, 3, 128], BF16)
    feC = make_front(2)
    hres = [AR.alloc("hres%d" % i, [128, D], F32) for i in range(2)]
    uTCs = [AR.alloc("uTC%d" % i, [128, 8, NC], BF16) for i in range(2)]
    junk2 = AR.alloc("junk2", [128, D], BF16)
    ss2 = AR.alloc("ss2", [128, 1], F32)
    rstd2 = AR.alloc("rstd2", [128, 1], F32)
    LSC = 6
    FW = max(2 + NC, NSEQ_S * LSC)
    fb = [[AR.alloc("fb%d%d" % (s, v), [128, FW], BF16) for v in range(2)] for s in range(2)]
    fhist = AR.alloc("fhist", [128, 2 * NQ, 2], BF16)
    f32hp = AR.alloc("f32hp", [128, 2 * NQ, 2], F32)
    f32hs = AR.alloc("f32hs", [128, 2 * NQ, 32], F32)
    sgc = [AR.alloc("sgc%d" % i, [128, NC], F32) for i in range(2)]
    acts = [AR.alloc("act%d" % i, [128, NQ, NC], BF16) for i in range(2)]
    houtC = [AR.alloc("houtC%d" % i, [128, D], F32) for i in range(2)]
    youtC = [AR.alloc("youtC%d" % i, [128, D], F32) for i in range(2)]
    stinC = AR.alloc("stinC", [32, 2 * NQ * 128], F32)
    stoutC = AR.alloc("stoutC", [32, 512], F32)

    def passC(half):
        q0 = half * NQ
        hsrc, r_src = (h_b, r_hb) if half == 0 else (h_c, r_hc)
        for k in range(8):
            dma("pool", wC_up.a[:, k, 0:NQ * 128], w_up_v[:, k, q0 * 128:(q0 + NQ) * 128], writes=[wC_up.r], key="wCu", chain=False)
            dma("pool", wC_up.a[:, k, NQ * 128:2 * NQ * 128], w_up_v[:, k, DFF + q0 * 128:DFF + (q0 + NQ) * 128], writes=[wC_up.r],
                key="wCu", chain=False)
        dma("pool", wC_dn.a, w_down_v[:, q0:q0 + NQ, :], writes=[wC_dn.r], key="wCd", chain=False)

        def vcj(t):
            return (q0 + t) if t < NQ else (22 + q0 + t - NQ)
        for t in range(2 * NQ):
            S.add("dve", lambda e, t=t: e.tensor_tensor(out=diag3.a[:, t, :, :], in0=V(ident_b.a, 0, 128, 0, [(0, 3), (1, 128)]),
                                                        in1=V(vecC.a, 0, 128, vcj(t) * 4, [(1, 3), (0, 128)]), op=ALU.mult),
                  reads=[ident_b.r, vecC.r], writes=[diag3.r])
        fhist_r = [Res("fh%d" % t) for t in range(2 * NQ)]
        act_rs = [[Res("act%d_%d" % (i, q)) for q in range(NQ)] for i in range(2)]
        S.add("pool", lambda e: e.memset(fhist.a, 0.0), writes=fhist_r)
        hs = [0]
        hon = [0]

        def mainC(tiles, kind, slot):
            uTC, act_, act_r = uTCs[slot], acts[slot], act_rs[slot]
            f32h = f32hp if kind == "P" else f32hs
            N = sum(tile_rows(i)[1] for i in tiles)
            tinfo = []
            col = 0
            for i in tiles:
                rb, r = tile_rows(i)
                xb = feC["xt"][hs[0] % 2]
                hs[0] += 1
                dma("sp", xb.a[0:r, :], h_b[rb:rb + r, :], reads=[r_hb], writes=[xb.r], key=rot("xa", 4))
                rms_to_uT(feC, xb, r, 35, uTC, col, tbank=0)
                tinfo.append((rb, r, col, i))
                col += r
            last_prompt = (kind == "P" and tiles[-1] == 16)
            if kind == "S":
                dma("sp", stinC.a[:, 0:NQ * 128], st_ffn.rearrange("b r d -> (b r) d")[:, q0 * 128:(q0 + NQ) * 128], writes=[stinC.r], key="sti")
                dma("sp", stinC.a[:, NQ * 128:2 * NQ * 128], st_ffn.rearrange("b r d -> (b r) d")[:, DFF + q0 * 128:DFF + (q0 + NQ) * 128],
                    writes=[stinC.r], key="sti")
            def stage_up(q):
                fbs = fb[q % 2]
                for v in range(2):
                    t = q + v * NQ
                    bu = (1 + v) if q % 2 == 0 else (3 + v)
                    fbuf = fbs[v]

                    def up(e, t=t, bu=bu):
                        for k in range(8):
                            i_ = e.matmul(bank(bu, N), lhsT=wC_up.a[:, k, t * 128:(t + 1) * 128], rhs=uTC.a[:, k, 0:N], start=(k == 0),
                                          stop=(k == 7))
                        return i_
                    S.add("pe", up, reads=[wC_up.r, uTC.r], writes=[PB[bu]])
                    if kind == "P":
                        S.add("pool", lambda e, t=t, fbuf=fbuf: e.tensor_copy(out=fbuf.a[:, 0:2], in_=fhist.a[:, t, :]), reads=[fhist_r[t]],
                              writes=[fbuf.r])
                        S.add("act", lambda e, bu=bu, fbuf=fbuf: e.copy(out=fbuf.a[:, 2:2 + N], in_=bank(bu, N)), reads=[PB[bu]],
                              writes=[fbuf.r])
                        S.add("pool", lambda e, t=t, fbuf=fbuf: e.tensor_copy(out=fhist.a[:, t, :], in_=fbuf.a[:, N:N + 2]), reads=[fbuf.r],
                              writes=[fhist_r[t]])
                        if last_prompt:
                            S.add("dve", lambda e, t=t, bu=bu: e.tensor_copy(out=f32h.a[:, t, 0:2], in_=bank(bu, N)[:, N - 2:N]),
                                  reads=[PB[bu]], writes=[f32h.r])
                    else:
                        def trh(e, t=t):
                            return e.transpose(out=PS[0:128, 0:32], in_=stinC.a[0:32, t * 128:(t + 1) * 128],
                                               identity=ident_f.a[0:32, 0:32])
                        S.add("pe", trh, reads=[stinC.r, ident_f.r], writes=[PB[0]])
                        S.add("act", lambda e, fbuf=fbuf: e.copy(out=V(fbuf.a, 0, 128, 0, [(LSC, 16), (1, 2)]),
                                                                 in_=V(PS, 0, 128, 0, [(2, 16), (1, 2)])),
                              reads=[PB[0]], writes=[fbuf.r])
                        S.add("act", lambda e, bu=bu, fbuf=fbuf: e.copy(out=V(fbuf.a, 0, 128, 2, [(LSC, 16), (1, 4)]),
                                                                        in_=V(PS, 0, 128, bu * 512, [(1, 16), (16, 4)])),
                              reads=[PB[bu]], writes=[fbuf.r])
                        S.add("dve", lambda e, t=t, bu=bu: e.tensor_copy(out=f32h.a[:, t, 0:32], in_=bank(bu, N)[:, 32:64]), reads=[PB[bu]],
                              writes=[f32h.r])

            def stage_conv(q):
                fbs = fb[q % 2]
                for v in range(2):
                    t = q + v * NQ
                    bcv = 5 + v
                    fbuf = fbs[v]
                    NN = N if kind == "P" else 94

                    def cv(e, t=t, bcv=bcv, fbuf=fbuf, NN=NN):
                        for k in range(3):
                            i_ = e.matmul(bank(bcv, NN), lhsT=diag3.a[:, t, k, :], rhs=fbuf.a[:, k:k + NN], start=(k == 0), stop=(k == 2))
                        return i_
                    S.add("pe", cv, reads=[diag3.r, fbuf.r], writes=[PB[bcv]])
                sg = sgc[q % 2]
                if kind == "P":
                    cg, cvv, so, ao = bank(5, N), bank(6, N), sg.a[:, 0:N], act_.a[:, q, 0:N]
                    si = so
                else:
                    cg = V(PS, 0, 128, 5 * 512, [(LSC, 16), (1, 4)])
                    cvv = V(PS, 0, 128, 6 * 512, [(LSC, 16), (1, 4)])
                    so = V(sg.a, 0, 128, 0, [(1, 16), (16, 4)])
                    si = so
                    ao = V(act_.a, 0, 128, q * NC, [(1, 16), (16, 4)])
                gj, vj = vcj(q), vcj(q + NQ)
                S.add("act", lambda e, cg=cg, so=so, gj=gj: e.activation(out=so, in_=cg, func=AF.Silu, bias=vecC.a[:, gj, 3:4]),
                      reads=[PB[5], vecC.r], writes=[sg.r])
                S.add("dve", lambda e, cvv=cvv, si=si, ao=ao, vj=vj: e.scalar_tensor_tensor(out=ao, in0=cvv, scalar=vecC.a[:, vj, 3:4], in1=si,
                                                                                            op0=ALU.add, op1=ALU.mult),
                      reads=[PB[6], vecC.r, sg.r], writes=[act_r[q]])

            stage_up(0)
            for q in range(1, NQ):
                stage_up(q)
                stage_conv(q - 1)
            stage_conv(NQ - 1)
            return dict(tinfo=tinfo, N=N, kind=kind, last_prompt=last_prompt, slot=slot)

        def downC(cx):
            tinfo, N, kind, last_prompt, slot = cx["tinfo"], cx["N"], cx["kind"], cx["last_prompt"], cx["slot"]
            act_, act_r = acts[slot], act_rs[slot]
            f32h = f32hp if kind == "P" else f32hs
            for (rb, r, c0, ti) in tinfo:
                ho = houtC[hon[0] % 2]
                hr = hres[hon[0] % 2]
                hon[0] += 1
                dma("sp", hr.a[0:r, :], hsrc[rb:rb + r, :], reads=[r_src], writes=[hr.r], key=rot("xr", 2))
                for hf in range(2):
                    bd = 7

                    def dn(e, r=r, c0=c0, hf=hf, bd=bd):
                        for k in range(NQ):
                            i_ = e.matmul(PS[0:r, bd * 512:(bd + 1) * 512], lhsT=act_.a[:, k, c0:c0 + r],
                                          rhs=wC_dn.a[:, k, hf * 512:(hf + 1) * 512], start=(k == 0), stop=(k == NQ - 1))
                        return i_
                    S.add("pe", dn, reads=act_r + [wC_dn.r], writes=[PB[bd]])
                    S.add("dve", lambda e, r=r, ho=ho, hr=hr, hf=hf, bd=bd: e.tensor_tensor(out=ho.a[0:r, hf * 512:(hf + 1) * 512],
                                                                                           in0=PS[0:r, bd * 512:(bd + 1) * 512],
                                                                                           in1=hr.a[0:r, hf * 512:(hf + 1) * 512], op=ALU.add),
                          reads=[PB[bd], hr.r], writes=[ho.r])
                if half == 0:
                    dma("sp", h_c[rb:rb + r, :], ho.a[0:r, :], reads=[ho.r], writes=[r_hc], key=rot("hco", 2))
                else:
                    yo = youtC[hon[0] % 2]
                    junk, ss, rstd = junk2, ss2, rstd2
                    S.add("dve", lambda e, r=r, ho=ho: e.scalar_tensor_tensor(out=junk.a[0:r, :], in0=ho.a[0:r, :], scalar=1.0, in1=ho.a[0:r, :],
                                                                              op0=ALU.mult, op1=ALU.mult, accum_out=ss.a[0:r, :]),
                          reads=[ho.r], writes=[junk.r, ss.r])
                    S.add("act", lambda e, r=r: e.activation(out=rstd.a[0:r, :], in_=ss.a[0:r, :], func=AF.Sqrt, bias=eps_c.a[0:r, :],
                                                             scale=1.0 / D),
                          reads=[ss.r, eps_c.r], writes=[rstd.r])
                    S.add("dve", lambda e, r=r: e.reciprocal(out=rstd.a[0:r, :], in_=rstd.a[0:r, :]), reads=[rstd.r], writes=[rstd.r])
                    S.add("dve", lambda e, r=r, ho=ho, yo=yo: e.scalar_tensor_tensor(out=yo.a[0:r, :], in0=ho.a[0:r, :], scalar=rstd.a[0:r, :],
                                                                                     in1=gfin_bc.a[0:r, :], op0=ALU.mult, op1=ALU.mult),
                          reads=[ho.r, rstd.r, gfin_bc.r], writes=[yo.r])
                    if 1 <= ti <= 16:
                        dma("sp", yp[128 * (ti - 1):128 * ti, :], yo.a[0:r, :], reads=[yo.r], key=rot("yo", 2))
                    elif ti == 17:
                        for l in range(LS):
                            dma("sp", ys[:, l, :], yo.a[16 * l:16 * l + 16, :], reads=[yo.r], key=rot("yo", 2))
            if last_prompt or kind == "S":
                nr = 2 if kind == "P" else 32
                for grp in range((2 * NQ + 3) // 4):
                    tl = list(range(grp * 4, min(grp * 4 + 4, 2 * NQ)))

                    def tro(e, tl=tl):
                        for jj, t in enumerate(tl):
                            i_ = e.transpose(out=PS[0:nr, 7 * 512 + jj * 128:7 * 512 + (jj + 1) * 128], in_=f32h.a[:, t, 0:nr], identity=ident_f.a)
                        return i_
                    S.add("pe", tro, reads=[f32h.r, ident_f.r], writes=[PB[7]])
                    S.add("act", lambda e, n_=len(tl): e.copy(out=stoutC.a[0:nr, 0:n_ * 128], in_=PS[0:nr, 7 * 512:7 * 512 + n_ * 128]),
                          reads=[PB[7]], writes=[stoutC.r])
                    runs = []
                    for jj, t in enumerate(tl):
                        if runs and vcj(t) == runs[-1][1] + runs[-1][2]:
                            runs[-1][2] += 1
                        else:
                            runs.append([jj, vcj(t), 1])
                    for (jj0, g0, n_) in runs:
                        if kind == "P":
                            dma("sp", o_pffn[:, g0 * 128:(g0 + n_) * 128], stoutC.a[0:2, jj0 * 128:(jj0 + n_) * 128], reads=[stoutC.r],
                                key="of", chain=False)
                        else:
                            for l in range(2):
                                dma("sp", o_sffn[:, l, g0 * 128:(g0 + n_) * 128], stoutC.a[16 * l:16 * l + 16, jj0 * 128:(jj0 + n_) * 128],
                                    reads=[stoutC.r], key="of", chain=False)

        blC = blocks(NC // 128)
        kindof = lambda tl: "S" if tl[0] == 17 else "P"
        cxs = {}
        cxs[0] = mainC(blC[0], kindof(blC[0]), 0)
        for bi_, tiles in enumerate(blC):
            dn_ = lambda bi_=bi_: downC(cxs[bi_])
            if bi_ + 1 < len(blC):
                nt = blC[bi_ + 1]

                def nf(bi_=bi_, nt=nt):
                    cxs[bi_ + 1] = mainC(nt, kindof(nt), (bi_ + 1) % 2)
                if os.environ.get("MK_NOIL"):
                    dn_()
                    nf()
                else:
                    interleave(S, [dn_, nf], turns=[int(x) for x in os.environ.get('MK_TC', '1,3').split(',')])
            else:
                dn_()

    passC(0)
    if STOP == "C0":
        S.emit(nc)
        return nc
    S.barrier()
    passC(1)

    S.emit(nc)
    return nc


_CACHE = {}


def _consts():
    j = np.arange(128)[:, None]
    l = np.arange(128)[None, :]
    m1 = (j <= l).astype(np.float32)
    su = (l < j).astype(np.float32)
    neg = np.where(j > l, NEGBIG, 0.0).astype(np.float32)
    neg4 = np.ascontiguousarray(np.broadcast_to(neg[:, None, :], (128, 4, 128)))
    return np.eye(128, dtype=np.float32), m1, su, neg4


def _consts_s():
    i = np.arange(NS)
    seq, pos = i % NSEQ_S, i // NSEQ_S
    same = seq[:, None] == seq[None, :]
    m1s = (same & (pos[:, None] <= pos[None, :])).astype(np.float32)
    sus = (same & (pos[None, :] < pos[:, None])).astype(np.float32)
    negs = np.where(m1s > 0, 0.0, NEGBIG).astype(np.float32)
    negs4 = np.ascontiguousarray(np.broadcast_to(negs[:, None, :], (NS, 4, NS)))
    rowsel = (seq[:, None] == np.arange(NSEQ_S)[None, :]).astype(np.float32)
    colsel = np.ascontiguousarray(np.broadcast_to(rowsel.T[None], (128, NSEQ_S, NS))).astype(np.float32)
    return m1s, sus, negs4, rowsel, colsel


def kernel(x_prompt, x_sample, state_conf_conv, state_xbc_conv, state_ssm, state_ffn_conv,
           meta_tokens, norm_mix_g, w_in, conf_conv_w, conf_conv_b, conf_ln_g, conf_ln_b,
           ssm_conv_w, ssm_conv_b, dt_bias, a_log, d_skip, ssm_norm_g, w_out, norm_ffn_g,
           w_up, ffn_conv_w, ffn_conv_b, w_down, norm_final_g):
    f = lambda a: np.ascontiguousarray(np.asarray(a, dtype=np.float32))
    debug = bool(int(os.environ.get("MK_DEBUG", "0")))
    if "nc" not in _CACHE:
        _CACHE["nc"] = build_program(debug)
    nc = _CACHE["nc"]
    ident, m1, su, neg4 = _consts()
    vtabA = np.concatenate([f(conf_conv_w)[0], f(conf_conv_b), f(conf_ln_g), f(conf_ln_b), f(norm_mix_g), f(norm_ffn_g)], axis=0)
    vtabB = np.concatenate([f(ssm_conv_w)[0], f(ssm_conv_b)], axis=0)
    vtabC = np.concatenate([f(ffn_conv_w)[0], f(ffn_conv_b)], axis=0)
    v16 = np.concatenate([f(dt_bias), f(a_log), f(d_skip)], axis=0)
    shared = {
        "meta": f(meta_tokens), "w_in": f(w_in)[0], "w_out": f(w_out)[0], "w_up": f(w_up)[0], "w_down": f(w_down)[0],
        "vtabA": np.ascontiguousarray(vtabA), "vtabB": np.ascontiguousarray(vtabB), "vtabC": np.ascontiguousarray(vtabC),
        "v16": np.ascontiguousarray(v16), "snorm_g": f(ssm_norm_g), "gfin": f(norm_final_g).reshape(1, D),
        "c_ident": ident, "c_m1": m1, "c_su": su, "c_neg": neg4,
    }
    m1s, sus, negs4, rowsel, colsel = _consts_s()
    shared.update({"c_m1s": m1s, "c_sus": sus, "c_negs": negs4, "c_rowsel": rowsel, "c_colsel": colsel})
    xpf, xsf = f(x_prompt), f(x_sample)
    sc, sx, ssm_, sf = f(state_conf_conv)[0], f(state_xbc_conv)[0], f(state_ssm)[0], f(state_ffn_conv)[0]
    in_maps = []
    for c in range(8):
        m = dict(shared)
        sl = slice(16 * c, 16 * c + 16)
        m["xp"] = xpf[c]
        m["xs"] = xsf[sl]
        m["st_conf"] = sc[sl]
        m["st_xbc"] = sx[sl]
        m["st_ssm"] = np.ascontiguousarray(ssm_[sl].reshape(16, 1024, 128))
        m["st_ffn"] = sf[sl]
        in_maps.append(m)
    res = run_bass_kernel_spmd(nc, in_maps, core_ids=list(range(8)))
    R = res.results
    _CACHE["last"] = R
    cat = lambda k: np.concatenate([np.asarray(R[c][k]) for c in range(8)], axis=0)
    stk = lambda k: np.stack([np.asarray(R[c][k]) for c in range(8)], axis=0)
    y_prompt = stk("yp")
    y_sample = cat("ys")
    return (y_prompt.astype(np.float32), y_sample.astype(np.float32),
            stk("o_pconf")[None].astype(np.float32), stk("o_pxbc")[None].astype(np.float32),
            stk("o_pssm").reshape(8, 16, 64, 128)[None].astype(np.float32), stk("o_pffn")[None].astype(np.float32),
            cat("o_sconf")[None].astype(np.float32), cat("o_sxbc")[None].astype(np.float32),
            cat("o_sssm").reshape(128, 16, 64, 128)[None].astype(np.float32), cat("o_sffn")[None].astype(np.float32))
```

```python
import contextlib
import os
import numpy as np
import concourse.bass as bass
import concourse.mybir as mybir
from concourse.bass_utils import run_bass_kernel_spmd

F32 = mybir.dt.float32
BF16 = mybir.dt.bfloat16
AF = mybir.ActivationFunctionType
ALU = mybir.AluOpType

D = 1024
NMETA = 16
SEQ = 2048
TP = NMETA + SEQ
NSEQ_S = 16
LS = 4
NS = NSEQ_S * LS
NTOK = TP + NS
DFF = 2816
INP = 4624
OFF_Z, OFF_XBC, OFF_DT = 2048, 3072, 4608
EPS = 1e-5
NEGBIG = -30000.0


class Res:
    __slots__ = ("name", "last_w", "rd_eng", "rd_dma", "excl")

    def __init__(self, name, excl=False):
        self.name = name
        self.excl = excl
        self.last_w = None
        self.rd_eng = {}
        self.rd_dma = []


class Op:
    __slots__ = ("eng", "fn", "deps", "is_dma", "key", "kidx", "signal", "sigcount", "idx", "tag")


class Sched:
    ENGS = ("sp", "act", "pool", "dve", "pe")

    def __init__(self):
        self.streams = {e: [] for e in self.ENGS}
        self.dma_keys = {}
        self.nops = 0

    def _add(self, eng, fn, reads=(), writes=(), dma_key=None, chain=True, tag=""):
        op = Op()
        op.eng = eng
        op.fn = fn
        op.is_dma = dma_key is not None
        op.key = dma_key
        op.signal = False
        op.sigcount = 0
        op.idx = self.nops
        op.tag = tag
        op.kidx = 0
        self.nops += 1
        deps = {}

        def dep(d, raw):
            if d is None:
                return
            p = deps.get(d.idx)
            deps[d.idx] = (d, raw or (p[1] if p else False))

        for r in reads:
            dep(r.last_w, True)
            if r.excl:
                for e2, rd in r.rd_eng.items():
                    if e2 != eng:
                        dep(rd, False)
        for w in writes:
            lw = w.last_w
            if not (dma_key is not None and not chain and lw is not None and lw.is_dma and lw.key == dma_key):
                dep(lw, False)
            for rd in w.rd_eng.values():
                dep(rd, False)
            for rd in w.rd_dma:
                dep(rd, False)
        if op.is_dma:
            lst = self.dma_keys.setdefault(dma_key, [])
            if chain and lst:
                dep(lst[-1], True)
            op.kidx = len(lst)
            lst.append(op)
        final = []
        for d, raw in deps.values():
            if d is op:
                continue
            if d.is_dma:
                final.append(d)
            elif d.eng == eng:
                if eng == "pe":
                    continue
                final.append(d)
            else:
                final.append(d)
        for d in final:
            d.signal = True
        op.deps = final
        for r in reads:
            if op.is_dma:
                r.rd_dma.append(op)
            else:
                r.rd_eng[eng] = op
        for w in writes:
            w.last_w = op
            w.rd_eng = {}
            w.rd_dma = []
        self.streams[eng].append(op)
        return op

    def add(self, *a, **k):
        return self._add(*a, **k)

    def barrier(self):
        lasts = {}
        for e in self.ENGS:
            l = None
            for op in reversed(self.streams[e]):
                if op.fn is not None and not op.is_dma:
                    l = op
                    break
            lasts[e] = l
        dmal = [l[-1] for l in self.dma_keys.values() if l]
        for e in self.ENGS:
            op = Op()
            op.eng = e
            op.fn = None
            op.is_dma = False
            op.key = None
            op.kidx = 0
            op.signal = False
            op.sigcount = 0
            op.idx = self.nops
            op.tag = "barrier"
            self.nops += 1
            deps = []
            for e2 in self.ENGS:
                l = lasts[e2]
                if e2 != e and l is not None:
                    deps.append(l)
            deps.extend(dmal)
            for d in deps:
                d.signal = True
            op.deps = deps
            self.streams[e].append(op)

    def emit(self, nc):
        for e in self.ENGS:
            c = 0
            for op in self.streams[e]:
                if op.signal and not op.is_dma and op.fn is not None:
                    c += 1
                    op.sigcount = c
        with contextlib.ExitStack() as st:
            esem = {e: st.enter_context(nc.semaphore("es_" + e)) for e in self.ENGS}
            dsem = {k: st.enter_context(nc.semaphore("ds_%d" % i)) for i, k in enumerate(self.dma_keys)}
            block = st.enter_context(nc.Block())

            def run(ename):
                def body(e):
                    waited = {}
                    for op in self.streams[ename]:
                        need = {}
                        for d in op.deps:
                            if d.is_dma:
                                s, v = dsem[d.key], 16 * (d.kidx + 1)
                            else:
                                s, v = esem[d.eng], d.sigcount
                            k = id(s)
                            if k not in need or need[k][1] < v:
                                need[k] = (s, v)
                        for k, (s, v) in need.items():
                            if waited.get(k, 0) >= v:
                                continue
                            e.wait_ge(s, v)
                            waited[k] = v
                        if op.fn is None:
                            continue
                        inst = op.fn(e)
                        if op.is_dma:
                            inst.then_inc(dsem[op.key], 16)
                        elif op.signal:
                            inst.then_inc(esem[ename], 1)
                    if ename == "sp":
                        for k, lst in self.dma_keys.items():
                            if lst:
                                v = 16 * len(lst)
                                if waited.get(id(dsem[k]), 0) < v:
                                    e.wait_ge(dsem[k], v)

                return body

            block.sync(run("sp"))
            block.scalar(run("act"))
            block.gpsimd(run("pool"))
            block.vector(run("dve"))
            block.tensor(run("pe"))


def interleave(S, fns, turns=None):
    import threading
    n = len(fns)
    turns = turns or [1] * n
    go = [threading.Semaphore(0) for _ in range(n)]
    back = threading.Semaphore(0)
    done = [False] * n
    err = []
    cur = [0]
    orig_add = S.add

    def hooked(*a, **k):
        r = orig_add(*a, **k)
        i = cur[0]
        back.release()
        go[i].acquire()
        return r

    def worker(i):
        go[i].acquire()
        try:
            fns[i]()
        except BaseException as ex:
            err.append(ex)
        done[i] = True
        back.release()

    ths = [threading.Thread(target=worker, args=(i,)) for i in range(n)]
    for t in ths:
        t.start()
    S.add = hooked
    try:
        while not all(done):
            for i in range(n):
                for _ in range(turns[i]):
                    if done[i]:
                        break
                    cur[0] = i
                    go[i].release()
                    back.acquire()
    finally:
        S.add = orig_add
    for t in ths:
        t.join()
    if err:
        raise err[0]


def V(a, p0, npart, off, dims):
    ps = a.ap[0][0]
    return bass.AP(a.tensor, a.offset + p0 * ps + off, [[ps, npart]] + [[s, n] for s, n in dims])


class Buf:
    __slots__ = ("t", "a", "r", "shape")

    def __init__(self, t, name, shape):
        self.t = t
        self.a = t.ap()
        self.r = Res(name)
        self.shape = shape


class Arena:
    def __init__(self, nc):
        self.nc = nc
        self.off = (nc.sbuf_base + 63) // 64 * 64
        self.top = nc.sbuf_top
        self.n = 0

    def alloc(self, name, shape, dt):
        sz = int(np.prod(shape[1:])) * (4 if dt == F32 else 2)
        sz = (sz + 63) // 64 * 64
        assert self.off + sz <= self.top, "SBUF overflow at %s: need %d have %d" % (name, sz, self.top - self.off)
        self.n += 1
        t = self.nc.alloc_sbuf_tensor_at("%s_%d" % (name, self.n), list(shape), dt, offset=self.off)
        self.off += sz
        return Buf(t, name, shape)

    def mark(self):
        return self.off

    def reset(self, m):
        self.off = m


def tile_rows(i):
    if i == 0:
        return 0, NMETA
    if i <= 16:
        return NMETA + 128 * (i - 1), 128
    return TP, NS


def build_program(debug=False):
    nc = bass.Bass("TRN2", target_bir_lowering=False)

    def din(name, shape):
        return nc.dram_tensor(name, list(shape), F32, kind="ExternalInput").ap()

    def dout(name, shape):
        return nc.dram_tensor(name, list(shape), F32, kind="ExternalOutput").ap()

    xp = din("xp", [SEQ, D])
    xs = din("xs", [NSEQ_S, LS, D])
    st_conf = din("st_conf", [NSEQ_S, 30, D])
    st_xbc = din("st_xbc", [NSEQ_S, 3, 1536])
    st_ssm = din("st_ssm", [NSEQ_S, 1024, 128])
    st_ffn = din("st_ffn", [NSEQ_S, 2, 2 * DFF])
    meta = din("meta", [NMETA, D])
    w_in = din("w_in", [D, INP])
    w_out = din("w_out", [2 * D, D])
    w_up = din("w_up", [D, 2 * DFF])
    w_down = din("w_down", [DFF, D])
    vtabA = din("vtabA", [36, D])
    vtabB = din("vtabB", [5, 1536])
    vtabC = din("vtabC", [4, 2 * DFF])
    v16 = din("v16", [3, 16])
    snorm_g = din("snorm_g", [1, D])
    gfin = din("gfin", [1, D])
    c_ident = din("c_ident", [128, 128])
    c_m1 = din("c_m1", [128, 128])
    c_su = din("c_su", [128, 128])
    c_neg = din("c_neg", [128, 4, 128])
    c_m1s = din("c_m1s", [NS, NS])
    c_sus = din("c_sus", [NS, NS])
    c_negs = din("c_negs", [NS, 4, NS])
    c_rowsel = din("c_rowsel", [NS, NSEQ_S])
    c_colsel = din("c_colsel", [128, NSEQ_S, NS])

    yp = dout("yp", [SEQ, D])
    ys = dout("ys", [NSEQ_S, LS, D])
    o_pconf = dout("o_pconf", [30, D])
    o_pxbc = dout("o_pxbc", [3, 1536])
    o_pssm = dout("o_pssm", [1024, 128])
    o_pffn = dout("o_pffn", [2, 2 * DFF])
    o_sconf = dout("o_sconf", [NSEQ_S, 30, D])
    o_sxbc = dout("o_sxbc", [NSEQ_S, 3, 1536])
    o_sssm = dout("o_sssm", [NSEQ_S, 1024, 128])
    o_sffn = dout("o_sffn", [NSEQ_S, 2, 2 * DFF])
    skind = "ExternalOutput" if debug else "Internal"
    h_a = nc.dram_tensor("h_a", [NTOK, D], F32, kind=skind).ap()
    h_b = nc.dram_tensor("h_b", [NTOK, D], F32, kind=skind).ap()
    h_c = nc.dram_tensor("h_c", [NTOK, D], F32, kind=skind).ap()
    r_ha, r_hb, r_hc = Res("h_a"), Res("h_b"), Res("h_c")

    S = Sched()
    AR = Arena(nc)
    PSt = nc.alloc_psum_tensor("ps", [128, 4096], F32)
    PS = PSt.ap()
    PB = [Res("bank%d" % i, excl=True) for i in range(8)]

    def bank(b, n=512):
        return PS[:, b * 512:b * 512 + n]

    def bankb(b):
        return PS[:, b * 512:(b + 1) * 512].bitcast(BF16)

    kctr = [0]

    def rot(prefix, n):
        kctr[0] += 1
        return "%s%d" % (prefix, kctr[0] % n)

    def dma(eng, out, in_, reads=(), writes=(), key=None, chain=True):
        return S.add(eng, lambda e: e.dma_start(out=out, in_=in_), reads=reads, writes=writes,
                     dma_key=key, chain=chain)

    ident_f = AR.alloc("ident_f", [128, 128], F32)
    ident_b = AR.alloc("ident_b", [128, 128], BF16)
    m1_b = AR.alloc("m1_b", [128, 128], BF16)
    su_b = AR.alloc("su_b", [128, 128], BF16)
    neg_b = AR.alloc("neg_b", [128, 4, 128], BF16)
    m1_f = AR.alloc("m1_f", [128, 128], F32)
    ones_f = AR.alloc("ones_f", [128, 128], F32)
    ones_b = AR.alloc("ones_b", [128, 128], BF16)
    vecA = AR.alloc("vecA", [128, 8, 36], F32)
    vecB = AR.alloc("vecB", [128, 12, 5], F32)
    vecC = AR.alloc("vecC", [128, 44, 4], F32)
    gfin_bc = AR.alloc("gfin_bc", [128, D], F32)
    snorm_bc = AR.alloc("snorm_bc", [128, D], F32)
    dtb_bc = AR.alloc("dtb_bc", [128, 16], F32)
    A_bc = AR.alloc("A_bc", [128, 16], F32)
    D_bc = AR.alloc("D_bc", [128, 16], F32)
    eps_c = AR.alloc("eps_c", [128, 1], F32)

    dma("sp", ident_f.a, c_ident, writes=[ident_f.r], key="c0")
    dma("sp", m1_f.a, c_m1, writes=[m1_f.r], key="c1")
    dma("pool", ident_b.a, c_ident, writes=[ident_b.r], key="c2")
    dma("pool", m1_b.a, c_m1, writes=[m1_b.r], key="c3")
    dma("pool", su_b.a, c_su, writes=[su_b.r], key="c4")
    dma("pool", neg_b.a, c_neg, writes=[neg_b.r], key="c5")
    S.add("dve", lambda e: e.memset(ones_f.a, 1.0), writes=[ones_f.r])
    S.add("dve", lambda e: e.memset(ones_b.a, 1.0), writes=[ones_b.r])
    S.add("dve", lambda e: e.memset(eps_c.a, EPS), writes=[eps_c.r])

    def bc_src(ap2d, row, n):
        return bass.AP(ap2d.tensor, ap2d.offset + row * n, [[0, 128], [1, n]])

    SK = os.environ.get("MK_SK", "")
    if "b" not in SK:
        dma("sp", gfin_bc.a, bc_src(gfin, 0, D), writes=[gfin_bc.r], key="c0")
        dma("sp", snorm_bc.a, bc_src(snorm_g, 0, D), writes=[snorm_bc.r], key="c1")
    if "d" not in SK:
        dma("sp", dtb_bc.a, bc_src(v16, 0, 16), writes=[dtb_bc.r], key="c0")
        dma("sp", A_bc.a, bc_src(v16, 1, 16), writes=[A_bc.r], key="c1")
        dma("sp", D_bc.a, bc_src(v16, 2, 16), writes=[D_bc.r], key="c0")
    S.add("act", lambda e: e.activation(out=A_bc.a, in_=A_bc.a, func=AF.Exp), reads=[A_bc.r], writes=[A_bc.r])
    S.add("dve", lambda e: e.tensor_scalar(out=A_bc.a, in0=A_bc.a, scalar1=-1.0, scalar2=None, op0=ALU.mult),
          reads=[A_bc.r], writes=[A_bc.r])

    gmark = AR.mark()
    stA = AR.alloc("stA", [36, D], F32)
    stB = AR.alloc("stB", [5, 1536], F32)
    stC = AR.alloc("stC", [4, 2 * DFF], F32)
    dma("sp", stA.a, vtabA, writes=[stA.r], key="c1")
    dma("sp", stB.a, vtabB, writes=[stB.r], key="c0")
    dma("sp", stC.a, vtabC, writes=[stC.r], key="c1")

    def vec_tr(st, nrow, nj, vec, bk):
        def f(e):
            for j in range(nj):
                i = e.transpose(out=bank(bk)[:, j * nrow:(j + 1) * nrow],
                                in_=st.a[0:nrow, j * 128:(j + 1) * 128], identity=ident_f.a[0:nrow, 0:nrow])
            return i
        S.add("pe", f, reads=[st.r, ident_f.r], writes=[PB[bk]])
        S.add("dve", lambda e: e.tensor_copy(out=vec.a, in_=bank(bk)[:, 0:nj * nrow].rearrange("p (j r) -> p j r", r=nrow)),
              reads=[PB[bk]], writes=[vec.r])

    if "v" not in SK:
        vec_tr(stA, 36, 8, vecA, 0)
        vec_tr(stB, 5, 12, vecB, 1)
        vec_tr(stC, 4, 44, vecC, 2)
    S.barrier()
    AR.reset(gmark)
    pass_mark = AR.mark()
    STOP = os.environ.get("MK_STOP", "")
    if STOP == "setup":
        S.emit(nc)
        return nc

    def make_front(nslots_x):
        fe = {}
        fe["xt"] = [AR.alloc("xt%d" % i, [128, D], F32) for i in range(nslots_x)]
        fe["junk"] = AR.alloc("junk", [128, D], BF16)
        fe["ss"] = AR.alloc("ss", [128, 1], F32)
        fe["rstd"] = AR.alloc("rstd", [128, 1], F32)
        fe["u"] = [AR.alloc("u%d" % i, [128, D], BF16) for i in range(2)]
        fe["n"] = 0
        return fe

    def tile_src(i):
        if i == 0:
            return [(meta, 0, NMETA)]
        if i <= 16:
            return [(xp[128 * (i - 1):128 * i, :], 0, 128)]
        return [(xs[:, l, :], 16 * l, 16) for l in range(LS)]

    def rms_to_uT(fe, xbuf, r, gcol, uT, col0, tbank=0):
        fe["n"] += 1
        u = fe["u"][fe["n"] % 2]
        junk, ss, rstd = fe["junk"], fe["ss"], fe["rstd"]
        xa = xbuf.a[0:r, :]
        S.add("dve", lambda e: e.scalar_tensor_tensor(out=junk.a[0:r, :], in0=xa, scalar=1.0, in1=xa, op0=ALU.mult,
                                                      op1=ALU.mult, accum_out=ss.a[0:r, :]),
              reads=[xbuf.r], writes=[junk.r, ss.r])
        S.add("act", lambda e: e.activation(out=rstd.a[0:r, :], in_=ss.a[0:r, :], func=AF.Sqrt, bias=eps_c.a[0:r, :],
                                            scale=1.0 / D),
              reads=[ss.r, eps_c.r], writes=[rstd.r])
        S.add("dve", lambda e: e.reciprocal(out=rstd.a[0:r, :], in_=rstd.a[0:r, :]), reads=[rstd.r], writes=[rstd.r])
        S.add("dve", lambda e: e.tensor_scalar(out=u.a[0:r, :], in0=xa, scalar1=rstd.a[0:r, :], scalar2=None, op0=ALU.mult),
              reads=[xbuf.r, rstd.r], writes=[u.r])
        pb = bankb(tbank)

        def tr(e):
            for j in range(8):
                i = e.transpose(out=pb[:, j * 128:j * 128 + r], in_=u.a[0:r, j * 128:(j + 1) * 128],
                                identity=ident_b.a[0:r, 0:r])
            return i
        S.add("pe", tr, reads=[u.r, ident_b.r], writes=[PB[tbank]])
        gb = V(vecA.a, 0, 128, gcol, [(36, 8), (0, r)])
        S.add("dve", lambda e: e.tensor_tensor(out=uT.a[:, :, col0:col0 + r],
                                               in0=pb.rearrange("p (j t) -> p j t", j=8)[:, :, 0:r], in1=gb, op=ALU.mult),
              reads=[PB[tbank], vecA.r], writes=[uT.r])

    def load_w(dst, src_view, key):
        dma("pool", dst.a if isinstance(dst, Buf) else dst, src_view, writes=[dst.r] if isinstance(dst, Buf) else (), key=key,
            chain=False)

    def blocks(tpb):
        bl = [[0]]
        t = 1
        while t <= 16:
            bl.append(list(range(t, min(t + tpb, 17))))
            t += tpb
        bl.append([17])
        return bl

    NA = 256
    wA_in = AR.alloc("wA_in", [128, 8, 2048], BF16)
    wA_out = AR.alloc("wA_out", [128, 8, D], BF16)
    diag31 = AR.alloc("diag31", [128, 8, 31, 128], BF16)
    w_in_v = w_in.rearrange("(k p) n -> p k n", p=128)
    w_out_v = w_out.rearrange("(k p) n -> p k n", p=128)
    for k in range(8):
        dma("pool", wA_in.a[:, k, :], w_in_v[:, k, 0:2048], writes=[wA_in.r], key="wAi", chain=False)
    dma("pool", wA_out.a, w_out_v[:, 0:8, :], writes=[wA_out.r], key="wAo", chain=False)
    for j in range(8):
        S.add("dve", lambda e, j=j: e.tensor_tensor(out=diag31.a[:, j, :, :], in0=V(ident_b.a, 0, 128, 0, [(0, 31), (1, 128)]),
                                                    in1=V(vecA.a, 0, 128, j * 36, [(1, 31), (0, 128)]), op=ALU.mult),
              reads=[ident_b.r, vecA.r], writes=[diag31.r])
    feA = make_front(4)
    uTA = AR.alloc("uTA", [128, 8, NA], BF16)
    sig = [AR.alloc("sig%d" % i, [128, NA], F32) for i in range(2)]
    LSA = 34
    GW = max(30 + NA, NSEQ_S * LSA)
    glu = AR.alloc("glu", [128, 8, GW], BF16)
    gluh = AR.alloc("gluh", [128, 8, 30], BF16)
    glu32p = AR.alloc("glu32p", [128, 8, 30], F32)
    glu32s = AR.alloc("glu32s", [128, 8, 64], F32)
    cbfs = [AR.alloc("cbf%d" % i, [128, 8, NA], BF16) for i in range(2)]
    mean_s = [AR.alloc("mean_s%d" % i, [128, NA], F32) for i in range(2)]
    ex2_s = [AR.alloc("ex2_s%d" % i, [128, NA], F32) for i in range(2)]
    csq = [AR.alloc("csq%d" % i, [128, NA], BF16) for i in range(2)]
    rsd = AR.alloc("rsd", [128, NA], F32)
    nmr = AR.alloc("nmr", [128, NA], F32)
    tn = [AR.alloc("tn%d" % i, [128, NA], F32) for i in range(2)]
    cT = AR.alloc("cT", [128, 8, NA], BF16)
    houtA = [AR.alloc("houtA%d" % i, [128, D], F32) for i in range(2)]
    stin = AR.alloc("stin", [120, D], F32)
    stout = AR.alloc("stout", [64, D], F32)
    glu_r = [Res("glu%d" % j) for j in range(8)]
    cbf_rs = [[Res("cbf%d_%d" % (i, j)) for j in range(8)] for i in range(2)]
    cT_r = [Res("cT%d" % j) for j in range(8)]
    S.add("pool", lambda e: e.memset(glu.a, 0.0), writes=glu_r)

    xslot = [0]
    houtn = [0]

    def passA_front(tiles, kind, slot):
        N = sum(tile_rows(i)[1] for i in tiles)
        xbufs = []
        col = 0
        for i in tiles:
            rb, r = tile_rows(i)
            xb = feA["xt"][xslot[0] % len(feA["xt"])]
            xslot[0] += 1
            for (src, p0, rr) in tile_src(i):
                dma("sp", xb.a[p0:p0 + rr, :], src, writes=[xb.r], key=rot("xa", 4))
            rms_to_uT(feA, xb, r, 34, uTA, col, tbank=0)
            xbufs.append((xb, rb, r, col))
            col += r
        last_prompt = (kind == "P" and tiles[-1] == 16)
        if kind == "S":
            for grp in range(4):
                dma("sp", stin.a, st_conf[4 * grp:4 * grp + 4, :, :].rearrange("b r d -> (b r) d"), writes=[stin.r], key="sti")

                def tr(e):
                    for j in range(8):
                        i_ = e.transpose(out=PS[0:128, 6 * 512 + j * 128:6 * 512 + j * 128 + 120], in_=stin.a[0:120, j * 128:(j + 1) * 128],
                                         identity=ident_f.a[0:120, 0:120])
                    return i_
                S.add("pe", tr, reads=[stin.r, ident_f.r], writes=[PB[6], PB[7]])
                outv = V(glu.a, 0, 128, grp * 4 * LSA, [(GW, 8), (LSA, 4), (1, 30)])
                inv = V(PS, 0, 128, 6 * 512, [(128, 8), (30, 4), (1, 30)])
                S.add("act", lambda e, outv=outv, inv=inv: e.copy(out=outv, in_=inv), reads=[PB[6], PB[7]], writes=glu_r)
        return dict(N=N, xbufs=xbufs, last_prompt=last_prompt, kind=kind, slot=slot)

    def passA_main(cx):
        N, kind, last_prompt, slot = cx["N"], cx["kind"], cx["last_prompt"], cx["slot"]
        cbf, cbf_r = cbfs[slot], cbf_rs[slot]
        glu32 = glu32p if kind == "P" else glu32s
        def stage1(j):
            bb, ba = 1, 2
            sg = sig[j % 2]

            def mmb(e, j=j):
                for k in range(8):
                    i_ = e.matmul(bank(bb, N), lhsT=wA_in.a[:, k, 1024 + j * 128:1024 + (j + 1) * 128], rhs=uTA.a[:, k, 0:N],
                                  start=(k == 0), stop=(k == 7))
                return i_
            S.add("pe", mmb, reads=[wA_in.r, uTA.r], writes=[PB[bb]])
            S.add("act", lambda e, sg=sg: e.activation(out=sg.a[:, 0:N], in_=bank(bb, N), func=AF.Sigmoid),
                  reads=[PB[bb]], writes=[sg.r])

            def mma(e, j=j):
                for k in range(8):
                    i_ = e.matmul(bank(ba, N), lhsT=wA_in.a[:, k, j * 128:(j + 1) * 128], rhs=uTA.a[:, k, 0:N],
                                  start=(k == 0), stop=(k == 7))
                return i_
            S.add("pe", mma, reads=[wA_in.r, uTA.r], writes=[PB[ba]])
            if kind == "P":
                S.add("dve", lambda e, j=j, sg=sg: e.tensor_tensor(out=glu.a[:, j, 30:30 + N], in0=bank(ba, N), in1=sg.a[:, 0:N],
                                                                   op=ALU.mult),
                      reads=[PB[ba], sg.r], writes=[glu_r[j]])
                if last_prompt:
                    S.add("dve", lambda e, j=j, sg=sg: e.tensor_tensor(out=glu32.a[:, j, 0:30], in0=bank(ba, N)[:, N - 30:N],
                                                                       in1=sg.a[:, N - 30:N], op=ALU.mult),
                          reads=[PB[ba], sg.r], writes=[glu32.r])
            else:
                outv = V(glu.a, 0, 128, j * GW + 30, [(LSA, 16), (1, 4)])
                in0v = V(PS, 0, 128, ba * 512, [(1, 16), (16, 4)])
                in1v = V(sg.a, 0, 128, 0, [(1, 16), (16, 4)])
                S.add("dve", lambda e, outv=outv, in0v=in0v, in1v=in1v: e.tensor_tensor(out=outv, in0=in0v, in1=in1v, op=ALU.mult),
                      reads=[PB[ba], sg.r], writes=[glu_r[j]])
                S.add("dve", lambda e, j=j, sg=sg: e.tensor_tensor(out=glu32.a[:, j, 0:N], in0=bank(ba, N), in1=sg.a[:, 0:N],
                                                                   op=ALU.mult),
                      reads=[PB[ba], sg.r], writes=[glu32.r])

        def stage2(j):
            bc = 4
            if kind == "P":
                def cv(e, j=j, bc=bc):
                    for k in range(31):
                        i_ = e.matmul(bank(bc, N), lhsT=diag31.a[:, j, k, :], rhs=glu.a[:, j, k:k + N], start=(k == 0), stop=(k == 30))
                    return i_
                S.add("pe", cv, reads=[diag31.r, glu_r[j]], writes=[PB[bc]])
                cin = bank(bc, N)
                cout = cbf.a[:, j, 0:N]
            else:
                def cv(e, j=j, bc=bc):
                    for hh in range(2):
                        for k in range(31):
                            i_ = e.matmul(PS[:, bc * 512 + hh * 256:bc * 512 + hh * 256 + 242], lhsT=diag31.a[:, j, k, :],
                                          rhs=glu.a[:, j, hh * 272 + k:hh * 272 + k + 242], start=(k == 0), stop=(k == 30))
                    return i_
                S.add("pe", cv, reads=[diag31.r, glu_r[j]], writes=[PB[bc]])
                cin = V(PS, 0, 128, bc * 512, [(256, 2), (LSA, 8), (1, 4)])
                cout = V(cbf.a, 0, 128, j * NA, [(8, 2), (1, 8), (16, 4)])
            S.add("act", lambda e, j=j, cin=cin, cout=cout: e.activation(out=cout, in_=cin, func=AF.Identity,
                                                                         bias=vecA.a[:, j, 31:32]),
                  reads=[PB[bc], vecA.r], writes=[cbf_r[j]])
            cs = csq[j % 2]
            S.add("act", lambda e, j=j, cs=cs: e.activation(out=cs.a[:, 0:N], in_=cbf.a[:, j, 0:N], func=AF.Square),
                  reads=[cbf_r[j]], writes=[cs.r])

            def st(e, j=j, cs=cs):
                e.matmul(bank(5, N), lhsT=ones_b.a, rhs=cbf.a[:, j, 0:N], start=(j == 0), stop=(j == 7))
                return e.matmul(bank(6, N), lhsT=ones_b.a, rhs=cs.a[:, 0:N], start=(j == 0), stop=(j == 7))
            S.add("pe", st, reads=[ones_b.r, cbf_r[j], cs.r], writes=[PB[5], PB[6]])
            if kind == "P":
                S.add("pool", lambda e, j=j: e.tensor_copy(out=gluh.a[:, j, :], in_=glu.a[:, j, N:N + 30]), reads=[glu_r[j]], writes=[gluh.r])
                S.add("pool", lambda e, j=j: e.tensor_copy(out=glu.a[:, j, 0:30], in_=gluh.a[:, j, :]), reads=[gluh.r], writes=[glu_r[j]])

        stage1(0)
        for j in range(1, 8):
            stage1(j)
            stage2(j - 1)
        stage2(7)
        S.add("dve", lambda e: e.tensor_scalar(out=mean_s[slot].a[:, 0:N], in0=bank(5, N), scalar1=1.0 / D, scalar2=None, op0=ALU.mult),
              reads=[PB[5]], writes=[mean_s[slot].r])
        S.add("dve", lambda e: e.tensor_scalar(out=ex2_s[slot].a[:, 0:N], in0=bank(6, N), scalar1=1.0 / D, scalar2=None, op0=ALU.mult),
              reads=[PB[6]], writes=[ex2_s[slot].r])

    def passA_tail(cx):
        N, kind, last_prompt, slot, xbufs = cx["N"], cx["kind"], cx["last_prompt"], cx["slot"], cx["xbufs"]
        cbf, cbf_r = cbfs[slot], cbf_rs[slot]
        glu32 = glu32p if kind == "P" else glu32s
        mean, ex2 = mean_s[slot], ex2_s[slot]
        S.add("dve", lambda e: e.tensor_tensor(out=nmr.a[:, 0:N], in0=mean.a[:, 0:N], in1=mean.a[:, 0:N], op=ALU.mult),
              reads=[mean.r], writes=[nmr.r])
        S.add("dve", lambda e: e.tensor_tensor(out=rsd.a[:, 0:N], in0=ex2.a[:, 0:N], in1=nmr.a[:, 0:N], op=ALU.subtract),
              reads=[ex2.r, nmr.r], writes=[rsd.r])
        S.add("act", lambda e: e.activation(out=rsd.a[:, 0:N], in_=rsd.a[:, 0:N], func=AF.Sqrt, bias=eps_c.a), reads=[rsd.r, eps_c.r],
              writes=[rsd.r])
        S.add("dve", lambda e: e.reciprocal(out=rsd.a[:, 0:N], in_=rsd.a[:, 0:N]), reads=[rsd.r], writes=[rsd.r])
        S.add("dve", lambda e: e.scalar_tensor_tensor(out=nmr.a[:, 0:N], in0=mean.a[:, 0:N], scalar=-1.0, in1=rsd.a[:, 0:N],
                                                      op0=ALU.mult, op1=ALU.mult),
              reads=[mean.r, rsd.r], writes=[nmr.r])
        for j in range(8):
            t = tn[j % 2]
            S.add("dve", lambda e, j=j, t=t: e.tensor_tensor(out=t.a[:, 0:N], in0=cbf.a[:, j, 0:N], in1=rsd.a[:, 0:N], op=ALU.mult),
                  reads=[cbf_r[j], rsd.r], writes=[t.r])
            S.add("dve", lambda e, t=t: e.tensor_tensor(out=t.a[:, 0:N], in0=t.a[:, 0:N], in1=nmr.a[:, 0:N], op=ALU.add),
                  reads=[t.r, nmr.r], writes=[t.r])
            S.add("act", lambda e, j=j, t=t: e.activation(out=cT.a[:, j, 0:N], in_=t.a[:, 0:N], func=AF.Silu,
                                                          scale=vecA.a[:, j, 32:33], bias=vecA.a[:, j, 33:34]),
                  reads=[t.r, vecA.r], writes=[cT_r[j]])
        for (xb, rb, r, c0) in xbufs:
            def op_(e, r=r, c0=c0):
                for hf in range(2):
                    bo = 3 if hf == 0 else 7
                    for k in range(8):
                        i_ = e.matmul(PS[0:r, bo * 512:(bo + 1) * 512], lhsT=cT.a[:, k, c0:c0 + r],
                                      rhs=wA_out.a[:, k, hf * 512:(hf + 1) * 512], start=(k == 0), stop=(k == 7))
                return i_
            S.add("pe", op_, reads=cT_r + [wA_out.r], writes=[PB[3], PB[7]])
            ho = houtA[houtn[0] % 2]
            houtn[0] += 1
            for hf in range(2):
                bo = 3 if hf == 0 else 7
                S.add("dve", lambda e, r=r, ho=ho, xb=xb, hf=hf, bo=bo: e.tensor_tensor(out=ho.a[0:r, hf * 512:(hf + 1) * 512],
                                                                                        in0=PS[0:r, bo * 512:(bo + 1) * 512],
                                                                                        in1=xb.a[0:r, hf * 512:(hf + 1) * 512], op=ALU.add),
                      reads=[PB[bo], xb.r], writes=[ho.r])
            dma("sp", h_a[rb:rb + r, :], ho.a[0:r, :], reads=[ho.r], writes=[r_ha], key=rot("ha", 2))
        if last_prompt or kind == "S":
            nr = 30 if kind == "P" else NS

            def tro(e):
                for j in range(8):
                    bo = 3 if j < 4 else 7
                    i_ = e.transpose(out=PS[0:nr, bo * 512 + (j % 4) * 128:bo * 512 + (j % 4 + 1) * 128], in_=glu32.a[:, j, 0:nr],
                                     identity=ident_f.a)
                return i_
            S.add("pe", tro, reads=[glu32.r, ident_f.r], writes=[PB[3], PB[7]])
            S.add("act", lambda e: e.copy(out=stout.a[0:nr, 0:512], in_=PS[0:nr, 3 * 512:4 * 512]), reads=[PB[3]], writes=[stout.r])
            S.add("act", lambda e: e.copy(out=stout.a[0:nr, 512:1024], in_=PS[0:nr, 7 * 512:8 * 512]), reads=[PB[7]], writes=[stout.r])
            if kind == "P":
                dma("sp", o_pconf, stout.a[0:30, :], reads=[stout.r], key="oc")
            else:
                for l in range(LS):
                    dma("sp", o_sconf[:, 26 + l, :], stout.a[16 * l:16 * l + 16, :], reads=[stout.r], key="oc")
                dma("sp", o_sconf[:, 0:26, :].rearrange("b r d -> b (r d)"), st_conf[:, 4:30, :].rearrange("b r d -> b (r d)"),
                    key="oc")

    blA = blocks(NA // 128)
    if STOP.startswith("A") and len(STOP) > 1:
        blA = blA[:int(STOP[1:])]
    if os.environ.get("MK_SKIPA"):
        blA = []
    kindof = lambda tl: "S" if tl[0] == 17 else "P"
    NOIL = bool(os.environ.get("MK_NOIL"))
    cxs = {}
    if blA:
        cxs[0] = passA_front(blA[0], kindof(blA[0]), 0)
        passA_main(cxs[0])
    for bi_, tiles in enumerate(blA):
        tl_ = lambda bi_=bi_: passA_tail(cxs[bi_])
        if bi_ + 1 < len(blA):
            nt = blA[bi_ + 1]

            def nf(bi_=bi_, nt=nt):
                cxs[bi_ + 1] = passA_front(nt, kindof(nt), (bi_ + 1) % 2)
                passA_main(cxs[bi_ + 1])
            if NOIL:
                tl_()
                nf()
            else:
                interleave(S, [tl_, nf], turns=[int(x) for x in os.environ.get('MK_TA', '1,2').split(',')])
        else:
            tl_()
    if STOP.startswith("A"):
        S.emit(nc)
        return nc

    S.barrier()
    AR.reset(pass_mark)

    NB = 256
    NCB = 2576
    wB_in = AR.alloc("wB_in", [128, 8, NCB], BF16)
    wB_out = AR.alloc("wB_out", [128, 8, D], BF16)
    diag4 = AR.alloc("diag4", [128, 12, 4, 128], BF16)
    for k in range(8):
        dma("pool", wB_in.a[:, k, :], w_in_v[:, k, OFF_Z:INP], writes=[wB_in.r], key="wBi", chain=False)
    dma("pool", wB_out.a, w_out_v[:, 8:16, :], writes=[wB_out.r], key="wBo", chain=False)
    for j in range(12):
        S.add("dve", lambda e, j=j: e.tensor_tensor(out=diag4.a[:, j, :, :], in0=V(ident_b.a, 0, 128, 0, [(0, 4), (1, 128)]),
                                                    in1=V(vecB.a, 0, 128, j * 5, [(1, 4), (0, 128)]), op=ALU.mult),
              reads=[ident_b.r, vecB.r], writes=[diag4.r])
    feB = make_front(1)
    uTBs = [AR.alloc("uTB%d" % i, [128, 8, NB], BF16) for i in range(2)]
    LSB = 7
    XW = max(3 + NB, NSEQ_S * LSB)
    xbuf_ = AR.alloc("xbcbuf", [128, 12, XW], BF16)
    xbh = AR.alloc("xbh", [128, 12, 3], BF16)
    xbc32p = AR.alloc("xbc32p", [128, 12, 4], F32)
    xcs = [AR.alloc("xc%d" % i, [128, 12, NB], BF16) for i in range(2)]
    zsils = [AR.alloc("zsil0", [128, D], F32)]
    dtt = AR.alloc("dtt", [128, 16], F32)
    at = AR.alloc("at", [128, 16], F32)
    ahi = AR.alloc("ahi", [128, 16], BF16)
    alo = AR.alloc("alo", [128, 16], BF16)
    ares = AR.alloc("ares", [128, 16], F32)
    Rhi = AR.alloc("Rhi", [128, 16, 128], BF16)
    Rlo = AR.alloc("Rlo", [128, 16, 128], BF16)
    decay = AR.alloc("decay", [128, 16, 128], BF16)
    Mm = AR.alloc("Mm", [128, 16, 128], BF16)
    cbt = AR.alloc("cbt", [128, 2, 128], BF16)
    xdt = AR.alloc("xdt", [128, D], BF16)
    xws = [AR.alloc("xw0", [128, D], BF16)]
    yxs = [AR.alloc("yx0", [128, D], F32)]
    btoks = [AR.alloc("btok0", [128, 256], BF16)]
    etaus = [AR.alloc("etau0", [128, 16], F32)]
    dchs = [AR.alloc("dch0", [128, 16], F32)]
    t1 = AR.alloc("t1", [128, D], F32)
    t2 = feB["junk"]
    gss = AR.alloc("gss", [128, 2], F32)
    yn = AR.alloc("yn", [128, D], BF16)
    yT = AR.alloc("yT", [128, 8, NB], BF16)
    S32s = [AR.alloc("S32_0", [128, D], F32)]
    Sbfs = [AR.alloc("Sbf_0", [128, D], BF16)]
    slds = [AR.alloc("sload0", [128, 8, 128], F32)]
    S32, Sbf, sload = S32s[0], Sbfs[0], slds[0]
    hpB = [AR.alloc("hpB%d" % i, [128, D], F32) for i in range(1)]
    houtB = [AR.alloc("houtB%d" % i, [128, D], F32) for i in range(1)]
    stoutB = AR.alloc("stoutB", [48, 1536], F32)
    stinB = AR.alloc("stinB", [48, 1536], F32)
    xbc32s = AR.alloc("xbc32s", [128, 12, 48], F32)
    ovl = AR.mark()
    zsils.append(AR.alloc("zsil1", [128, D], F32))
    xws.append(AR.alloc("xw1", [128, D], BF16))
    yxs.append(AR.alloc("yx1", [128, D], F32))
    btoks.append(AR.alloc("btok1", [128, 256], BF16))
    etaus.append(AR.alloc("etau1", [128, 16], F32))
    dchs.append(AR.alloc("dch1", [128, 16], F32))
    ovl_hi = AR.mark()
    AR.reset(ovl)
    S32s.append(AR.alloc("S32_1", [128, D], F32))
    Sbfs.append(AR.alloc("Sbf_1", [128, D], BF16))
    slds.append(AR.alloc("sload1", [128, 8, 128], F32))
    m1s_b = AR.alloc("m1s_b", [NS, NS], BF16)
    sus_b = AR.alloc("sus_b", [NS, NS], BF16)
    negs_b = AR.alloc("negs_b", [NS, 4, NS], BF16)
    m1s_f = AR.alloc("m1s_f", [NS, NS], F32)
    rowsel_f = AR.alloc("rowsel_f", [NS, NSEQ_S], F32)
    colsel_b = AR.alloc("colsel_b", [128, NSEQ_S, NS], BF16)
    Cmb = [AR.alloc("Cmb%d" % i, [128, 2, NS], BF16) for i in range(2)]
    btm = [AR.alloc("btm%d" % i, [NS, 256], BF16) for i in range(2)]
    am_ = AR.alloc("am", [NS, NSEQ_S * 16], F32)
    dch_all = AR.alloc("dch_all", [128, NSEQ_S * 16], F32)
    dte_ = AR.alloc("dte", [128, 16], F32)
    AR.off = max(AR.off, ovl_hi)
    S.add("pool", lambda e: e.memset(xbuf_.a, 0.0), writes=[xbuf_.r])
    S.add("pool", lambda e: e.memset(S32.a, 0.0), writes=[S32.r])
    S.add("pool", lambda e: e.memset(Sbf.a, 0.0), writes=[Sbf.r])
    r_dt = r_acs = r_al = r_cbt = r_btk = PB[3]
    B3 = 3 * 512

    def ssd_head(r, csl, batched=False, slot=0, hs=0):
        uTB, xc = uTBs[slot], xcs[slot]
        M1b, SUb, NEGb, M1f = (m1s_b, sus_b, negs_b, m1s_f) if batched else (m1_b, su_b, neg_b, m1_f)
        zsil, xw, btok, etau, dch, xD = zsils[hs], xws[hs], btoks[hs], etaus[hs], dchs[hs], yxs[hs]
        def zf(e):
            for hf in range(2):
                for k in range(8):
                    i_ = e.matmul(PS[0:r, (4 + hf) * 512:(5 + hf) * 512], lhsT=uTB.a[:, k, csl], rhs=wB_in.a[:, k, hf * 512:(hf + 1) * 512],
                                  start=(k == 0), stop=(k == 7))
            return i_
        S.add("pe", zf, reads=[uTB.r, wB_in.r], writes=[PB[4], PB[5]])
        S.add("act", lambda e: e.activation(out=zsil.a[0:r, :], in_=PS[0:r, 4 * 512:6 * 512], func=AF.Silu), reads=[PB[4], PB[5]],
              writes=[zsil.r])

        def dtf(e):
            for k in range(8):
                i_ = e.matmul(PS[0:r, B3:B3 + 16], lhsT=uTB.a[:, k, csl], rhs=wB_in.a[:, k, 2560:2576], start=(k == 0), stop=(k == 7))
            return i_
        S.add("pe", dtf, reads=[uTB.r, wB_in.r], writes=[r_dt])
        S.add("dve", lambda e: e.tensor_tensor(out=dtt.a[0:r, :], in0=PS[0:r, B3:B3 + 16], in1=dtb_bc.a[0:r, :], op=ALU.add),
              reads=[r_dt, dtb_bc.r], writes=[dtt.r])
        S.add("act", lambda e: e.activation(out=dtt.a[0:r, :], in_=dtt.a[0:r, :], func=AF.Exp), reads=[dtt.r], writes=[dtt.r])
        S.add("act", lambda e: e.activation(out=dtt.a[0:r, :], in_=dtt.a[0:r, :], func=AF.Ln, bias=1.0), reads=[dtt.r], writes=[dtt.r])
        S.add("dve", lambda e: e.tensor_tensor(out=at.a[0:r, :], in0=dtt.a[0:r, :], in1=A_bc.a[0:r, :], op=ALU.mult),
              reads=[dtt.r, A_bc.r], writes=[at.r])
        S.add("dve", lambda e: e.tensor_copy(out=ahi.a[0:r, :], in_=at.a[0:r, :]), reads=[at.r], writes=[ahi.r])
        S.add("dve", lambda e: e.tensor_tensor(out=alo.a[0:r, :], in0=at.a[0:r, :], in1=ahi.a[0:r, :], op=ALU.subtract),
              reads=[at.r, ahi.r], writes=[alo.r])
        for (src, dst) in ((ahi, Rhi), (alo, Rlo)):
            S.add("dve", lambda e, src=src, dst=dst: e.tensor_tensor(out=dst.a[0:r, :, 0:r], in0=V(src.a, 0, r, 0, [(1, 16), (0, r)]),
                                                                     in1=V(M1b.a, 0, r, 0, [(0, 16), (1, r)]), op=ALU.mult),
                  reads=[src.r, M1b.r], writes=[dst.r])
        pb0 = bankb(2)

        def trx(e):
            for j in range(8):
                i_ = e.transpose(out=pb0[0:r, j * 128:(j + 1) * 128], in_=xc.a[:, j, csl], identity=ident_b.a)
            return i_
        S.add("pe", trx, reads=[xc.r, ident_b.r], writes=[PB[2]])
        pbk = PS[:, B3 + 384:B3 + 512].bitcast(BF16)

        def trb(e):
            for g in range(2):
                i_ = e.transpose(out=pbk[0:r, g * 128:(g + 1) * 128], in_=xc.a[:, 8 + g, csl], identity=ident_b.a)
            return i_
        S.add("pe", trb, reads=[xc.r, ident_b.r], writes=[r_btk])
        S.add("act", lambda e: e.copy(out=btok.a[0:r, :], in_=pbk[0:r, :]), reads=[r_btk], writes=[btok.r])
        S.add("dve", lambda e: e.tensor_tensor(out=xdt.a[0:r, :].rearrange("p (h q) -> p h q", h=16),
                                               in0=pb0[0:r, :].rearrange("p (h q) -> p h q", h=16),
                                               in1=V(dtt.a, 0, r, 0, [(1, 16), (0, 64)]), op=ALU.mult),
              reads=[PB[2], dtt.r], writes=[xdt.r])
        S.add("dve", lambda e: e.tensor_tensor(out=xD.a[0:r, :].rearrange("p (h q) -> p h q", h=16),
                                               in0=pb0[0:r, :].rearrange("p (h q) -> p h q", h=16),
                                               in1=V(D_bc.a, 0, r, 0, [(1, 16), (0, 64)]), op=ALU.mult),
              reads=[PB[2], D_bc.r], writes=[xD.r])

        for rnd in range(2):
            def dff(e, rnd=rnd):
                for qq in range(2):
                    q = rnd * 2 + qq
                    o = PS[0:r, (4 + qq) * 512:(4 + qq) * 512 + 4 * r].rearrange("p (h l) -> p h l", h=4)
                    e.matmul(o, lhsT=SUb.a[0:r, 0:r], rhs=Rhi.a[0:r, 4 * q:4 * q + 4, 0:r], start=True, stop=False)
                    e.matmul(o, lhsT=SUb.a[0:r, 0:r], rhs=Rlo.a[0:r, 4 * q:4 * q + 4, 0:r], start=False, stop=False)
                    i_ = e.matmul(o, lhsT=ident_b.a[0:r, 0:r], rhs=NEGb.a[0:r, :, 0:r], start=False, stop=True)
                return i_
            S.add("pe", dff, reads=[SUb.r, Rhi.r, Rlo.r, ident_b.r, NEGb.r], writes=[PB[4], PB[5]])
            for qq in range(2):
                q = rnd * 2 + qq
                S.add("act", lambda e, q=q, qq=qq: e.activation(out=decay.a[0:r, 4 * q:4 * q + 4, 0:r],
                                                                in_=PS[0:r, (4 + qq) * 512:(4 + qq) * 512 + 4 * r].rearrange("p (h l) -> p h l", h=4),
                                                                func=AF.Exp),
                      reads=[PB[4 + qq]], writes=[decay.r])

        def cbf_(e):
            for g in range(2):
                i_ = e.matmul(PS[0:r, B3 + 64 + g * 128:B3 + 64 + g * 128 + r], lhsT=xc.a[:, 8 + g, csl], rhs=xc.a[:, 10 + g, csl],
                              start=True, stop=True)
            return i_
        S.add("pe", cbf_, reads=[xc.r], writes=[r_cbt])
        S.add("act", lambda e: e.copy(out=cbt.a[0:r, :, 0:r], in_=PS[0:r, B3 + 64:B3 + 320].rearrange("p (g l) -> p g l", g=2)[:, :, 0:r]),
              reads=[r_cbt], writes=[cbt.r])
        S.add("dve", lambda e: e.tensor_tensor(out=V(Mm.a, 0, r, 0, [(8 * 128, 2), (128, 8), (1, r)]),
                                               in0=V(decay.a, 0, r, 0, [(8 * 128, 2), (128, 8), (1, r)]),
                                               in1=V(cbt.a, 0, r, 0, [(128, 2), (0, 8), (1, r)]), op=ALU.mult),
              reads=[decay.r, cbt.r], writes=[Mm.r])
        if batched:
            S.add("dve", lambda e: e.tensor_reduce(out=dte_.a[0:r, :], in_=decay.a[0:r, :, r - 16:r], axis=mybir.AxisListType.X, op=ALU.add),
                  reads=[decay.r], writes=[dte_.r])
            S.add("dve", lambda e: e.tensor_tensor(out=xw.a[0:r, :].rearrange("p (h q) -> p h q", h=16),
                                                   in0=xdt.a[0:r, :].rearrange("p (h q) -> p h q", h=16),
                                                   in1=V(dte_.a, 0, r, 0, [(1, 16), (0, 64)]), op=ALU.mult),
                  reads=[xdt.r, dte_.r], writes=[xw.r])
        else:
            S.add("dve", lambda e: e.tensor_tensor(out=xw.a[0:r, :].rearrange("p (h q) -> p h q", h=16),
                                                   in0=xdt.a[0:r, :].rearrange("p (h q) -> p h q", h=16),
                                                   in1=V(decay.a, 0, r, r - 1, [(128, 16), (0, 64)]), op=ALU.mult),
                  reads=[xdt.r, decay.r], writes=[xw.r])

        if batched:
            S.add("dve", lambda e: e.tensor_tensor(out=am_.a[0:r, :].rearrange("p (b h) -> p b h", b=NSEQ_S),
                                                   in0=V(at.a, 0, r, 0, [(0, NSEQ_S), (1, 16)]),
                                                   in1=V(rowsel_f.a, 0, r, 0, [(1, NSEQ_S), (0, 16)]), op=ALU.mult),
                  reads=[at.r, rowsel_f.r], writes=[am_.r])

            def acf(e):
                e.matmul(PS[0:r, B3 + 16:B3 + 32], lhsT=M1f.a[0:r, 0:r], rhs=at.a[0:r, :], start=True, stop=True)
                return e.matmul(PS[:, B3 + 64:B3 + 64 + NSEQ_S * 16], lhsT=ones_f.a[0:r, :], rhs=am_.a[0:r, :], start=True, stop=True)
            S.add("pe", acf, reads=[M1f.r, ones_f.r, at.r, am_.r], writes=[r_acs, r_al])
            S.add("act", lambda e: e.activation(out=etau.a[0:r, :], in_=PS[0:r, B3 + 16:B3 + 32], func=AF.Exp), reads=[r_acs], writes=[etau.r])
            S.add("act", lambda e: e.activation(out=dch_all.a, in_=PS[:, B3 + 64:B3 + 64 + NSEQ_S * 16], func=AF.Exp), reads=[r_al],
                  writes=[dch_all.r])
        else:
            def acf(e):
                e.matmul(PS[0:r, B3 + 16:B3 + 32], lhsT=M1f.a[0:r, 0:r], rhs=at.a[0:r, :], start=True, stop=True)
                return e.matmul(PS[:, B3 + 32:B3 + 48], lhsT=ones_f.a[0:r, :], rhs=at.a[0:r, :], start=True, stop=True)
            S.add("pe", acf, reads=[M1f.r, ones_f.r, at.r], writes=[r_acs, r_al])
            S.add("act", lambda e: e.activation(out=etau.a[0:r, :], in_=PS[0:r, B3 + 16:B3 + 32], func=AF.Exp), reads=[r_acs], writes=[etau.r])
            S.add("act", lambda e: e.activation(out=dch.a, in_=PS[:, B3 + 32:B3 + 48], func=AF.Exp), reads=[r_al], writes=[dch.r])

        def ydf(e):
            for h in range(16):
                i_ = e.matmul(PS[0:r, 4 * 512 + h * 64:4 * 512 + (h + 1) * 64], lhsT=Mm.a[0:r, h, 0:r], rhs=xdt.a[0:r, h * 64:(h + 1) * 64],
                              start=True, stop=True)
            return i_
        S.add("pe", ydf, reads=[Mm.r, xdt.r], writes=[PB[4], PB[5]])

        S.add("dve", lambda e: e.tensor_tensor(out=xD.a[0:r, :], in0=PS[0:r, 4 * 512:6 * 512], in1=xD.a[0:r, :], op=ALU.add),
              reads=[PB[4], PB[5], xD.r], writes=[xD.r])

    def ssd_tail(r, csl, batched=False, slot=0, hs=0):
        uTB, xc = uTBs[slot], xcs[slot]
        M1b, SUb, NEGb, M1f = (m1s_b, sus_b, negs_b, m1s_f) if batched else (m1_b, su_b, neg_b, m1_f)
        zsil, xw, btok, etau, dch, xD = zsils[hs], xws[hs], btoks[hs], etaus[hs], dchs[hs], yxs[hs]
        pb6 = bankb(6)
        if not batched:
            def zzf(e):
                for g in range(2):
                    i_ = e.matmul(PS[0:r, (6 + g) * 512:(7 + g) * 512], lhsT=xc.a[:, 10 + g, csl], rhs=Sbf.a[:, g * 512:(g + 1) * 512],
                                  start=True, stop=True)
                return i_
            S.add("pe", zzf, reads=[xc.r, Sbf.r], writes=[PB[6], PB[7]])
            zb = 6
        else:
            zzc = [0]

            def seq_chain(par):
                S32x, Sbfx, sldx = S32s[par], Sbfs[par], slds[par]
                bk = 4 + 2 * par
                for b in range(par, NSEQ_S, 2):
                    cm = Cmb[par]
                    bt = btm[par]
                    dma("sp", sldx.a, st_ssm[b].rearrange("(c q) n -> q c n", q=128), writes=[sldx.r], key="sld%d" % par)

                    def trl(e):
                        for c in range(8):
                            i_ = e.transpose(out=PS[:, bk * 512 + c * 128:bk * 512 + (c + 1) * 128], in_=sldx.a[:, c, :], identity=ident_f.a)
                        return i_
                    S.add("pe", trl, reads=[sldx.r, ident_f.r], writes=[PB[bk], PB[bk + 1]])
                    S.add("dve", lambda e: e.tensor_copy(out=S32x.a, in_=PS[:, bk * 512:(bk + 2) * 512]), reads=[PB[bk], PB[bk + 1]],
                          writes=[S32x.r])
                    S.add("act", lambda e: e.copy(out=Sbfx.a, in_=S32x.a), reads=[S32x.r], writes=[Sbfx.r])
                    S.add("dve", lambda e, b=b, cm=cm: e.tensor_tensor(out=cm.a, in0=xc.a[:, 10:12, 0:r],
                                                                       in1=V(colsel_b.a, 0, 128, b * NS, [(0, 2), (1, r)]), op=ALU.mult),
                          reads=[xc.r, colsel_b.r], writes=[cm.r])
                    S.add("dve", lambda e, b=b, bt=bt: e.tensor_scalar(out=bt.a[0:r, :], in0=btok.a[0:r, :], scalar1=rowsel_f.a[0:r, b:b + 1],
                                                                       scalar2=None, op0=ALU.mult),
                          reads=[btok.r, rowsel_f.r], writes=[bt.r])
                    zi = zzc[0]
                    zzc[0] += 1

                    def zzb(e, cm=cm, zi=zi):
                        for g in range(2):
                            i_ = e.matmul(PS[0:r, (1 + g) * 512:(2 + g) * 512], lhsT=cm.a[:, g, :], rhs=Sbfx.a[:, g * 512:(g + 1) * 512],
                                          start=(zi == 0), stop=(zi == NSEQ_S - 1))
                        return i_
                    S.add("pe", zzb, reads=[cm.r, Sbfx.r], writes=[PB[1], PB[2]])

                    def sub(e, bt=bt):
                        for g in range(2):
                            i_ = e.matmul(PS[:, (bk + g) * 512:(bk + 1 + g) * 512], lhsT=bt.a[0:r, g * 128:(g + 1) * 128],
                                          rhs=xw.a[0:r, g * 512:(g + 1) * 512], start=True, stop=True)
                        return i_
                    S.add("pe", sub, reads=[bt.r, xw.r], writes=[PB[bk], PB[bk + 1]])
                    S.add("dve", lambda e, b=b: e.tensor_tensor(out=S32x.a.rearrange("p (h q) -> p h q", h=16),
                                                                in0=S32x.a.rearrange("p (h q) -> p h q", h=16),
                                                                in1=V(dch_all.a, 0, 128, b * 16, [(1, 16), (0, 64)]), op=ALU.mult),
                          reads=[S32x.r, dch_all.r], writes=[S32x.r])
                    S.add("dve", lambda e: e.tensor_tensor(out=S32x.a, in0=PS[:, bk * 512:(bk + 2) * 512], in1=S32x.a, op=ALU.add),
                          reads=[PB[bk], PB[bk + 1], S32x.r], writes=[S32x.r])
                    state_out(o_sssm[b], S32x, sldx, bk, "oss%d" % par)
            if os.environ.get("MK_NOIL"):
                seq_chain(0)
                seq_chain(1)
            else:
                interleave(S, [lambda: seq_chain(0), lambda: seq_chain(1)])
            zb = 1
        S.add("dve", lambda e, zb=zb: e.tensor_tensor(out=t1.a[0:r, :].rearrange("p (h q) -> p h q", h=16),
                                                      in0=PS[0:r, zb * 512:(zb + 2) * 512].rearrange("p (h q) -> p h q", h=16),
                                                      in1=V(etau.a, 0, r, 0, [(1, 16), (0, 64)]), op=ALU.mult),
              reads=[PB[zb], PB[zb + 1], etau.r], writes=[t1.r])
        S.add("dve", lambda e: e.tensor_tensor(out=t1.a[0:r, :], in0=t1.a[0:r, :], in1=xD.a[0:r, :], op=ALU.add),
              reads=[t1.r, xD.r], writes=[t1.r])
        S.add("dve", lambda e: e.tensor_tensor(out=t1.a[0:r, :], in0=t1.a[0:r, :], in1=zsil.a[0:r, :], op=ALU.mult),
              reads=[t1.r, zsil.r], writes=[t1.r])
        for g in range(2):
            S.add("dve", lambda e, g=g: e.scalar_tensor_tensor(out=t2.a[0:r, g * 512:(g + 1) * 512], in0=t1.a[0:r, g * 512:(g + 1) * 512],
                                                               scalar=1.0, in1=t1.a[0:r, g * 512:(g + 1) * 512], op0=ALU.mult,
                                                               op1=ALU.mult, accum_out=gss.a[0:r, g:g + 1]),
                  reads=[t1.r], writes=[t2.r, gss.r])
        S.add("act", lambda e: e.activation(out=gss.a[0:r, :], in_=gss.a[0:r, :], func=AF.Sqrt, bias=eps_c.a[0:r, :], scale=1.0 / 512),
              reads=[gss.r, eps_c.r], writes=[gss.r])
        S.add("dve", lambda e: e.reciprocal(out=gss.a[0:r, :], in_=gss.a[0:r, :]), reads=[gss.r], writes=[gss.r])
        for g in range(2):
            S.add("dve", lambda e, g=g: e.scalar_tensor_tensor(out=yn.a[0:r, g * 512:(g + 1) * 512], in0=t1.a[0:r, g * 512:(g + 1) * 512],
                                                               scalar=gss.a[0:r, g:g + 1], in1=snorm_bc.a[0:r, g * 512:(g + 1) * 512],
                                                               op0=ALU.mult, op1=ALU.mult),
                  reads=[t1.r, gss.r, snorm_bc.r], writes=[yn.r])

        def try_(e):
            for j in range(8):
                i_ = e.transpose(out=pb6[:, j * 128:j * 128 + r], in_=yn.a[0:r, j * 128:(j + 1) * 128], identity=ident_b.a[0:r, 0:r])
            return i_
        S.add("pe", try_, reads=[yn.r, ident_b.r], writes=[PB[6]])
        S.add("act", lambda e: e.copy(out=yT.a[:, :, csl], in_=pb6.rearrange("p (j t) -> p j t", j=8)[:, :, 0:r]), reads=[PB[6]],
              writes=[yT.r])
        if not batched:

            def suf(e):
                for g in range(2):
                    i_ = e.matmul(PS[:, (6 + g) * 512:(7 + g) * 512], lhsT=btok.a[0:r, g * 128:(g + 1) * 128], rhs=xw.a[0:r, g * 512:(g + 1) * 512],
                                  start=True, stop=True)
                return i_
            S.add("pe", suf, reads=[btok.r, xw.r], writes=[PB[6], PB[7]])
            S.add("dve", lambda e: e.tensor_tensor(out=S32.a.rearrange("p (h q) -> p h q", h=16), in0=S32.a.rearrange("p (h q) -> p h q", h=16),
                                                   in1=V(dch.a, 0, 128, 0, [(1, 16), (0, 64)]), op=ALU.mult),
                  reads=[S32.r, dch.r], writes=[S32.r])
            S.add("dve", lambda e: e.tensor_tensor(out=S32.a, in0=PS[:, 6 * 512:8 * 512], in1=S32.a, op=ALU.add),
                  reads=[PB[6], PB[7], S32.r], writes=[S32.r])
            S.add("act", lambda e: e.copy(out=Sbf.a, in_=S32.a), reads=[S32.r], writes=[Sbf.r])

    def ssd_chunk(r, csl, batched=False, slot=0, hs=0):
        ssd_head(r, csl, batched, slot, hs)
        ssd_tail(r, csl, batched, slot, hs)

    def state_out(dst2d, S32x=None, sldx=None, bk=6, key="oss"):
        S32x = S32x or S32
        sldx = sldx or sload

        def trs(e):
            for c in range(8):
                i_ = e.transpose(out=PS[:, bk * 512 + c * 128:bk * 512 + (c + 1) * 128], in_=S32x.a[:, c * 128:(c + 1) * 128], identity=ident_f.a)
            return i_
        S.add("pe", trs, reads=[S32x.r, ident_f.r], writes=[PB[bk], PB[bk + 1]])
        S.add("act", lambda e: e.copy(out=sldx.a, in_=PS[:, bk * 512:(bk + 2) * 512].rearrange("p (c n) -> p c n", c=8)),
              reads=[PB[bk], PB[bk + 1]], writes=[sldx.r])
        dma("sp", dst2d.rearrange("(c q) n -> q c n", q=128), sldx.a, reads=[sldx.r], key=key)

    hslot = [0]

    def passB_front(tiles, kind, slot, part=None):
        uTB, xc = uTBs[slot], xcs[slot]
        N = sum(tile_rows(i)[1] for i in tiles)
        tinfo = []
        col = 0
        for i in tiles:
            rb, r = tile_rows(i)
            if part in (None, 0):
                xb = feB["xt"][xslot[0] % len(feB["xt"])]
                xslot[0] += 1
                for (src, p0, rr) in tile_src(i):
                    dma("sp", xb.a[p0:p0 + rr, :], src, writes=[xb.r], key=rot("xa", 4))
                rms_to_uT(feB, xb, r, 34, uTB, col, tbank=0)
            tinfo.append((rb, r, col))
            col += r
        last_prompt = (kind == "P" and tiles[-1] == 16)
        if kind == "S" and part in (None, 0):
            dma("sp", stinB.a, st_xbc.rearrange("b r d -> (b r) d"), writes=[stinB.r], key="sti")
            for j in range(12):
                def tr(e, j=j):
                    return e.transpose(out=PS[0:128, 0:48], in_=stinB.a[0:48, j * 128:(j + 1) * 128],
                                       identity=ident_f.a[0:48, 0:48])
                S.add("pe", tr, reads=[stinB.r, ident_f.r], writes=[PB[0]])
                outv = V(xbuf_.a, 0, 128, j * XW, [(LSB, 16), (1, 3)])
                inv = V(PS, 0, 128, 0, [(3, 16), (1, 3)])
                S.add("act", lambda e, outv=outv, inv=inv: e.copy(out=outv, in_=inv), reads=[PB[0]], writes=[xbuf_.r])
        jr = range(12) if part is None else (range(0, 6) if part == 0 else range(6, 12))
        for j in jr:
            def mx(e, j=j):
                for k in range(8):
                    i_ = e.matmul(bank(1, N), lhsT=wB_in.a[:, k, 1024 + j * 128:1024 + (j + 1) * 128], rhs=uTB.a[:, k, 0:N],
                                  start=(k == 0), stop=(k == 7))
                return i_
            S.add("pe", mx, reads=[wB_in.r, uTB.r], writes=[PB[1]])
            if kind == "P":
                S.add("act", lambda e, j=j: e.copy(out=xbuf_.a[:, j, 3:3 + N], in_=bank(1, N)), reads=[PB[1]], writes=[xbuf_.r])
                if last_prompt and "c" not in SK:
                    S.add("dve", lambda e, j=j: e.tensor_copy(out=xbc32p.a[:, j, 0:4], in_=bank(1, N)[:, N - 4:N]), reads=[PB[1]],
                          writes=[xbc32p.r])

                def cv(e, j=j):
                    for k in range(4):
                        i_ = e.matmul(bank(0, N), lhsT=diag4.a[:, j, k, :], rhs=xbuf_.a[:, j, k:k + N], start=(k == 0), stop=(k == 3))
                    return i_
                S.add("pe", cv, reads=[diag4.r, xbuf_.r], writes=[PB[0]])
                cin = bank(0, N)
                cout = xc.a[:, j, 0:N]
            else:
                outv = V(xbuf_.a, 0, 128, j * XW + 3, [(LSB, 16), (1, 4)])
                inv = V(PS, 0, 128, 512, [(1, 16), (16, 4)])
                S.add("act", lambda e, outv=outv, inv=inv: e.copy(out=outv, in_=inv), reads=[PB[1]], writes=[xbuf_.r])
                S.add("dve", lambda e, j=j: e.tensor_copy(out=xbc32s.a[:, j, 0:48], in_=bank(1, N)[:, 16:64]), reads=[PB[1]],
                      writes=[xbc32s.r])

                def cv(e, j=j):
                    for k in range(4):
                        i_ = e.matmul(bank(0, 109), lhsT=diag4.a[:, j, k, :], rhs=xbuf_.a[:, j, k:k + 109], start=(k == 0), stop=(k == 3))
                    return i_
                S.add("pe", cv, reads=[diag4.r, xbuf_.r], writes=[PB[0]])
                cin = V(PS, 0, 128, 0, [(LSB, 16), (1, 4)])
                cout = V(xc.a, 0, 128, j * NB, [(1, 16), (16, 4)])
            S.add("act", lambda e, j=j, cin=cin, cout=cout: e.activation(out=cout, in_=cin, func=AF.Silu, bias=vecB.a[:, j, 4:5]),
                  reads=[PB[0], vecB.r], writes=[xc.r])
        if kind == "P" and part in (None, 1):
            S.add("pool", lambda e: e.tensor_copy(out=xbh.a, in_=xbuf_.a[:, :, N:N + 3]), reads=[xbuf_.r], writes=[xbh.r])
            S.add("pool", lambda e: e.tensor_copy(out=xbuf_.a[:, :, 0:3], in_=xbh.a), reads=[xbh.r], writes=[xbuf_.r])
        return tinfo, N, last_prompt

    def passB_end(tiles, kind, slot, tinfo, N, last_prompt):
        uTB, xc = uTBs[slot], xcs[slot]
        for (rb, r, c0) in tinfo:
            hp = hpB[0]
            ho = houtB[0]
            hslot[0] += 1
            dma("sp", hp.a[0:r, :], h_a[rb:rb + r, :], reads=[r_ha], writes=[hp.r], key=rot("hb", 2))

            def op_(e, r=r, c0=c0):
                for hf in range(2):
                    for k in range(8):
                        i_ = e.matmul(PS[0:r, (6 + hf) * 512:(7 + hf) * 512], lhsT=yT.a[:, k, c0:c0 + r],
                                      rhs=wB_out.a[:, k, hf * 512:(hf + 1) * 512], start=(k == 0), stop=(k == 7))
                return i_
            S.add("pe", op_, reads=[yT.r, wB_out.r], writes=[PB[6], PB[7]])
            S.add("dve", lambda e, r=r, ho=ho, hp=hp: e.tensor_tensor(out=ho.a[0:r, :], in0=PS[0:r, 6 * 512:8 * 512], in1=hp.a[0:r, :],
                                                                      op=ALU.add),
                  reads=[PB[6], PB[7], hp.r], writes=[ho.r])
            dma("sp", h_b[rb:rb + r, :], ho.a[0:r, :], reads=[ho.r], writes=[r_hb], key=rot("hbo", 2))
        if (last_prompt or kind == "S") and "x" not in SK:
            nr = 4 if kind == "P" else 48
            xbc32 = xbc32p if kind == "P" else xbc32s
            for grp in range(3):
                def tro(e, grp=grp):
                    for jj in range(4):
                        j = grp * 4 + jj
                        i_ = e.transpose(out=PS[0:nr, 6 * 512 + jj * 128:6 * 512 + (jj + 1) * 128], in_=xbc32.a[:, j, 0:nr], identity=ident_f.a)
                    return i_
                S.add("pe", tro, reads=[xbc32.r, ident_f.r], writes=[PB[6]])
                S.add("act", lambda e, grp=grp: e.copy(out=stoutB.a[0:nr, grp * 512:(grp + 1) * 512], in_=PS[0:nr, 6 * 512:7 * 512]),
                      reads=[PB[6]], writes=[stoutB.r])
            if kind == "P":
                dma("sp", o_pxbc, stoutB.a[1:4, :], reads=[stoutB.r], key="oc")
            else:
                for l in range(3):
                    dma("sp", o_sxbc[:, l, :], stoutB.a[16 * l:16 * l + 16, :], reads=[stoutB.r], key="oc")

    blB = blocks(NB // 128)
    if STOP.startswith("B") and len(STOP) > 1:
        blB = blB[:int(STOP[1:])]
    if os.environ.get("MK_SKIPB"):
        blB = []
    prB = [t for t in blB if t[0] != 17]
    NOIL = bool(os.environ.get("MK_NOIL"))
    frs = {}
    TB_ = [int(x) for x in os.environ.get('MK_TB', '1,1,1').split(',')]
    chunksB = []
    for bi_, tiles in enumerate(prB):
        c0 = 0
        for ti_, i in enumerate(tiles):
            r = tile_rows(i)[1]
            chunksB.append((bi_, r, c0, ti_ == 0, ti_ == len(tiles) - 1))
            c0 += r
    hasS = any(t[0] == 17 for t in blB)
    if prB:
        frs[0] = passB_front(prB[0], "P", 0)
    for k in range(len(chunksB) + 1):
        chains = []
        turns = []
        if k >= 1:
            (pb_, pr_, pc0, _, plast) = chunksB[k - 1]

            def tailc(pb_=pb_, pr_=pr_, pc0=pc0, plast=plast, k=k):
                ssd_tail(pr_, slice(pc0, pc0 + pr_), False, pb_ % 2, (k - 1) % 2)
                if plast:
                    if prB[pb_][-1] == 16 and "s" not in SK:
                        state_out(o_pssm)
                    passB_end(prB[pb_], "P", pb_ % 2, *frs[pb_])
            chains.append(tailc)
            turns.append(TB_[0])
        if k < len(chunksB):
            (cb_, cr_, cc0, cfirst, clast) = chunksB[k]
            chains.append(lambda cb_=cb_, cr_=cr_, cc0=cc0, k=k: ssd_head(cr_, slice(cc0, cc0 + cr_), False, cb_ % 2, k % 2))
            turns.append(TB_[1])
            if cb_ + 1 < len(prB) or hasS:
                nb = cb_ + 1
                parts = ([0] if cfirst else []) + ([1] if clast else [])

                def frc(nb=nb, parts=parts):
                    for p_ in parts:
                        if nb < len(prB):
                            frs[nb] = passB_front(prB[nb], "P", nb % 2, part=p_)
                        else:
                            frs[nb] = passB_front([17], "S", nb % 2, part=p_)
                chains.append(frc)
                turns.append(TB_[2])
        if NOIL:
            for c_ in chains:
                c_()
        else:
            interleave(S, chains, turns=turns)
    if any(t[0] == 17 for t in blB):
        S.barrier()
        dma("pool", m1s_b.a, c_m1s, writes=[m1s_b.r], key="c2")
        dma("pool", sus_b.a, c_sus, writes=[sus_b.r], key="c3")
        dma("pool", negs_b.a, c_negs, writes=[negs_b.r], key="c4")
        dma("pool", colsel_b.a, c_colsel, writes=[colsel_b.r], key="c5")
        dma("sp", m1s_f.a, c_m1s, writes=[m1s_f.r], key="c0")
        dma("sp", rowsel_f.a, c_rowsel, writes=[rowsel_f.r], key="c1")
        sl_ = len(prB) % 2
        fS = frs[len(prB)] if len(prB) in frs else passB_front([17], "S", sl_)
        ssd_chunk(NS, slice(0, NS), batched=True, slot=sl_, hs=0)
        passB_end([17], "S", sl_, *fS)
    if STOP.startswith("B"):
        S.emit(nc)
        return nc

    S.barrier()
    AR.reset(pass_mark)

    NC = 512
    NQ = 11
    w_up_v = w_up.rearrange("(k p) n -> p k n", p=128)
    w_down_v = w_down.rearrange("(k p) n -> p k n", p=128)
    wC_up = AR.alloc("wC_up", [128, 8, 2 * NQ * 128], BF16)
    wC_dn = AR.alloc("wC_dn", [128, NQ, D], BF16)
    diag3 = AR.alloc("diag3", [128, 2 * NQ, 3, 128], BF16)
    feC = make_front(2)
    hres = [AR.alloc("hres%d" % i, [128, D], F32) for i in range(2)]
    uTCs = [AR.alloc("uTC%d" % i, [128, 8, NC], BF16) for i in range(2)]
    junk2 = AR.alloc("junk2", [128, D], BF16)
    ss2 = AR.alloc("ss2", [128, 1], F32)
    rstd2 = AR.alloc("rstd2", [128, 1], F32)
    LSC = 6
    FW = max(2 + NC, NSEQ_S * LSC)
    fb = [[AR.alloc("fb%d%d" % (s, v), [128, FW], BF16) for v in range(2)] for s in range(2)]
    fhist = AR.alloc("fhist", [128, 2 * NQ, 2], BF16)
    f32hp = AR.alloc("f32hp", [128, 2 * NQ, 2], F32)
    f32hs = AR.alloc("f32hs", [128, 2 * NQ, 32], F32)
    sgc = [AR.alloc("sgc%d" % i, [128, NC], F32) for i in range(2)]
    acts = [AR.alloc("act%d" % i, [128, NQ, NC], BF16) for i in range(2)]
    houtC = [AR.alloc("houtC%d" % i, [128, D], F32) for i in range(2)]
    youtC = [AR.alloc("youtC%d" % i, [128, D], F32) for i in range(2)]
    stinC = AR.alloc("stinC", [32, 2 * NQ * 128], F32)
    stoutC = AR.alloc("stoutC", [32, 512], F32)

    def passC(half):
        q0 = half * NQ
        hsrc, r_src = (h_b, r_hb) if half == 0 else (h_c, r_hc)
        for k in range(8):
            dma("pool", wC_up.a[:, k, 0:NQ * 128], w_up_v[:, k, q0 * 128:(q0 + NQ) * 128], writes=[wC_up.r], key="wCu", chain=False)
            dma("pool", wC_up.a[:, k, NQ * 128:2 * NQ * 128], w_up_v[:, k, DFF + q0 * 128:DFF + (q0 + NQ) * 128], writes=[wC_up.r],
                key="wCu", chain=False)
        dma("pool", wC_dn.a, w_down_v[:, q0:q0 + NQ, :], writes=[wC_dn.r], key="wCd", chain=False)

        def vcj(t):
            return (q0 + t) if t < NQ else (22 + q0 + t - NQ)
        for t in range(2 * NQ):
            S.add("dve", lambda e, t=t: e.tensor_tensor(out=diag3.a[:, t, :, :], in0=V(ident_b.a, 0, 128, 0, [(0, 3), (1, 128)]),
                                                        in1=V(vecC.a, 0, 128, vcj(t) * 4, [(1, 3), (0, 128)]), op=ALU.mult),
                  reads=[ident_b.r, vecC.r], writes=[diag3.r])
        fhist_r = [Res("fh%d" % t) for t in range(2 * NQ)]
        act_rs = [[Res("act%d_%d" % (i, q)) for q in range(NQ)] for i in range(2)]
        S.add("pool", lambda e: e.memset(fhist.a, 0.0), writes=fhist_r)
        hs = [0]
        hon = [0]

        def mainC(tiles, kind, slot):
            uTC, act_, act_r = uTCs[slot], acts[slot], act_rs[slot]
            f32h = f32hp if kind == "P" else f32hs
            N = sum(tile_rows(i)[1] for i in tiles)
            tinfo = []
            col = 0
            for i in tiles:
                rb, r = tile_rows(i)
                xb = feC["xt"][hs[0] % 2]
                hs[0] += 1
                dma("sp", xb.a[0:r, :], h_b[rb:rb + r, :], reads=[r_hb], writes=[xb.r], key=rot("xa", 4))
                rms_to_uT(feC, xb, r, 35, uTC, col, tbank=0)
                tinfo.append((rb, r, col, i))
                col += r
            last_prompt = (kind == "P" and tiles[-1] == 16)
            if kind == "S":
                dma("sp", stinC.a[:, 0:NQ * 128], st_ffn.rearrange("b r d -> (b r) d")[:, q0 * 128:(q0 + NQ) * 128], writes=[stinC.r], key="sti")
                dma("sp", stinC.a[:, NQ * 128:2 * NQ * 128], st_ffn.rearrange("b r d -> (b r) d")[:, DFF + q0 * 128:DFF + (q0 + NQ) * 128],
                    writes=[stinC.r], key="sti")
            def stage_up(q):
                fbs = fb[q % 2]
                for v in range(2):
                    t = q + v * NQ
                    bu = (1 + v) if q % 2 == 0 else (3 + v)
                    fbuf = fbs[v]

                    def up(e, t=t, bu=bu):
                        for k in range(8):
                            i_ = e.matmul(bank(bu, N), lhsT=wC_up.a[:, k, t * 128:(t + 1) * 128], rhs=uTC.a[:, k, 0:N], start=(k == 0),
                                          stop=(k == 7))
                        return i_
                    S.add("pe", up, reads=[wC_up.r, uTC.r], writes=[PB[bu]])
                    if kind == "P":
                        S.add("act", lambda e, t=t, fbuf=fbuf: e.copy(out=fbuf.a[:, 0:2], in_=fhist.a[:, t, :]), reads=[fhist_r[t]],
                              writes=[fbuf.r])
                        S.add("act", lambda e, bu=bu, fbuf=fbuf: e.copy(out=fbuf.a[:, 2:2 + N], in_=bank(bu, N)), reads=[PB[bu]],
                              writes=[fbuf.r])
                        S.add("pool", lambda e, t=t, fbuf=fbuf: e.tensor_copy(out=fhist.a[:, t, :], in_=fbuf.a[:, N:N + 2]), reads=[fbuf.r],
                              writes=[fhist_r[t]])
                        if last_prompt:
                            S.add("dve", lambda e, t=t, bu=bu: e.tensor_copy(out=f32h.a[:, t, 0:2], in_=bank(bu, N)[:, N - 2:N]),
                                  reads=[PB[bu]], writes=[f32h.r])
                    else:
                        def trh(e, t=t):
                            return e.transpose(out=PS[0:128, 0:32], in_=stinC.a[0:32, t * 128:(t + 1) * 128],
                                               identity=ident_f.a[0:32, 0:32])
                        S.add("pe", trh, reads=[stinC.r, ident_f.r], writes=[PB[0]])
                        S.add("act", lambda e, fbuf=fbuf: e.copy(out=V(fbuf.a, 0, 128, 0, [(LSC, 16), (1, 2)]),
                                                                 in_=V(PS, 0, 128, 0, [(2, 16), (1, 2)])),
                              reads=[PB[0]], writes=[fbuf.r])
                        S.add("act", lambda e, bu=bu, fbuf=fbuf: e.copy(out=V(fbuf.a, 0, 128, 2, [(LSC, 16), (1, 4)]),
                                                                        in_=V(PS, 0, 128, bu * 512, [(1, 16), (16, 4)])),
                              reads=[PB[bu]], writes=[fbuf.r])
                        S.add("dve", lambda e, t=t, bu=bu: e.tensor_copy(out=f32h.a[:, t, 0:32], in_=bank(bu, N)[:, 32:64]), reads=[PB[bu]],
                              writes=[f32h.r])

            def stage_conv(q):
                fbs = fb[q % 2]
                for v in range(2):
                    t = q + v * NQ
                    bcv = 5 + v
                    fbuf = fbs[v]
                    NN = N if kind == "P" else 94

                    def cv(e, t=t, bcv=bcv, fbuf=fbuf, NN=NN):
                        for k in range(3):
                            i_ = e.matmul(bank(bcv, NN), lhsT=diag3.a[:, t, k, :], rhs=fbuf.a[:, k:k + NN], start=(k == 0), stop=(k == 2))
                        return i_
                    S.add("pe", cv, reads=[diag3.r, fbuf.r], writes=[PB[bcv]])
                sg = sgc[q % 2]
                if kind == "P":
                    cg, cvv, so, ao = bank(5, N), bank(6, N), sg.a[:, 0:N], act_.a[:, q, 0:N]
                    si = so
                else:
                    cg = V(PS, 0, 128, 5 * 512, [(LSC, 16), (1, 4)])
                    cvv = V(PS, 0, 128, 6 * 512, [(LSC, 16), (1, 4)])
                    so = V(sg.a, 0, 128, 0, [(1, 16), (16, 4)])
                    si = so
                    ao = V(act_.a, 0, 128, q * NC, [(1, 16), (16, 4)])
                gj, vj = vcj(q), vcj(q + NQ)
                S.add("act", lambda e, cg=cg, so=so, gj=gj: e.activation(out=so, in_=cg, func=AF.Silu, bias=vecC.a[:, gj, 3:4]),
                      reads=[PB[5], vecC.r], writes=[sg.r])
                S.add("dve", lambda e, cvv=cvv, si=si, ao=ao, vj=vj: e.scalar_tensor_tensor(out=ao, in0=cvv, scalar=vecC.a[:, vj, 3:4], in1=si,
                                                                                            op0=ALU.add, op1=ALU.mult),
                      reads=[PB[6], vecC.r, sg.r], writes=[act_r[q]])

            stage_up(0)
            for q in range(1, NQ):
                stage_up(q)
                stage_conv(q - 1)
            stage_conv(NQ - 1)
            return dict(tinfo=tinfo, N=N, kind=kind, last_prompt=last_prompt, slot=slot)

        def downC(cx):
            tinfo, N, kind, last_prompt, slot = cx["tinfo"], cx["N"], cx["kind"], cx["last_prompt"], cx["slot"]
            act_, act_r = acts[slot], act_rs[slot]
            f32h = f32hp if kind == "P" else f32hs
            for (rb, r, c0, ti) in tinfo:
                ho = houtC[hon[0] % 2]
                hr = hres[hon[0] % 2]
                hon[0] += 1
                dma("sp", hr.a[0:r, :], hsrc[rb:rb + r, :], reads=[r_src], writes=[hr.r], key=rot("xr", 2))
                for hf in range(2):
                    bd = 7

                    def dn(e, r=r, c0=c0, hf=hf, bd=bd):
                        for k in range(NQ):
                            i_ = e.matmul(PS[0:r, bd * 512:(bd + 1) * 512], lhsT=act_.a[:, k, c0:c0 + r],
                                          rhs=wC_dn.a[:, k, hf * 512:(hf + 1) * 512], start=(k == 0), stop=(k == NQ - 1))
                        return i_
                    S.add("pe", dn, reads=act_r + [wC_dn.r], writes=[PB[bd]])
                    S.add("dve", lambda e, r=r, ho=ho, hr=hr, hf=hf, bd=bd: e.tensor_tensor(out=ho.a[0:r, hf * 512:(hf + 1) * 512],
                                                                                           in0=PS[0:r, bd * 512:(bd + 1) * 512],
                                                                                           in1=hr.a[0:r, hf * 512:(hf + 1) * 512], op=ALU.add),
                          reads=[PB[bd], hr.r], writes=[ho.r])
                if half == 0:
                    dma("sp", h_c[rb:rb + r, :], ho.a[0:r, :], reads=[ho.r], writes=[r_hc], key=rot("hco", 2))
                else:
                    yo = youtC[hon[0] % 2]
                    junk, ss, rstd = junk2, ss2, rstd2
                    S.add("dve", lambda e, r=r, ho=ho: e.scalar_tensor_tensor(out=junk.a[0:r, :], in0=ho.a[0:r, :], scalar=1.0, in1=ho.a[0:r, :],
                                                                              op0=ALU.mult, op1=ALU.mult, accum_out=ss.a[0:r, :]),
                          reads=[ho.r], writes=[junk.r, ss.r])
                    S.add("act", lambda e, r=r: e.activation(out=rstd.a[0:r, :], in_=ss.a[0:r, :], func=AF.Sqrt, bias=eps_c.a[0:r, :],
                                                             scale=1.0 / D),
                          reads=[ss.r, eps_c.r], writes=[rstd.r])
                    S.add("dve", lambda e, r=r: e.reciprocal(out=rstd.a[0:r, :], in_=rstd.a[0:r, :]), reads=[rstd.r], writes=[rstd.r])
                    S.add("dve", lambda e, r=r, ho=ho, yo=yo: e.scalar_tensor_tensor(out=yo.a[0:r, :], in0=ho.a[0:r, :], scalar=rstd.a[0:r, :],
                                                                                     in1=gfin_bc.a[0:r, :], op0=ALU.mult, op1=ALU.mult),
                          reads=[ho.r, rstd.r, gfin_bc.r], writes=[yo.r])
                    if 1 <= ti <= 16:
                        dma("sp", yp[128 * (ti - 1):128 * ti, :], yo.a[0:r, :], reads=[yo.r], key=rot("yo", 2))
                    elif ti == 17:
                        for l in range(LS):
                            dma("sp", ys[:, l, :], yo.a[16 * l:16 * l + 16, :], reads=[yo.r], key=rot("yo", 2))
            if last_prompt or kind == "S":
                nr = 2 if kind == "P" else 32
                for grp in range((2 * NQ + 3) // 4):
                    tl = list(range(grp * 4, min(grp * 4 + 4, 2 * NQ)))

                    def tro(e, tl=tl):
                        for jj, t in enumerate(tl):
                            i_ = e.transpose(out=PS[0:nr, 7 * 512 + jj * 128:7 * 512 + (jj + 1) * 128], in_=f32h.a[:, t, 0:nr], identity=ident_f.a)
                        return i_
                    S.add("pe", tro, reads=[f32h.r, ident_f.r], writes=[PB[7]])
                    S.add("act", lambda e, n_=len(tl): e.copy(out=stoutC.a[0:nr, 0:n_ * 128], in_=PS[0:nr, 7 * 512:7 * 512 + n_ * 128]),
                          reads=[PB[7]], writes=[stoutC.r])
                    runs = []
                    for jj, t in enumerate(tl):
                        if runs and vcj(t) == runs[-1][1] + runs[-1][2]:
                            runs[-1][2] += 1
                        else:
                            runs.append([jj, vcj(t), 1])
                    for (jj0, g0, n_) in runs:
                        if kind == "P":
                            dma("sp", o_pffn[:, g0 * 128:(g0 + n_) * 128], stoutC.a[0:2, jj0 * 128:(jj0 + n_) * 128], reads=[stoutC.r],
                                key="of", chain=False)
                        else:
                            for l in range(2):
                                dma("sp", o_sffn[:, l, g0 * 128:(g0 + n_) * 128], stoutC.a[16 * l:16 * l + 16, jj0 * 128:(jj0 + n_) * 128],
                                    reads=[stoutC.r], key="of", chain=False)

        blC = blocks(NC // 128)
        kindof = lambda tl: "S" if tl[0] == 17 else "P"
        cxs = {}
        cxs[0] = mainC(blC[0], kindof(blC[0]), 0)
        for bi_, tiles in enumerate(blC):
            dn_ = lambda bi_=bi_: downC(cxs[bi_])
            if bi_ + 1 < len(blC):
                nt = blC[bi_ + 1]

                def nf(bi_=bi_, nt=nt):
                    cxs[bi_ + 1] = mainC(nt, kindof(nt), (bi_ + 1) % 2)
                if os.environ.get("MK_NOIL"):
                    dn_()
                    nf()
                else:
                    interleave(S, [dn_, nf], turns=[int(x) for x in os.environ.get('MK_TC', '1,3').split(',')])
            else:
                dn_()

    passC(0)
    if STOP == "C0":
        S.emit(nc)
        return nc
    S.barrier()
    passC(1)

    S.emit(nc)
    return nc


_CACHE = {}


def _consts():
    j = np.arange(128)[:, None]
    l = np.arange(128)[None, :]
    m1 = (j <= l).astype(np.float32)
    su = (l < j).astype(np.float32)
    neg = np.where(j > l, NEGBIG, 0.0).astype(np.float32)
    neg4 = np.ascontiguousarray(np.broadcast_to(neg[:, None, :], (128, 4, 128)))
    return np.eye(128, dtype=np.float32), m1, su, neg4


def _consts_s():
    i = np.arange(NS)
    seq, pos = i % NSEQ_S, i // NSEQ_S
    same = seq[:, None] == seq[None, :]
    m1s = (same & (pos[:, None] <= pos[None, :])).astype(np.float32)
    sus = (same & (pos[None, :] < pos[:, None])).astype(np.float32)
    negs = np.where(m1s > 0, 0.0, NEGBIG).astype(np.float32)
    negs4 = np.ascontiguousarray(np.broadcast_to(negs[:, None, :], (NS, 4, NS)))
    rowsel = (seq[:, None] == np.arange(NSEQ_S)[None, :]).astype(np.float32)
    colsel = np.ascontiguousarray(np.broadcast_to(rowsel.T[None], (128, NSEQ_S, NS))).astype(np.float32)
    return m1s, sus, negs4, rowsel, colsel


def kernel(x_prompt, x_sample, state_conf_conv, state_xbc_conv, state_ssm, state_ffn_conv,
           meta_tokens, norm_mix_g, w_in, conf_conv_w, conf_conv_b, conf_ln_g, conf_ln_b,
           ssm_conv_w, ssm_conv_b, dt_bias, a_log, d_skip, ssm_norm_g, w_out, norm_ffn_g,
           w_up, ffn_conv_w, ffn_conv_b, w_down, norm_final_g):
    f = lambda a: np.ascontiguousarray(np.asarray(a, dtype=np.float32))
    debug = bool(int(os.environ.get("MK_DEBUG", "0")))
    if "nc" not in _CACHE:
        _CACHE["nc"] = build_program(debug)
    nc = _CACHE["nc"]
    ident, m1, su, neg4 = _consts()
    vtabA = np.concatenate([f(conf_conv_w)[0], f(conf_conv_b), f(conf_ln_g), f(conf_ln_b), f(norm_mix_g), f(norm_ffn_g)], axis=0)
    vtabB = np.concatenate([f(ssm_conv_w)[0], f(ssm_conv_b)], axis=0)
    vtabC = np.concatenate([f(ffn_conv_w)[0], f(ffn_conv_b)], axis=0)
    v16 = np.concatenate([f(dt_bias), f(a_log), f(d_skip)], axis=0)
    shared = {
        "meta": f(meta_tokens), "w_in": f(w_in)[0], "w_out": f(w_out)[0], "w_up": f(w_up)[0], "w_down": f(w_down)[0],
        "vtabA": np.ascontiguousarray(vtabA), "vtabB": np.ascontiguousarray(vtabB), "vtabC": np.ascontiguousarray(vtabC),
        "v16": np.ascontiguousarray(v16), "snorm_g": f(ssm_norm_g), "gfin": f(norm_final_g).reshape(1, D),
        "c_ident": ident, "c_m1": m1, "c_su": su, "c_neg": neg4,
    }
    m1s, sus, negs4, rowsel, colsel = _consts_s()
    shared.update({"c_m1s": m1s, "c_sus": sus, "c_negs": negs4, "c_rowsel": rowsel, "c_colsel": colsel})
    xpf, xsf = f(x_prompt), f(x_sample)
    sc, sx, ssm_, sf = f(state_conf_conv)[0], f(state_xbc_conv)[0], f(state_ssm)[0], f(state_ffn_conv)[0]
    in_maps = []
    for c in range(8):
        m = dict(shared)
        sl = slice(16 * c, 16 * c + 16)
        m["xp"] = xpf[c]
        m["xs"] = xsf[sl]
        m["st_conf"] = sc[sl]
        m["st_xbc"] = sx[sl]
        m["st_ssm"] = np.ascontiguousarray(ssm_[sl].reshape(16, 1024, 128))
        m["st_ffn"] = sf[sl]
        in_maps.append(m)
    res = run_bass_kernel_spmd(nc, in_maps, core_ids=list(range(8)))
    R = res.results
    _CACHE["last"] = R
    cat = lambda k: np.concatenate([np.asarray(R[c][k]) for c in range(8)], axis=0)
    stk = lambda k: np.stack([np.asarray(R[c][k]) for c in range(8)], axis=0)
    y_prompt = stk("yp")
    y_sample = cat("ys")
    return (y_prompt.astype(np.float32), y_sample.astype(np.float32),
            stk("o_pconf")[None].astype(np.float32), stk("o_pxbc")[None].astype(np.float32),
            stk("o_pssm").reshape(8, 16, 64, 128)[None].astype(np.float32), stk("o_pffn")[None].astype(np.float32),
            cat("o_sconf")[None].astype(np.float32), cat("o_sxbc")[None].astype(np.float32),
            cat("o_sssm").reshape(128, 16, 64, 128)[None].astype(np.float32), cat("o_sffn")[None].astype(np.float32))
```
